# Optimizing a Trainium2 kernel written in Bass

```python
import math
import jax
import jax.numpy as jnp
from jax import lax
import numpy as np

D_MODEL = 1024
BATCH = 4
SEQ = 8192
DEPTH = 2

GRID_W = 64
CTX_LEN = 256
Q_BLOCK = 128
NORM_EPS = 1e-6

ATTN_HEADS = 8
ATTN_KV_HEADS = 2
ATTN_HEAD_DIM = 64
ATTN_AXIS_FREQS = ATTN_HEAD_DIM // 4
ROPE_THETA = 10000.0

SSD_HEADS = 8
SSD_HEAD_DIM = 64
SSD_D_INNER = SSD_HEADS * SSD_HEAD_DIM
SSD_GROUPS = 2
SSD_STATE = 128
SSD_CONV_K = 3
SSD_CONV_DIM = SSD_D_INNER + 2 * SSD_GROUPS * SSD_STATE
SSD_CHUNK = 128

RET_HEADS = 4
RET_DK = 128
RET_DV = 128
RET_CHUNK = 128

N_BRANCH = 3
BRANCH_W = 512
MLP_HIDDEN = 4 * D_MODEL

IN_SPLITS = (ATTN_HEADS * ATTN_HEAD_DIM, ATTN_KV_HEADS * ATTN_HEAD_DIM, ATTN_KV_HEADS * ATTN_HEAD_DIM,
             SSD_D_INNER, SSD_CONV_DIM, 2 * SSD_HEADS,
             RET_HEADS * RET_DK, RET_HEADS * RET_DK, RET_HEADS * RET_DV, RET_HEADS * RET_DV,
             N_BRANCH * D_MODEL)
IN_DIM = sum(IN_SPLITS)

kernel_name = 'hybrid_attn_ssd_retention_prefix_dit'


def rms_norm(x, w):
    xf = x.astype(jnp.float32)
    y = xf * lax.rsqrt(jnp.mean(xf * xf, axis=-1, keepdims=True) + NORM_EPS)
    return (y * w.astype(jnp.float32)).astype(x.dtype)


def modulate(x, shift, scale):
    return x * (1 + scale[:, None, :]) + shift[:, None, :]


def split_cols(p):
    out = []
    off = 0
    for size in IN_SPLITS:
        out.append(p[..., off:off + size])
        off += size
    return out


def flip(t):
    return jnp.flip(t, axis=1)


def rope_apply(x, cos, sin):
    half = x.shape[-1] // 2
    x1 = x[..., :half]
    x2 = x[..., half:]
    cs = cos[:, None, :].astype(x.dtype)
    sn = sin[:, None, :].astype(x.dtype)
    return jnp.concatenate([x1 * cs - x2 * sn, x1 * sn + x2 * cs], axis=-1)


def axial_angles(rows):
    row = jnp.repeat(jnp.arange(rows, dtype=jnp.float32), GRID_W)
    col = jnp.tile(jnp.arange(GRID_W, dtype=jnp.float32), rows)
    inv = ROPE_THETA ** (-jnp.arange(ATTN_AXIS_FREQS, dtype=jnp.float32) / ATTN_AXIS_FREQS)
    ang = jnp.concatenate([row[:, None] * inv, col[:, None] * inv], axis=-1)
    return jnp.cos(ang), jnp.sin(ang)


def seq_angles(start, n):
    pos = jnp.arange(n, dtype=jnp.float32) + start
    inv = ROPE_THETA ** (-jnp.linspace(0.0, 1.0, RET_DK // 2, dtype=jnp.float32))
    ang = pos[:, None] * inv
    return jnp.cos(ang), jnp.sin(ang)


def dwconv_centred(x, w, bias):
    y = lax.conv_general_dilated(
        x, w[:, None, :].astype(x.dtype), window_strides=(1,),
        padding=[(SSD_CONV_K // 2, SSD_CONV_K // 2)],
        dimension_numbers=('NWC', 'WIO', 'NWC'), feature_group_count=x.shape[-1])
    return y + bias.astype(x.dtype)


def gqa_attend(q, k, v):
    b, lq, h, hd = q.shape
    kvh = k.shape[2]
    grp = h // kvh
    qb = q.reshape(b, lq // Q_BLOCK, Q_BLOCK, kvh, grp, hd).transpose(1, 0, 2, 3, 4, 5)
    scale = hd ** -0.5

    def block(qblk):
        s = jnp.einsum('bqkgd,bskd->bkgqs', qblk, k, preferred_element_type=jnp.float32) * scale
        p = jax.nn.softmax(s, axis=-1).astype(v.dtype)
        return jnp.einsum('bkgqs,bskd->bqkgd', p, v)

    o = lax.map(block, qb)
    return o.transpose(1, 0, 2, 3, 4, 5).reshape(b, lq, h * hd)


def chunk_scan(init, states, decay):
    def step(s, inp):
        st, dc = inp
        return s * dc + st, s
    final, prev = lax.scan(step, init, (states, decay))
    return final, prev


def ssd_scan(xh, dt, a_neg, bm, cm, init, return_y):
    b, l, nh, hp = xh.shape
    ng, ns = bm.shape[2], bm.shape[3]
    r = nh // ng
    L = SSD_CHUNK
    nc = l // L
    dtf = dt.astype(jnp.float32)
    xd = (xh.astype(jnp.float32) * dtf[..., None]).reshape(b, nc, L, ng, r, hp)
    a = (dtf * a_neg).reshape(b, nc, L, ng, r).transpose(0, 3, 4, 1, 2)
    bc = bm.astype(jnp.float32).reshape(b, nc, L, ng, ns)
    a_cum = jnp.cumsum(a, axis=-1)
    decay_states = jnp.exp(a_cum[..., -1:] - a_cum)
    states = jnp.einsum('bclgn,bgrcl,bclgrp->cbgrpn', bc, decay_states, xd)
    chunk_decay = jnp.exp(a_cum[..., -1]).transpose(3, 0, 1, 2)[..., None, None]
    final, prev = chunk_scan(init, states, chunk_decay)
    if not return_y:
        return None, final
    cc = cm.astype(jnp.float32).reshape(b, nc, L, ng, ns)
    seg = a_cum[..., :, None] - a_cum[..., None, :]
    causal = jnp.tril(jnp.ones((L, L), dtype=bool))
    lmat = jnp.where(causal, jnp.exp(jnp.where(causal, seg, 0.0)), 0.0)
    cb = jnp.einsum('bclgn,bcsgn->bcgls', cc, bc)
    y_diag = jnp.einsum('bcgls,bgrcls,bcsgrp->bclgrp', cb, lmat, xd)
    y_off = jnp.einsum('bclgn,cbgrpn,bgrcl->bclgrp', cc, prev, jnp.exp(a_cum))
    return (y_diag + y_off).reshape(b, l, nh, hp), final


def retention_scan(q, k, v, lg, init, return_y):
    b, l, nh, dk = k.shape
    dv = v.shape[-1]
    L = RET_CHUNK
    nc = l // L
    kc = k.astype(jnp.float32).reshape(b, nc, L, nh, dk)
    vc = v.astype(jnp.float32).reshape(b, nc, L, nh, dv)
    pos = jnp.arange(L, dtype=jnp.float32)
    k_decay = jnp.exp((L - 1 - pos)[:, None] * lg)
    states = jnp.einsum('bcshk,sh,bcshv->cbhkv', kc, k_decay, vc)
    chunk_decay = jnp.broadcast_to(jnp.exp(L * lg)[None, None, :, None, None], (nc, 1, nh, 1, 1))
    final, prev = chunk_scan(init, states, chunk_decay)
    if not return_y:
        return None, final
    qc = q.astype(jnp.float32).reshape(b, nc, L, nh, dk)
    diff = pos[:, None] - pos[None, :]
    dmat = jnp.where(diff[None] >= 0, jnp.exp(jnp.maximum(diff, 0.0)[None] * lg[:, None, None]), 0.0)
    s = jnp.einsum('bclhk,bcshk->bchls', qc, kc) * dmat
    y_in = jnp.einsum('bchls,bcshv->bclhv', s, vc)
    q_decay = jnp.exp((pos + 1)[:, None] * lg)
    y_x = jnp.einsum('bclhk,cbhkv,lh->bclhv', qc, prev, q_decay)
    return (y_in + y_x).reshape(b, l, nh, dv), final


def attn_q(aq, q_norm, rope):
    b, l, _ = aq.shape
    q = rms_norm(aq.reshape(b, l, ATTN_HEADS, ATTN_HEAD_DIM), q_norm)
    return q if rope is None else rope_apply(q, *rope)


def attn_kv(ak, av, k_norm, rope):
    b, l, _ = ak.shape
    k = rms_norm(ak.reshape(b, l, ATTN_KV_HEADS, ATTN_HEAD_DIM), k_norm)
    if rope is not None:
        k = rope_apply(k, *rope)
    return k, av.reshape(b, l, ATTN_KV_HEADS, ATTN_HEAD_DIM)


def ssd_inputs(xbc_raw, dt_raw, conv_w, conv_b, dt_bias):
    b, l, _ = xbc_raw.shape
    gn = SSD_GROUPS * SSD_STATE
    xbc = jax.nn.silu(dwconv_centred(xbc_raw, conv_w, conv_b))
    xs = xbc[..., :SSD_D_INNER].reshape(b, l, SSD_HEADS, SSD_HEAD_DIM)
    bm = xbc[..., SSD_D_INNER:SSD_D_INNER + gn].reshape(b, l, SSD_GROUPS, SSD_STATE)
    cm = xbc[..., SSD_D_INNER + gn:].reshape(b, l, SSD_GROUPS, SSD_STATE)
    dt = jax.nn.softplus(dt_raw.astype(jnp.float32).reshape(b, l, 2, SSD_HEADS) + dt_bias.astype(jnp.float32))
    return xs, bm, cm, dt[:, :, 0], dt[:, :, 1]


def ssd_finish(y, xh, z, d_skip, norm_w):
    b, l, nh, hp = xh.shape
    y = y + d_skip.astype(jnp.float32)[:, None] * xh.astype(jnp.float32)
    y = y.reshape(b, l, nh * hp) * jax.nn.silu(z.astype(jnp.float32))
    return rms_norm(y, norm_w).astype(z.dtype)


def ret_q(rq, rope):
    b, l, _ = rq.shape
    return rope_apply(rq.reshape(b, l, RET_HEADS, RET_DK), *rope)


def ret_kv(rk, rv, rope):
    b, l, _ = rk.shape
    k = rope_apply(rk.reshape(b, l, RET_HEADS, RET_DK), *rope) * (RET_DK ** -0.5)
    return k, rv.reshape(b, l, RET_HEADS, RET_DV)


def ret_finish(y, g, gn_w):
    b, l, nh, dv = y.shape
    mu = jnp.mean(y, axis=-1, keepdims=True)
    yc = y - mu
    var = jnp.mean(yc * yc, axis=-1, keepdims=True)
    yn = (yc * lax.rsqrt(var + NORM_EPS)).reshape(b, l, nh * dv) * gn_w.astype(jnp.float32)
    return (yn * jax.nn.silu(g.astype(jnp.float32))).astype(g.dtype)


def merge_branches(br_attn, br_ssd, br_ret, gate_logits, w_branch, w_out):
    b, l, _ = gate_logits.shape
    gates = jax.nn.sigmoid(gate_logits.reshape(b, l, N_BRANCH, D_MODEL))
    merged = (gates[:, :, 0] * (br_attn @ w_branch[0])
              + gates[:, :, 1] * (br_ssd @ w_branch[1])
              + gates[:, :, 2] * (br_ret @ w_branch[2]))
    return merged @ w_out


def sq_relu_mlp(x, w1, w2):
    h = jax.nn.relu(x @ w1)
    return (h * h) @ w2


def hybrid_mixer(u_lat, u_ctx, w_in, q_norm, k_norm, conv_w, conv_b, dt_bias, a_log, d_skip,
                 ssd_norm_w, ret_log_decay, ret_gn_w, w_branch, w_out,
                 rope_lat, ret_rope_ctx, ret_rope_lat, need_ctx):
    b = u_lat.shape[0]
    pl = split_cols(u_lat @ w_in)
    pc = split_cols(u_ctx @ w_in)

    k_c, v_c = attn_kv(pc[1], pc[2], k_norm, None)
    k_l, v_l = attn_kv(pl[1], pl[2], k_norm, rope_lat)
    q_l = attn_q(pl[0], q_norm, rope_lat)
    attn_l = gqa_attend(q_l, jnp.concatenate([k_c, k_l], axis=1), jnp.concatenate([v_c, v_l], axis=1))

    x_c, b_c, c_c, dtf_c, dtb_c = ssd_inputs(pc[4], pc[5], conv_w, conv_b, dt_bias)
    x_l, b_l, c_l, dtf_l, dtb_l = ssd_inputs(pl[4], pl[5], conv_w, conv_b, dt_bias)
    a_neg = -jnp.exp(a_log.astype(jnp.float32))
    s_init = jnp.zeros((b, SSD_GROUPS, SSD_HEADS // SSD_GROUPS, SSD_HEAD_DIM, SSD_STATE), jnp.float32)
    yc_f, st_f = ssd_scan(x_c, dtf_c, a_neg[0], b_c, c_c, s_init, need_ctx)
    yc_b, st_b = ssd_scan(flip(x_c), flip(dtb_c), a_neg[1], flip(b_c), flip(c_c), s_init, need_ctx)
    yl_f, _ = ssd_scan(x_l, dtf_l, a_neg[0], b_l, c_l, st_f, True)
    yl_b, _ = ssd_scan(flip(x_l), flip(dtb_l), a_neg[1], flip(b_l), flip(c_l), st_b, True)
    ssd_l = ssd_finish(yl_f + flip(yl_b), x_l, pl[3], d_skip, ssd_norm_w)

    lg = -jnp.exp(ret_log_decay.astype(jnp.float32))
    r_init = jnp.zeros((b, RET_HEADS, RET_DK, RET_DV), jnp.float32)
    rk_c, rv_c = ret_kv(pc[7], pc[8], ret_rope_ctx)
    rq_c = ret_q(pc[6], ret_rope_ctx) if need_ctx else None
    rk_l, rv_l = ret_kv(pl[7], pl[8], ret_rope_lat)
    rq_l = ret_q(pl[6], ret_rope_lat)
    rc_f, rs_f = retention_scan(rq_c, rk_c, rv_c, lg[0], r_init, need_ctx)
    rc_b, rs_b = retention_scan(None if rq_c is None else flip(rq_c), flip(rk_c), flip(rv_c), lg[1], r_init, need_ctx)
    rl_f, _ = retention_scan(rq_l, rk_l, rv_l, lg[0], rs_f, True)
    rl_b, _ = retention_scan(flip(rq_l), flip(rk_l), flip(rv_l), lg[1], rs_b, True)
    ret_l = ret_finish(rl_f + flip(rl_b), pl[9], ret_gn_w)

    out_l = merge_branches(attn_l, ssd_l, ret_l, pl[10], w_branch, w_out)
    if not need_ctx:
        return out_l, None

    attn_c = gqa_attend(attn_q(pc[0], q_norm, None), k_c, v_c)
    ssd_c = ssd_finish(yc_f + flip(yc_b), x_c, pc[3], d_skip, ssd_norm_w)
    ret_c = ret_finish(rc_f + flip(rc_b), pc[9], ret_gn_w)
    out_c = merge_branches(attn_c, ssd_c, ret_c, pc[10], w_branch, w_out)
    return out_l, out_c


def setup_inputs(seed: int = 0) -> dict:
    key = jax.random.key(seed)
    ks = jax.random.split(key, 24)
    f32 = jnp.float32

    def nrm(k, shape, scale):
        return jax.random.normal(k, shape, f32) * scale

    dt = jnp.exp(jax.random.uniform(ks[13], (DEPTH, 2, SSD_HEADS), f32, math.log(1e-3), math.log(1e-1)))
    return {
        'x': nrm(ks[0], (BATCH, SEQ, D_MODEL), 1.0),
        'c': nrm(ks[1], (BATCH, D_MODEL), 1.0),
        'ctx': nrm(ks[2], (BATCH, CTX_LEN, D_MODEL), 1.0),
        'c_ctx': nrm(ks[3], (D_MODEL,), 1.0),
        'w_mod': nrm(ks[4], (DEPTH, D_MODEL, 6 * D_MODEL), 0.5 * D_MODEL ** -0.5),
        'b_mod': nrm(ks[5], (DEPTH, 6 * D_MODEL), 0.01),
        'norm1_w': 1.0 + nrm(ks[6], (DEPTH, D_MODEL), 0.02),
        'norm2_w': 1.0 + nrm(ks[7], (DEPTH, D_MODEL), 0.02),
        'w_in': nrm(ks[8], (DEPTH, D_MODEL, IN_DIM), D_MODEL ** -0.5),
        'attn_q_norm': 1.0 + nrm(ks[9], (DEPTH, ATTN_HEAD_DIM), 0.02),
        'attn_k_norm': 1.0 + nrm(ks[10], (DEPTH, ATTN_HEAD_DIM), 0.02),
        'ssd_conv_w': nrm(ks[11], (DEPTH, SSD_CONV_K, SSD_CONV_DIM), SSD_CONV_K ** -0.5),
        'ssd_conv_b': nrm(ks[12], (DEPTH, SSD_CONV_DIM), 0.01),
        'ssd_dt_bias': dt + jnp.log(-jnp.expm1(-dt)),
        'ssd_a_log': jnp.log(jax.random.uniform(ks[14], (DEPTH, 2, SSD_HEADS), f32, 1.0, 16.0)),
        'ssd_d': 1.0 + nrm(ks[15], (DEPTH, SSD_HEADS), 0.1),
        'ssd_norm_w': 1.0 + nrm(ks[16], (DEPTH, SSD_D_INNER), 0.02),
        'ret_log_decay': (-5.0 - jnp.arange(RET_HEADS, dtype=f32)) * math.log(2.0)
                         + nrm(ks[17], (DEPTH, 2, RET_HEADS), 0.1),
        'ret_gn_w': 1.0 + nrm(ks[18], (DEPTH, RET_HEADS * RET_DV), 0.02),
        'w_branch': nrm(ks[19], (DEPTH, N_BRANCH, BRANCH_W, D_MODEL), BRANCH_W ** -0.5),
        'w_out': nrm(ks[20], (DEPTH, D_MODEL, D_MODEL), D_MODEL ** -0.5),
        'w_mlp1': nrm(ks[21], (DEPTH, D_MODEL, MLP_HIDDEN), D_MODEL ** -0.5),
        'w_mlp2': nrm(ks[22], (DEPTH, MLP_HIDDEN, D_MODEL), MLP_HIDDEN ** -0.5),
        'final_norm_w': 1.0 + nrm(ks[23], (D_MODEL,), 0.02),
    }


def reference(x, c, ctx, c_ctx, w_mod, b_mod, norm1_w, norm2_w, w_in, attn_q_norm, attn_k_norm,
              ssd_conv_w, ssd_conv_b, ssd_dt_bias, ssd_a_log, ssd_d, ssd_norm_w, ret_log_decay,
              ret_gn_w, w_branch, w_out, w_mlp1, w_mlp2, final_norm_w):
    n = x.shape[1]
    m = ctx.shape[1]
    ROWS = n // GRID_W
    rope_lat = axial_angles(ROWS)
    ret_rope_ctx = seq_angles(0, m)
    ret_rope_lat = seq_angles(m, n)
    h_lat, h_ctx = x, ctx
    for layer in range(DEPTH):
        need_ctx = layer < DEPTH - 1
        mod_lat = jnp.split(jax.nn.silu(c) @ w_mod[layer] + b_mod[layer], 6, axis=-1)
        mod_ctx = jnp.split(jax.nn.silu(c_ctx)[None, :] @ w_mod[layer] + b_mod[layer], 6, axis=-1)
        u_lat = modulate(rms_norm(h_lat, norm1_w[layer]), mod_lat[0], mod_lat[1])
        u_ctx = modulate(rms_norm(h_ctx, norm1_w[layer]), mod_ctx[0], mod_ctx[1])
        mix_lat, mix_ctx = hybrid_mixer(
            u_lat, u_ctx, w_in[layer], attn_q_norm[layer], attn_k_norm[layer],
            ssd_conv_w[layer], ssd_conv_b[layer], ssd_dt_bias[layer], ssd_a_log[layer], ssd_d[layer],
            ssd_norm_w[layer], ret_log_decay[layer], ret_gn_w[layer], w_branch[layer], w_out[layer],
            rope_lat, ret_rope_ctx, ret_rope_lat, need_ctx)
        h_lat = h_lat + mod_lat[2][:, None, :] * mix_lat
        v_lat = modulate(rms_norm(h_lat, norm2_w[layer]), mod_lat[3], mod_lat[4])
        h_lat = h_lat + mod_lat[5][:, None, :] * sq_relu_mlp(v_lat, w_mlp1[layer], w_mlp2[layer])
        if need_ctx:
            h_ctx = h_ctx + mod_ctx[2][:, None, :] * mix_ctx
            v_ctx = modulate(rms_norm(h_ctx, norm2_w[layer]), mod_ctx[3], mod_ctx[4])
            h_ctx = h_ctx + mod_ctx[5][:, None, :] * sq_relu_mlp(v_ctx, w_mlp1[layer], w_mlp2[layer])
    return rms_norm(h_lat, final_norm_w)
```

```python
import math
import re
import numpy as np
import ml_dtypes
import concourse.bass as bass
import concourse.mybir as mybir
from concourse.bass_utils import run_bass_kernel_spmd

F32 = mybir.dt.float32
BF16 = mybir.dt.bfloat16
AF = mybir.ActivationFunctionType
ALU = mybir.AluOpType
AX = mybir.AxisListType

ENGS = ("pe", "act", "dve", "pool", "sp")

D = 1024
BATCH = 4
NLAT = 8192
NCTX = 256
S = NLAT + NCTX
NT = S // 128
DEPTH = 2
GRID_W = 64
EPS = 1e-6
IN_DIM = 7440
C_AQ, C_AK, C_AV, C_Z, C_XBC, C_DT, C_RQ, C_RK, C_RV, C_RG, C_GATE = (
    0, 512, 640, 768, 1280, 2304, 2320, 2832, 3344, 3856, 4368)
NEG = -30000.0


class Res:
    __slots__ = ("name", "last_w", "readers", "sem", "dma_total", "last_dma")

    def __init__(self, name):
        self.name = name
        self.last_w = None
        self.readers = []
        self.sem = None
        self.dma_total = 0
        self.last_dma = None


class Ins:
    __slots__ = ("eng", "fn", "deps", "signal", "count", "sem", "is_dma")

    def __init__(self, eng, fn, is_dma=False):
        self.eng = eng
        self.fn = fn
        self.deps = []
        self.signal = False
        self.count = None
        self.sem = None
        self.is_dma = is_dma


class Tl:
    __slots__ = ("ap", "res")

    def __init__(self, ap, name):
        self.ap = ap
        self.res = Res(name)


def _res(lst):
    return [x.res if isinstance(x, Tl) else x for x in lst]


class Prog:
    def __init__(self, nc):
        self.nc = nc
        self.streams = {e: [] for e in ENGS}
        self.eng_sem = {}
        self.stack = []
        self.slots = []
        self.phase_map = {}
        self.MAX_DMA_SEMS = 72

    def enter(self, cm):
        v = cm.__enter__()
        self.stack.append(cm)
        return v

    def close(self):
        for cm in reversed(self.stack):
            cm.__exit__(None, None, None)
        self.stack = []

    def sem(self, name):
        return self.enter(self.nc.semaphore(name))

    def _add_dep(self, ins, dep):
        if dep is None or dep is ins:
            return
        if dep.eng == "pe" and ins.eng == "pe" and not dep.is_dma and not ins.is_dma:
            return
        ins.deps.append(dep)
        dep.signal = True

    def _track(self, ins, reads, writes):
        for r in reads:
            self._add_dep(ins, r.last_w)
        for w in writes:
            self._add_dep(ins, w.last_w)
            for rd in w.readers:
                self._add_dep(ins, rd)
        for r in reads:
            r.readers.append(ins)
        for w in writes:
            w.last_w = ins
            w.readers = []

    def op(self, eng, fn, reads=(), writes=()):
        ins = Ins(eng, fn)
        self._track(ins, _res(reads), _res(writes))
        self.streams[eng].append(ins)
        return ins

    def dma(self, eng, out_ap, in_ap, semres, reads=(), writes=(), semkey=None, slow=False):
        if isinstance(semres, Tl):
            semres = semres.res
        if semres.sem is None:
            key = semkey or semres.name
            if semkey is None:
                m = re.match(r"^(.*?)(\d+)$", key)
                if m:
                    key = m.group(1) + str(int(m.group(2)) % 3)
            if key not in self.phase_map:
                idx = len(self.phase_map) % self.MAX_DMA_SEMS
                if idx >= len(self.slots):
                    self.slots.append([self.sem(f"d{idx}"), 0, None])
                self.phase_map[key] = self.slots[idx]
            semres.sem = self.phase_map[key]
        slot = semres.sem
        ins = Ins(eng, None, is_dma=True)
        ins.sem = slot[0]
        if slot[2] is not None:
            ins.deps.append(slot[2])
        slot[1] += 16
        ins.count = slot[1]
        slot[2] = ins
        ins.signal = True

        if slow:
            def fn(e, out_ap=out_ap, in_ap=in_ap):
                return e.dma_start(out=out_ap, in_=in_ap, allow_slow_non_contiguous=True)
        else:
            def fn(e, out_ap=out_ap, in_ap=in_ap):
                return e.dma_start(out=out_ap, in_=in_ap)
        ins.fn = fn
        self._track(ins, _res(reads), _res(writes))
        self.streams[eng].append(ins)
        return ins

    def barrier(self):
        lasts = []
        for e in ENGS:
            for i in reversed(self.streams[e]):
                if not i.is_dma:
                    lasts.append(i)
                    break
        dmas = [slot[2] for slot in self.slots if slot[2] is not None]
        self.phase_map = {}
        for e in ENGS:
            ins = Ins(e, lambda eng: eng.nop())
            for l in lasts:
                if l.eng != e:
                    ins.deps.append(l)
                    l.signal = True
            for d_ in dmas:
                ins.deps.append(d_)
            self.streams[e].append(ins)

    def emit(self):
        nc = self.nc
        for e in ENGS:
            if any(i.signal and not i.is_dma for i in self.streams[e]):
                self.eng_sem[e] = self.sem("e_" + e)
        for e in ENGS:
            c = 0
            for i in self.streams[e]:
                if i.is_dma:
                    continue
                if i.signal:
                    c += 1
                    i.count = c
                    i.sem = self.eng_sem[e]
        engmap = {"pe": "tensor", "act": "scalar", "dve": "vector", "pool": "gpsimd", "sp": "sync"}
        streams = self.streams
        self.n_instr = sum(len(v) for v in streams.values())

        def body(ename):
            def run(eng):
                known = {}
                for i in streams[ename]:
                    need = {}
                    for d_ in i.deps:
                        k = id(d_.sem)
                        if known.get(k, 0) >= d_.count:
                            continue
                        if k not in need or need[k][1] < d_.count:
                            need[k] = (d_.sem, d_.count)
                    needl = list(need.items())
                    for k, (s_, v) in needl[:-1]:
                        eng.wait_ge(s_, v)
                        known[k] = v
                    bi = i.fn(eng)
                    if needl:
                        k, (s_, v) = needl[-1]
                        bi._wait_ge(s_, v)
                        known[k] = v
                    if i.signal:
                        bi.then_inc(i.sem, 16 if i.is_dma else 1)
            return run

        with nc.Block() as block:
            for e in ENGS:
                if streams[e]:
                    getattr(block, engmap[e])(body(e))


class Deferred:
    def __init__(self):
        self.q = {}
        self.j = 0

    def at(self, k, fn):
        self.q.setdefault(self.j + k, []).append(fn)

    def step(self):
        for fn in self.q.pop(self.j, []):
            fn()
        self.j += 1

    def flush(self):
        while self.q:
            self.step()


class Arena:
    def __init__(self, ap, nelem_bf16):
        self.ap = ap
        self.cap = nelem_bf16
        self.off = 0
        self.peak = 0

    def reset(self):
        self.off = 0

    def alloc(self, name, n, dtype):
        nb = n * 2 if dtype == F32 else n
        nb = (nb + 1) // 2 * 2
        assert self.off + nb <= self.cap, f"arena overflow at {name}: {self.off}+{nb}>{self.cap}"
        ap = self.ap[:, self.off:self.off + nb]
        if dtype == F32:
            ap = ap.bitcast(F32)
        self.off += nb
        self.peak = max(self.peak, self.off)
        return Tl(ap, name)


class Builder:
    def __init__(self, debug=False, phases=None, layers=(0, 1)):
        self.debug = debug
        self.phases = phases
        self.layers = layers
        nc = bass.Bass("TRN2", target_bir_lowering=False)
        self.nc = nc
        self.P = Prog(nc)
        P = self.P
        ein = lambda n, sh, dt=F32: nc.dram_tensor(n, list(sh), dt, kind="ExternalInput").ap()
        self.x = ein("x", [NLAT, D])
        self.ctx = ein("ctx", [NCTX, D])
        self.c2 = ein("c2", [128, 16])
        self.w_mod = ein("w_mod", [DEPTH, D, 6 * D])
        self.b_mod = ein("b_mod", [DEPTH, 6 * D])
        self.norm1_w = ein("norm1_w", [DEPTH, D])
        self.norm2_w = ein("norm2_w", [DEPTH, D])
        self.w_in = ein("w_in", [DEPTH, D, IN_DIM])
        self.attn_q_norm = ein("attn_q_norm", [DEPTH, 64])
        self.attn_k_norm = ein("attn_k_norm", [DEPTH, 64])
        self.ssd_conv_w = ein("ssd_conv_w", [DEPTH, 3, 1024])
        self.ssd_conv_b = ein("ssd_conv_b", [DEPTH, 1024])
        self.ssd_dt_bias = ein("ssd_dt_bias", [DEPTH, 16])
        self.ssd_a_log = ein("ssd_a_log", [DEPTH, 16])
        self.ssd_d = ein("ssd_d", [DEPTH, 8])
        self.ssd_norm_w = ein("ssd_norm_w", [DEPTH, 512])
        self.ret_log_decay = ein("ret_log_decay", [DEPTH, 8])
        self.ret_gn_w = ein("ret_gn_w", [DEPTH, 512])
        self.w_branch = ein("w_branch", [DEPTH, 3, 512, D])
        self.w_out = ein("w_out", [DEPTH, D, D])
        self.w_mlp1 = ein("w_mlp1", [DEPTH, D, 4 * D])
        self.w_mlp2 = ein("w_mlp2", [DEPTH, 4 * D, D])
        self.final_norm_w = ein("final_norm_w", [1, D])
        self.ropeA = ein("ropeA", [S, 64])
        self.ropeR = ein("ropeR", [S, 128])
        self.cst = ein("cst", [128, 8 * 128])
        self.out = nc.dram_tensor("out", [NLAT, D], F32, kind="ExternalOutput").ap()

        kind = "ExternalOutput" if debug else "Internal"
        def scr(n, sh, dt):
            t = nc.dram_tensor(n, list(sh), dt, kind=kind).ap()
            return Tl(t, n)
        self.mod_d = scr("mod_d", [2, 6 * D], F32)
        self.h_d = scr("h_d", [S, D], F32)
        self.h1_d = scr("h1_d", [S, D], F32)
        self.qT_d = scr("qT_d", [4, 128, S], BF16)
        self.kT_d = scr("kT_d", [128, S], BF16)
        self.v_d = scr("v_d", [S, 128], BF16)
        self.z_d = scr("z_d", [S, 512], BF16)
        self.xbcT_d = scr("xbcT_d", [1024, S], BF16)
        self.dt_d = scr("dt_d", [S, 16], F32)
        self.rqT_d = scr("rqT_d", [4, 128, S], BF16)
        self.rkT_d = scr("rkT_d", [4, 128, S], BF16)
        self.rk_d = scr("rk_d", [S, 512], BF16)
        self.rv_d = scr("rv_d", [S, 512], BF16)
        self.rg_d = scr("rg_d", [S, 512], BF16)
        self.gate_d = scr("gate_d", [S, 3072], BF16)
        self.attnT_d = scr("attnT_d", [512, S], BF16)
        self.ssdT_d = scr("ssdT_d", [512, S], BF16)
        self.retT_d = scr("retT_d", [512, S], BF16)
        self.yf_d = scr("yf_d", [S, 512], F32)
        self.uT_d = scr("uT_d", [D, S], BF16)
        self.xcT_d = scr("xcT_d", [D, S], BF16)
        self.xb_d = scr("xb_d", [S, 768], BF16)
        self.in_res = Res("inputs")

        ar = P.enter(nc.sbuf_tensor("arena", [128, 98304], BF16))
        self.A = Arena(ar, 98304)
        psa = P.enter(nc.psum_tensor("psa", [128, 4096], F32))
        self.psa = psa
        self.ps = [Tl(psa[:, 512 * i:512 * (i + 1)], f"ps{i}") for i in range(8)]

    def new_phase(self):
        self.P.barrier()
        self.A.reset()
        self.ps = [Tl(self.psa[:, 512 * i:512 * (i + 1)], f"ps{i}") for i in range(8)]

    def mm(self, out, lhsT, rhs, start, stop, reads, writes):
        self.P.op("pe", lambda e: e.matmul(out, lhsT, rhs, start=start, stop=stop), reads, writes)

    def tr(self, out, in_, ident, reads, writes):
        self.P.op("pe", lambda e: e.transpose(out, in_, ident), reads, writes)

    def act(self, out, in_, func, reads, writes, bias=None, scale=None, accum=None):
        kw = {}
        if bias is not None:
            kw["bias"] = bias
        if scale is not None:
            kw["scale"] = scale
        if accum is not None:
            kw["accum_out"] = accum
        self.P.op("act", lambda e: e.activation(out, in_, func, **kw), reads, writes)

    def tt(self, eng, out, in0, in1, op, reads, writes):
        self.P.op(eng, lambda e: e.tensor_tensor(out, in0, in1, op), reads, writes)

    def ts(self, eng, out, in0, s1, s2, op0, op1, reads, writes):
        if op1 is None:
            self.P.op(eng, lambda e: e.tensor_scalar(out, in0, s1, None, op0), reads, writes)
        else:
            self.P.op(eng, lambda e: e.tensor_scalar(out, in0, s1, s2, op0, op1), reads, writes)

    def stt(self, eng, out, in0, scalar, in1, op0, op1, reads, writes):
        self.P.op(eng, lambda e: e.scalar_tensor_tensor(out, in0, scalar, in1, op0, op1), reads, writes)

    def cp(self, eng, out, in_, reads, writes):
        if eng == "act":
            self.P.op("act", lambda e: e.copy(out, in_), reads, writes)
        else:
            self.P.op(eng, lambda e: e.tensor_copy(out, in_), reads, writes)

    def rstd(self, out, ss, n, reads_writes_tile, tmp):
        t = reads_writes_tile
        self.ts("dve", tmp, ss, 1.0 / n, EPS, ALU.mult, ALU.add, [t], [t])
        self.act(tmp, tmp, AF.Sqrt, [t], [t])
        self.P.op("dve", lambda e: e.reciprocal(out, tmp), [t.res], [t.res])

    def hsrc(self, layer, t):
        if layer == 0:
            if t < 2:
                return self.ctx[t * 128:(t + 1) * 128, :], self.in_res
            return self.x[(t - 2) * 128:(t - 1) * 128, :], self.in_res
        return self.h_d.ap[t * 128:(t + 1) * 128, :], self.h_d.res

    def consts(self):
        A, P = self.A, self.P
        c = A.alloc("cst", 8 * 128, F32)
        P.dma("sp", c.ap, self.cst[:, :], c, reads=[self.in_res], writes=[c])
        self.C = c
        cv = c.ap.rearrange("p (a b) -> p a b", a=8)
        self.triU = cv[:, 0, :]
        self.triL = cv[:, 1, :]
        self.onesf = cv[:, 2, :]
        self.identf = cv[:, 3, :]
        self.negU = cv[:, 4, :]
        self.negL = cv[:, 5, :]
        self.posd = cv[:, 6, :]
        self.misc = cv[:, 7, :]
        cb = A.alloc("cstb", 3 * 128, BF16)
        cbv = cb.ap.rearrange("p (a b) -> p a b", a=3)
        self.Cb = cb
        self.ident = cbv[:, 0, :]
        self.negUb = cbv[:, 1, :]
        self.negLb = cbv[:, 2, :]
        self.cp("dve", self.ident, self.identf, [c], [cb])
        self.cp("dve", self.negUb, self.negU, [c], [cb])
        self.cp("dve", self.negLb, self.negL, [c], [cb])

    def bcast_row(self, dst_tile, dst_ap, src_ap, src_res, n_part=128):
        self.P.dma("sp", dst_ap, src_ap.partition_broadcast(n_part), dst_tile, reads=[src_res], writes=[dst_tile])

    def phase_mod(self, l):
        A, P = self.A, self.P
        self.new_phase()
        self.consts()
        cc = A.alloc("cc", 16, F32)
        ccv = cc.ap.rearrange("p (r k) -> p k r", r=2)
        P.dma("sp", cc.ap, self.c2[:, :], cc, reads=[self.in_res], writes=[cc])
        self.act(cc.ap, cc.ap, AF.Silu, [cc], [cc])
        bm = A.alloc("bm", 6 * D, F32)
        for r in range(2):
            P.dma("sp", bm.ap[r:r + 1, :], self.b_mod[l:l + 1, :], bm, reads=[self.in_res], writes=[bm])
        modsb = A.alloc("modsb", 6 * D, F32)
        wst = [A.alloc(f"wst{i}", 8 * 512, F32) for i in range(2)]
        wm = self.w_mod[l].rearrange("(k p) n -> p k n", p=128)
        for j in range(12):
            st = wst[j % 2]
            stv = st.ap.rearrange("p (k n) -> p k n", k=8)
            P.dma("sp" if j % 2 == 0 else "pool", stv, wm[:, :, j * 512:(j + 1) * 512], st, reads=[self.in_res], writes=[st])
            ps = self.ps[j % 2]
            for k in range(8):
                self.mm(ps.ap[0:2, :], ccv[:, k, :], stv[:, k, :], k == 0, k == 7, [cc, st], [ps])
            self.tt("dve", modsb.ap[0:2, j * 512:(j + 1) * 512], ps.ap[0:2, :], bm.ap[0:2, j * 512:(j + 1) * 512], ALU.add, [ps, bm], [modsb])
        P.dma("sp", self.mod_d.ap[:, :], modsb.ap[0:2, :], modsb, reads=[modsb], writes=[self.mod_d])

    def mod_tile(self, name, r, idx):
        t = self.A.alloc(name, D, F32)
        self.bcast_row(t, t.ap, self.mod_d.ap[r:r + 1, idx * D:(idx + 1) * D], self.mod_d.res)
        return t

    def gain_shift_tiles(self, l, norm_w_ap, idx_shift, idx_scale, rows):
        nw = self.A.alloc("nwb", D, F32)
        self.bcast_row(nw, nw.ap, norm_w_ap, self.in_res)
        out = {}
        for r in rows:
            g = self.mod_tile(f"G{r}", r, idx_scale)
            self.stt("dve", g.ap, g.ap, 1.0, nw.ap, ALU.add, ALU.mult, [g, nw], [g])
            sh = self.mod_tile(f"Sh{r}", r, idx_shift)
            out[r] = (g, sh)
        return out

    def load_weight_bf16(self, dst_tile, dst_view, src_view, K, N, chunk=None, stg_tiles=None, engs=None):
        P = self.P
        lst = []
        for k in range(K):
            lst.append(P.dma("pool", dst_view[:, k, :], src_view[:, k, :], Res("wld"), semkey=f"wld{k % 6}"))
        join = P.op("pool", lambda e: e.nop(), [], [dst_tile])
        join.deps.extend(lst)

    def mod_cols(self, l, norm_w, idx_shift, idx_scale, rows):
        A, P = self.A, self.P
        mc = A.alloc("mcols", 48, F32)
        mv = mc.ap[:, 0:32].rearrange("p (r a k) -> p r a k", r=2, a=2)
        nwc = mc.ap[:, 32:40]
        P.dma("sp", nwc, norm_w[l, :].rearrange("(k p) -> p k", p=128), mc, reads=[self.in_res], writes=[mc], slow=True)
        for r in rows:
            P.dma("sp", mv[:, r, 0, :], self.mod_d.ap[r, idx_scale * D:(idx_scale + 1) * D].rearrange("(k p) -> p k", p=128), mc, reads=[self.mod_d], writes=[mc], slow=True)
            P.dma("sp", mv[:, r, 1, :], self.mod_d.ap[r, idx_shift * D:(idx_shift + 1) * D].rearrange("(k p) -> p k", p=128), mc, reads=[self.mod_d], writes=[mc], slow=True)
            self.stt("dve", mv[:, r, 0, :], mv[:, r, 0, :], 1.0, nwc, ALU.add, ALU.mult, [mc], [mc])
        return mc, mv

    def norm_mod_T(self, h_, s_, hn, pst, mc, mv, r, dstv, dst_tile, col0, dq=None):
        self.act(hn.ap, h_.ap, AF.Square, [h_], [hn, s_], accum=s_.ap[:, 0:1])
        self.rstd(s_.ap[:, 1:2], s_.ap[:, 0:1], D, s_, s_.ap[:, 2:3])
        self.ts("dve", hn.ap, h_.ap, s_.ap[:, 1:2], None, ALU.mult, None, [h_, s_], [hn])

        def part2():
            for k in range(8):
                self.tr(pst.ap[:, k * 128:(k + 1) * 128], hn.ap[:, k * 128:(k + 1) * 128], self.identf, [hn, self.C], [pst])
            for k in range(8):
                self.act(dstv[:, k, col0:col0 + 128], pst.ap[:, k * 128:(k + 1) * 128], AF.Identity, [pst, mc], [dst_tile],
                         scale=mv[:, r, 0, k:k + 1], bias=mv[:, r, 1, k:k + 1])
        if dq is None:
            part2()
        else:
            dq.at(1, part2)

    def phase_a0(self, l):
        A, P = self.A, self.P
        self.new_phase()
        self.consts()
        mc, mv = self.mod_cols(l, self.norm1_w, 0, 1, (0, 1))
        NHB = 4
        ht = [A.alloc(f"ht{i}", D, F32) for i in range(NHB)]
        hn = [A.alloc(f"hn{i}", D, F32) for i in range(3)]
        sm = [A.alloc(f"sm{i}", 8, F32) for i in range(NHB)]
        uTs = [A.alloc(f"uTs{i}", 8 * 512, BF16) for i in range(3)]
        pst = [Tl(self.psa[:, 0:1024], "pst0"), Tl(self.psa[:, 1024:2048], "pst1"), Tl(self.psa[:, 2048:3072], "pst2")]
        supers = [(0, 2)] + [(2 + 4 * j, 4) for j in range(16)]
        tiles = [(si, t0, nt, tc) for si, (t0, nt) in enumerate(supers) for tc in range(nt)]
        dst = self.uT_d.ap.rearrange("(k p) s -> p k s", p=128)
        dq = Deferred()

        def loadh(gi):
            if gi < len(tiles):
                si, t0, nt, tc = tiles[gi]
                src, sres = self.hsrc(l, t0 + tc)
                P.dma("sp", ht[gi % NHB].ap, src, ht[gi % NHB], reads=[sres], writes=[ht[gi % NHB]])
        loadh(0)
        loadh(1)
        for gi, (si, t0, nt, tc) in enumerate(tiles):
            loadh(gi + 2)
            W = nt * 128
            t = t0 + tc
            u = uTs[si % 3]
            uv = u.ap.rearrange("p (k w) -> p k w", k=8)
            self.norm_mod_T(ht[gi % NHB], sm[gi % NHB], hn[gi % 3], pst[gi % 3], mc, mv, 1 if t < 2 else 0, uv, u, tc * 128, dq=dq)
            if tc == nt - 1:
                dq.at(2, lambda u=u, uv=uv, t0=t0, W=W: P.dma("sp", dst[:, :, t0 * 128:t0 * 128 + W], uv[:, :, 0:W], u, reads=[u], writes=[self.uT_d]))
            dq.step()
        dq.flush()

    def a_common(self, l, c0, ncols, name):
        A = self.A
        w = A.alloc(name, 8 * ncols, BF16)
        wv = w.ap.rearrange("p (k n) -> p k n", k=8)
        self.load_weight_bf16(w, wv, self.w_in[l].rearrange("(k p) n -> p k n", p=128)[:, :, c0:c0 + ncols], 8, ncols)
        return w, wv

    def load_uT(self, uTt, t0, W, eng="sp"):
        uv = uTt.ap.rearrange("p (k w) -> p k w", k=8)
        self.P.dma(eng, uv[:, :, 0:W], self.uT_d.ap.rearrange("(k p) s -> p k s", p=128)[:, :, t0 * 128:t0 * 128 + W], uTt, reads=[self.uT_d], writes=[uTt])
        return uv

    SUPERS = [(0, 2)] + [(2 + 4 * j, 4) for j in range(16)]

    def phase_a1(self, l):
        A, P = self.A, self.P
        self.new_phase()
        self.consts()
        ident = self.ident
        wsb, wv = self.a_common(l, C_AQ, 768, "w_a1")
        qkw = A.alloc("qkw", 128, F32)
        self.bcast_row(qkw, qkw.ap[:, 0:64], self.attn_q_norm[l:l + 1, :], self.in_res)
        self.bcast_row(qkw, qkw.ap[:, 64:128], self.attn_k_norm[l:l + 1, :], self.in_res)
        NB = 3
        uT = [A.alloc(f"uT{i}", 8 * 512, BF16) for i in range(2)]
        rA = [A.alloc(f"rA{i}", 64, F32) for i in range(NB)]
        sm = [A.alloc(f"sm{i}", 64, F32) for i in range(NB)]
        sqt = [A.alloc(f"sqt{i}", 640, F32) for i in range(NB)]
        qn = [A.alloc(f"qn{i}", 640, F32) for i in range(NB)]
        rt = [A.alloc(f"rt{i}", 4 * 320, F32) for i in range(NB)]
        qr = [A.alloc(f"qr{i}", 640, BF16) for i in range(NB)]
        vt = [A.alloc(f"vt{i}", 128, BF16) for i in range(NB)]
        qTs = [A.alloc(f"qTs{i}", 4 * 512, BF16) for i in range(2)]
        kTs = [A.alloc(f"kTs{i}", 512, BF16) for i in range(2)]
        ps = self.ps
        ps_t = [ps[0], ps[1]]
        mmb = ps[2:8]
        mi = 0
        gi = 0
        dq = Deferred()
        uvs = {}

        def pre(si_):
            if si_ < len(self.SUPERS):
                uvs[si_] = self.load_uT(uT[si_ % 2], self.SUPERS[si_][0], self.SUPERS[si_][1] * 128, "sp")
        pre(0)
        for si, (t0, nt) in enumerate(self.SUPERS):
            W = nt * 128
            pre(si + 1)
            uTt = uT[si % 2]
            uv = uvs[si]
            qT, kT = qTs[si % 2], kTs[si % 2]
            qTv = qT.ap.rearrange("p (a w) -> p a w", a=4)
            for tc in range(nt):
                t = t0 + tc
                b = gi % NB
                gi += 1
                tok = slice(t * 128, (t + 1) * 128)
                P.dma("sp", rA[b].ap, self.ropeA[tok, :], rA[b], reads=[self.in_res], writes=[rA[b]])
                pq = mmb[mi % 6]; mi += 1
                pk = mmb[mi % 6]; mi += 1
                for k in range(8):
                    self.mm(pq.ap, uv[:, k, tc * 128:(tc + 1) * 128], wv[:, k, 0:512], k == 0, k == 7, [uTt, wsb], [pq])
                for k in range(8):
                    self.mm(pk.ap[:, 0:256], uv[:, k, tc * 128:(tc + 1) * 128], wv[:, k, 512:768], k == 0, k == 7, [uTt, wsb], [pk])
                self.act(sqt[b].ap[:, 0:512], pq.ap, AF.Square, [pq], [sqt[b]])
                self.act(sqt[b].ap[:, 512:640], pk.ap[:, 0:128], AF.Square, [pk], [sqt[b]])
                self.cp("act", vt[b].ap, pk.ap[:, 128:256], [pk], [vt[b]])
                P.op("dve", lambda e, o=sm[b].ap[:, 8:18], i=sqt[b].ap.rearrange("p (h d) -> p h d", d=64): e.tensor_reduce(o, i, AX.X, ALU.add), [sqt[b].res], [sm[b].res])
                self.rstd(sm[b].ap[:, 20:30], sm[b].ap[:, 8:18], 64, sm[b], sm[b].ap[:, 32:42])
                qn3 = qn[b].ap.rearrange("p (h d) -> p h d", d=64)
                self.tt("dve", qn3[:, 0:8, :], pq.ap.rearrange("p (h d) -> p h d", d=64), sm[b].ap[:, 20:28].unsqueeze(2).to_broadcast([128, 8, 64]), ALU.mult, [pq, sm[b]], [qn[b]])
                self.tt("dve", qn3[:, 8:10, :], pk.ap[:, 0:128].rearrange("p (h d) -> p h d", d=64), sm[b].ap[:, 28:30].unsqueeze(2).to_broadcast([128, 2, 64]), ALU.mult, [pk, sm[b]], [qn[b]])
                self.tt("dve", qn3[:, 0:8, :], qn3[:, 0:8, :], qkw.ap[:, 0:64].unsqueeze(1).to_broadcast([128, 8, 64]), ALU.mult, [qn[b], qkw], [qn[b]])
                self.tt("dve", qn3[:, 8:10, :], qn3[:, 8:10, :], qkw.ap[:, 64:128].unsqueeze(1).to_broadcast([128, 2, 64]), ALU.mult, [qn[b], qkw], [qn[b]])
                cosb = rA[b].ap[:, 0:32].unsqueeze(1).to_broadcast([128, 10, 32])
                sinb = rA[b].ap[:, 32:64].unsqueeze(1).to_broadcast([128, 10, 32])
                x1, x2 = qn3[:, :, 0:32], qn3[:, :, 32:64]
                rtv = rt[b].ap.rearrange("p (a h d) -> p a h d", a=4, h=10)
                qr3 = qr[b].ap.rearrange("p (h d) -> p h d", d=64)
                self.tt("dve", rtv[:, 0], x1, cosb, ALU.mult, [qn[b], rA[b]], [rt[b]])
                self.tt("dve", rtv[:, 1], x2, sinb, ALU.mult, [qn[b], rA[b]], [rt[b]])
                self.tt("dve", rtv[:, 2], x1, sinb, ALU.mult, [qn[b], rA[b]], [rt[b]])
                self.tt("dve", rtv[:, 3], x2, cosb, ALU.mult, [qn[b], rA[b]], [rt[b]])
                self.tt("dve", qr3[:, :, 0:32], rtv[:, 0], rtv[:, 1], ALU.subtract, [rt[b]], [qr[b]])
                self.tt("dve", qr3[:, :, 32:64], rtv[:, 2], rtv[:, 3], ALU.add, [rt[b]], [qr[b]])
                def fin(b=b, tc=tc, tok=tok, qT=qT, kT=kT, qTv=qTv, pt_=ps_t[gi % 2]):
                    p1 = pt_.ap.bitcast(BF16)
                    for a in range(5):
                        self.tr(p1[:, a * 128:(a + 1) * 128], qr[b].ap[:, a * 128:(a + 1) * 128], ident, [qr[b], self.Cb], [pt_])
                    self.cp("act", qTv[:, :, tc * 128:(tc + 1) * 128], p1[:, 0:512].rearrange("p (a w) -> p a w", a=4), [pt_], [qT])
                    self.cp("act", kT.ap[:, tc * 128:(tc + 1) * 128], p1[:, 512:640], [pt_], [kT])
                    P.dma("sp", self.v_d.ap[tok, :], vt[b].ap, vt[b], reads=[vt[b]], writes=[self.v_d])
                dq.at(1, fin)
                if tc == nt - 1:
                    def st(t0=t0, W=W, qT=qT, kT=kT, qTv=qTv):
                        tsl = slice(t0 * 128, t0 * 128 + W)
                        P.dma("sp", self.qT_d.ap.rearrange("a p s -> p a s")[:, :, tsl], qTv[:, :, 0:W], qT, reads=[qT], writes=[self.qT_d])
                        P.dma("sp", self.kT_d.ap[:, tsl], kT.ap[:, 0:W], kT, reads=[kT], writes=[self.kT_d])
                    dq.at(2, st)
                dq.step()
        dq.flush()

    def phase_a2(self, l):
        A, P = self.A, self.P
        self.new_phase()
        wsb, wv = self.a_common(l, C_Z, 1552, "w_a2")
        NB = 2
        uT = [A.alloc(f"uT{i}", 8 * 512, BF16) for i in range(2)]
        zt = [A.alloc(f"zt{i}", 512, BF16) for i in range(NB)]
        dtt = [A.alloc(f"dtt{i}", 16, F32) for i in range(NB)]
        xst = [A.alloc(f"xst{i}", 8 * 512, BF16) for i in range(2)]
        mmb = self.ps
        mi = 0
        gi = 0
        uvs = {}

        def pre(si_):
            if si_ < len(self.SUPERS):
                uvs[si_] = self.load_uT(uT[si_ % 2], self.SUPERS[si_][0], self.SUPERS[si_][1] * 128, "sp")
        pre(0)
        for si, (t0, nt) in enumerate(self.SUPERS):
            W = nt * 128
            pre(si + 1)
            uTt = uT[si % 2]
            uv = uvs[si]
            xs = xst[si % 2]
            xsv = xs.ap.rearrange("p (c w) -> p c w", c=8)
            for c8 in range(8):
                pb = mmb[mi % 8]; mi += 1
                for k in range(8):
                    self.mm(pb.ap[:, 0:W], wv[:, k, 512 + c8 * 128:512 + (c8 + 1) * 128], uv[:, k, 0:W], k == 0, k == 7, [wsb, uTt], [pb])
                self.cp("dve", xsv[:, c8, 0:W], pb.ap[:, 0:W], [pb], [xs])
            P.dma("pool", self.xbcT_d.ap.rearrange("(c p) s -> p c s", p=128)[:, :, t0 * 128:t0 * 128 + W], xsv[:, :, 0:W], xs, reads=[xs], writes=[self.xbcT_d])
            for tc in range(nt):
                t = t0 + tc
                b = gi % NB
                gi += 1
                tok = slice(t * 128, (t + 1) * 128)
                pz = mmb[mi % 8]; mi += 1
                for k in range(8):
                    self.mm(pz.ap, uv[:, k, tc * 128:(tc + 1) * 128], wv[:, k, 0:512], k == 0, k == 7, [uTt, wsb], [pz])
                self.act(zt[b].ap, pz.ap, AF.Silu, [pz], [zt[b]])
                pd = mmb[mi % 8]; mi += 1
                for k in range(8):
                    self.mm(pd.ap[:, 0:16], uv[:, k, tc * 128:(tc + 1) * 128], wv[:, k, 1536:1552], k == 0, k == 7, [uTt, wsb], [pd])
                self.cp("dve", dtt[b].ap, pd.ap[:, 0:16], [pd], [dtt[b]])
                P.dma("sp", self.z_d.ap[tok, :], zt[b].ap, zt[b], reads=[zt[b]], writes=[self.z_d])
                P.dma("sp", self.dt_d.ap[tok, :], dtt[b].ap, dtt[b], reads=[dtt[b]], writes=[self.dt_d])

    def phase_a3(self, l):
        A, P = self.A, self.P
        self.new_phase()
        self.consts()
        ident = self.ident
        wsb, wv = self.a_common(l, C_RQ, 2048, "w_a3")
        NB = 3
        dq = Deferred()
        uT = [A.alloc(f"uT{i}", 8 * 512, BF16) for i in range(2)]
        rR = [A.alloc(f"rR{i}", 128, F32) for i in range(NB)]
        rt2 = [A.alloc(f"rtb{i}", 4 * 512, F32) for i in range(NB)]
        rqk = [A.alloc(f"rqk{i}", 1024, BF16) for i in range(NB)]
        rvg = [A.alloc(f"rvg{i}", 1024, BF16) for i in range(NB)]
        rqTs = [A.alloc(f"rqTs{i}", 4 * 512, BF16) for i in range(2)]
        rkTs = [A.alloc(f"rkTs{i}", 4 * 512, BF16) for i in range(2)]
        ps = self.ps
        ps_t = [ps[0], ps[1]]
        mmb = ps[2:8]
        mi = 0
        gi = 0
        uvs = {}

        def pre(si_):
            if si_ < len(self.SUPERS):
                uvs[si_] = self.load_uT(uT[si_ % 2], self.SUPERS[si_][0], self.SUPERS[si_][1] * 128, "sp")
        pre(0)
        for si, (t0, nt) in enumerate(self.SUPERS):
            W = nt * 128
            pre(si + 1)
            uTt = uT[si % 2]
            uv = uvs[si]
            rqT, rkT = rqTs[si % 2], rkTs[si % 2]
            rqTv = rqT.ap.rearrange("p (a w) -> p a w", a=4)
            rkTv = rkT.ap.rearrange("p (a w) -> p a w", a=4)
            for tc in range(nt):
                t = t0 + tc
                b = gi % NB
                gi += 1
                tok = slice(t * 128, (t + 1) * 128)
                P.dma("sp", rR[b].ap, self.ropeR[tok, :], rR[b], reads=[self.in_res], writes=[rR[b]])
                banks = []
                for j in range(4):
                    pb = mmb[mi % 6]; mi += 1
                    banks.append(pb)
                    for k in range(8):
                        self.mm(pb.ap, uv[:, k, tc * 128:(tc + 1) * 128], wv[:, k, j * 512:(j + 1) * 512], k == 0, k == 7, [uTt, wsb], [pb])
                cosr = rR[b].ap[:, 0:64].unsqueeze(1).to_broadcast([128, 4, 64])
                sinr = rR[b].ap[:, 64:128].unsqueeze(1).to_broadcast([128, 4, 64])
                r2 = rt2[b].ap.rearrange("p (a h d) -> p a h d", a=4, h=8)
                rq3 = rqk[b].ap.rearrange("p (h d) -> p h d", d=128)
                for j in range(2):
                    x3 = banks[j].ap.rearrange("p (h d) -> p h d", d=128)
                    y1, y2 = x3[:, :, 0:64], x3[:, :, 64:128]
                    hs = slice(j * 4, (j + 1) * 4)
                    self.tt("dve", r2[:, 0, hs], y1, cosr, ALU.mult, [banks[j], rR[b]], [rt2[b]])
                    self.tt("dve", r2[:, 1, hs], y2, sinr, ALU.mult, [banks[j], rR[b]], [rt2[b]])
                    self.tt("dve", r2[:, 2, hs], y1, sinr, ALU.mult, [banks[j], rR[b]], [rt2[b]])
                    self.tt("dve", r2[:, 3, hs], y2, cosr, ALU.mult, [banks[j], rR[b]], [rt2[b]])
                self.tt("pool", rq3[:, :, 0:64], r2[:, 0], r2[:, 1], ALU.subtract, [rt2[b]], [rqk[b]])
                self.tt("pool", rq3[:, :, 64:128], r2[:, 2], r2[:, 3], ALU.add, [rt2[b]], [rqk[b]])
                self.cp("act", rvg[b].ap[:, 0:512], banks[2].ap, [banks[2]], [rvg[b]])
                self.act(rvg[b].ap[:, 512:1024], banks[3].ap, AF.Silu, [banks[3]], [rvg[b]])
                def fin(b=b, tc=tc, tok=tok, rqT=rqT, rkT=rkT, rqTv=rqTv, rkTv=rkTv, pt_=ps_t[gi % 2]):
                    p2 = pt_.ap.bitcast(BF16)
                    for a in range(8):
                        self.tr(p2[:, a * 128:(a + 1) * 128], rqk[b].ap[:, a * 128:(a + 1) * 128], ident, [rqk[b], self.Cb], [pt_])
                    self.cp("act", rqTv[:, :, tc * 128:(tc + 1) * 128], p2[:, 0:512].rearrange("p (a w) -> p a w", a=4), [pt_], [rqT])
                    self.cp("act", rkTv[:, :, tc * 128:(tc + 1) * 128], p2[:, 512:1024].rearrange("p (a w) -> p a w", a=4), [pt_], [rkT])
                    P.dma("sp", self.rk_d.ap[tok, :], rqk[b].ap[:, 512:1024], rqk[b], reads=[rqk[b]], writes=[self.rk_d])
                    P.dma("sp", self.rv_d.ap[tok, :], rvg[b].ap[:, 0:512], rvg[b], reads=[rvg[b]], writes=[self.rv_d])
                    P.dma("sp", self.rg_d.ap[tok, :], rvg[b].ap[:, 512:1024], rvg[b], reads=[rvg[b]], writes=[self.rg_d])
                dq.at(1, fin)
                if tc == nt - 1:
                    def st(t0=t0, W=W, rqT=rqT, rkT=rkT, rqTv=rqTv, rkTv=rkTv):
                        tsl = slice(t0 * 128, t0 * 128 + W)
                        P.dma("sp", self.rqT_d.ap.rearrange("a p s -> p a s")[:, :, tsl], rqTv[:, :, 0:W], rqT, reads=[rqT], writes=[self.rqT_d])
                        P.dma("sp", self.rkT_d.ap.rearrange("a p s -> p a s")[:, :, tsl], rkTv[:, :, 0:W], rkT, reads=[rkT], writes=[self.rkT_d])
                    dq.at(2, st)
                dq.step()
        dq.flush()

    def phase_a4(self, l):
        A, P = self.A, self.P
        self.new_phase()
        wsb, wv = self.a_common(l, C_GATE, 3072, "w_a4")
        NB = 2
        uT = [A.alloc(f"uT{i}", 8 * 512, BF16) for i in range(2)]
        gat = [A.alloc(f"gat{i}", 3072, BF16) for i in range(NB)]
        mmb = self.ps
        mi = 0
        gi = 0
        uvs = {}

        def pre(si_):
            if si_ < len(self.SUPERS):
                uvs[si_] = self.load_uT(uT[si_ % 2], self.SUPERS[si_][0], self.SUPERS[si_][1] * 128, "sp")
        pre(0)
        for si, (t0, nt) in enumerate(self.SUPERS):
            W = nt * 128
            pre(si + 1)
            uTt = uT[si % 2]
            uv = uvs[si]
            for tc in range(nt):
                t = t0 + tc
                b = gi % NB
                gi += 1
                tok = slice(t * 128, (t + 1) * 128)
                for gch in range(6):
                    pg = mmb[mi % 8]; mi += 1
                    for k in range(8):
                        self.mm(pg.ap, uv[:, k, tc * 128:(tc + 1) * 128], wv[:, k, gch * 512:(gch + 1) * 512], k == 0, k == 7, [uTt, wsb], [pg])
                    self.act(gat[b].ap[:, gch * 512:(gch + 1) * 512], pg.ap, AF.Sigmoid, [pg], [gat[b]])
                P.dma("pool", self.gate_d.ap[tok, :], gat[b].ap, gat[b], reads=[gat[b]], writes=[self.gate_d])

    def phase_b(self, l, need_ctx):
        A, P = self.A, self.P
        self.new_phase()
        kTz = A.alloc("kTz", 4 * S, BF16)
        kv = kTz.ap.rearrange("p (a s) -> p a s", a=4)
        P.op("pool", lambda e: e.memset(kTz.ap, 0.0), [], [kTz])
        for g in range(2):
            src = self.kT_d.ap[g * 64:(g + 1) * 64, :]
            P.dma("sp", kv[0:64, g * 2 + 0, :], src, kTz, reads=[self.kT_d], writes=[kTz], semkey=f"kTz{g}a")
            P.dma("pool", kv[64:128, g * 2 + 1, :], src, kTz, reads=[self.kT_d], writes=[kTz], semkey=f"kTz{g}b")
        V1 = A.alloc("V1", NT * 2 * 128, BF16)
        V1v = V1.ap.rearrange("p (t g c) -> p t g c", t=NT, g=2)
        P.op("pool", lambda e: e.memset(V1.ap, 1.0), [], [V1])
        vsrc = self.v_d.ap.rearrange("(t p) (g c) -> p t g c", p=128, g=2)
        for j in range(0, NT, 6):
            for g in range(2):
                P.dma("sp" if g == 0 else "pool", V1v[:, j:j + 6, g, 0:64], vsrc[:, j:j + 6, g, :], V1, reads=[self.v_d], writes=[V1], semkey=f"V1_{g}")
        qc = [A.alloc(f"qc{i}", 512, BF16) for i in range(3)]
        NP = 4
        pb = [A.alloc(f"pb{i}", 512, BF16) for i in range(NP)]
        rec = [A.alloc(f"rec{i}", 512, F32) for i in range(2)]
        aT = [A.alloc(f"aT{i}", 512, BF16) for i in range(2)]
        NS = 6
        pss = self.ps[0:NS]
        pso = self.ps[NS:NS + 2]
        LAG = 2
        chunks = ([(0, 256)] if need_ctx else []) + [(256 + 512 * j, 512) for j in range(16)]
        it = 0
        hi = 0
        units = [(tok0, W, p) for (tok0, W) in chunks for p in range(4)]

        def loadq(ui):
            if ui < len(units):
                tok0, W, p = units[ui]
                q = qc[ui % 3]
                P.dma("sp", q.ap[:, 0:W], self.qT_d.ap[p, :, tok0:tok0 + W], q, reads=[self.qT_d], writes=[q])
        loadq(0)
        loadq(1)
        for ui, (tok0, W, p) in enumerate(units):
            loadq(ui + 2)
            kts = list(range(2)) if tok0 == 0 else list(range(NT))
            n = len(kts)
            q = qc[ui % 3]
            at = aT[ui % 2]
            g = p // 2
            for half in range(2):
                po = pso[hi % 2]
                rc = rec[hi % 2]
                hi += 1
                its = []
                for ii in range(n + LAG):
                    if ii < n:
                        kt = kts[ii]
                        sb = pss[it % NS]
                        pbuf = pb[it % NP]
                        its.append((sb, pbuf))
                        it += 1
                        self.mm(sb.ap[:, 0:W], kv[:, g * 2 + half, kt * 128:(kt + 1) * 128], q.ap[:, 0:W], True, True, [kTz, q], [sb])
                        self.act(pbuf.ap[:, 0:W], sb.ap[:, 0:W], AF.Exp, [sb], [pbuf], scale=0.125)
                    jj = ii - LAG
                    if jj >= 0:
                        kt = kts[jj]
                        sb, pbuf = its[jj]
                        self.mm(po.ap[:, 0:W], V1v[:, kt, g, :], pbuf.ap[:, 0:W], jj == 0, jj == n - 1, [V1, pbuf], [po])
                P.op("dve", lambda e, o=rc.ap[64:128, 0:W], i=po.ap[64:128, 0:W]: e.reciprocal(o, i), [po.res], [rc.res])
                self.tt("dve", at.ap[half * 64:(half + 1) * 64, 0:W], po.ap[0:64, 0:W], rc.ap[64:128, 0:W], ALU.mult, [po, rc], [at])
            P.dma("pool", self.attnT_d.ap[p * 128:(p + 1) * 128, tok0:tok0 + W], at.ap[:, 0:W], at, reads=[at], writes=[self.attnT_d])

    def phase_c0(self, l):
        A, P = self.A, self.P
        self.new_phase()
        self.consts()
        ident = self.ident
        cw = A.alloc("convw", 8 * 4, F32)
        cwv = cw.ap.rearrange("p (c k) -> p c k", k=4)
        for kk in range(3):
            P.dma("sp", cwv[:, :, kk], self.ssd_conv_w[l, kk, :].rearrange("(c p) -> p c", p=128), cw, reads=[self.in_res], writes=[cw], slow=True)
        P.dma("sp", cwv[:, :, 3], self.ssd_conv_b[l, :].rearrange("(c p) -> p c", p=128), cw, reads=[self.in_res], writes=[cw], slow=True)
        PW = 1024
        xr = [A.alloc(f"xr{i}", 8 * (PW + 2), BF16) for i in range(2)]
        xo = [A.alloc(f"xo{i}", 8 * PW, BF16) for i in range(2)]
        acc = [A.alloc(f"cacc{i}", PW, F32) for i in range(2)]
        tst = [A.alloc(f"tst{i}", 768, BF16) for i in range(3)]
        ptr = [self.ps[0], self.ps[1]]
        pieces = [(0, 256)] + [(256 + PW * j, PW) for j in range(8)]
        xsrc = self.xbcT_d.ap.rearrange("(c p) s -> p c s", p=128)
        xdst = self.xcT_d.ap.rearrange("(c p) s -> p c s", p=128)
        ai = 0
        ti = 0
        for pi, (s0, w) in enumerate(pieces):
            x_ = xr[pi % 2]
            xv = x_.ap.rearrange("p (c s) -> p c s", c=8)
            o_ = xo[pi % 2]
            ov = o_.ap.rearrange("p (c s) -> p c s", c=8)
            left_edge = s0 in (0, 256)
            right_edge = (s0 + w) in (256, S)
            lo = s0 if left_edge else s0 - 1
            hi_ = s0 + w if right_edge else s0 + w + 1
            if left_edge:
                P.op("pool", lambda e, a=xv[:, :, 0:1]: e.memset(a, 0.0), [], [x_])
            if right_edge:
                P.op("pool", lambda e, a=xv[:, :, w + 1:w + 2]: e.memset(a, 0.0), [], [x_])
            d0 = 1 if left_edge else 0
            P.dma("sp", xv[:, :, d0:d0 + (hi_ - lo)], xsrc[:, :, lo:hi_], x_, reads=[self.xbcT_d], writes=[x_])
            for c8 in range(8):
                ac = acc[ai % 2]
                ai += 1
                a_ = ac.ap[:, 0:w]
                self.ts("dve", a_, xv[:, c8, 1:w + 1], cwv[:, c8, 1:2], cwv[:, c8, 3:4], ALU.mult, ALU.add, [x_, cw], [ac])
                self.stt("dve", a_, xv[:, c8, 0:w], cwv[:, c8, 0:1], a_, ALU.mult, ALU.add, [x_, cw, ac], [ac])
                self.stt("dve", a_, xv[:, c8, 2:w + 2], cwv[:, c8, 2:3], a_, ALU.mult, ALU.add, [x_, cw, ac], [ac])
                self.act(ov[:, c8, 0:w], a_, AF.Silu, [ac], [o_])
            P.dma("pool", xdst[:, :, s0:s0 + w], ov[:, :, 0:w], o_, reads=[o_], writes=[self.xcT_d])
            for tt_ in range(w // 128):
                pt_ = ptr[ti % 2]
                st_ = tst[ti % 3]
                ti += 1
                pv = pt_.ap.bitcast(BF16)
                for k in range(6):
                    self.tr(pv[:, k * 128:(k + 1) * 128], ov[:, k, tt_ * 128:(tt_ + 1) * 128], ident, [o_, self.Cb], [pt_])
                self.cp("pool" if False else "act", st_.ap, pv[:, 0:768], [pt_], [st_])
                r0 = s0 + tt_ * 128
                P.dma("sp", self.xb_d.ap[r0:r0 + 128, :], st_.ap, st_, reads=[st_], writes=[self.xb_d])

    def phase_c(self, l, need_ctx):
        A, P = self.A, self.P
        self.new_phase()
        self.consts()
        ident = self.ident
        par = A.alloc("cpar", 64, F32)
        self.bcast_row(par, par.ap[:, 0:16], self.ssd_dt_bias[l:l + 1, :], self.in_res)
        self.bcast_row(par, par.ap[:, 16:32], self.ssd_a_log[l:l + 1, :], self.in_res)
        self.bcast_row(par, par.ap[:, 32:40], self.ssd_d[l:l + 1, :], self.in_res)
        self.act(par.ap[:, 16:32], par.ap[:, 16:32], AF.Exp, [par], [par])
        self.ts("dve", par.ap[:, 16:32], par.ap[:, 16:32], -1.0, None, ALU.mult, None, [par], [par])
        nwb = A.alloc("ssdnw", 512, F32)
        self.bcast_row(nwb, nwb.ap, self.ssd_norm_w[l:l + 1, :], self.in_res)
        dta = A.alloc("dta", NT * 16, F32)
        dtv = dta.ap.rearrange("p (t c) -> p t c", c=16)
        P.dma("sp", dtv, self.dt_d.ap.rearrange("(t p) c -> p t c", p=128), dta, reads=[self.dt_d], writes=[dta])
        self.tt("dve", dtv, dtv, par.ap[:, 0:16].unsqueeze(1).to_broadcast([128, NT, 16]), ALU.add, [dta, par], [dta])
        self.act(dta.ap, dta.ap, AF.Exp, [dta], [dta])
        self.ts("dve", dta.ap, dta.ap, 1.0, None, ALU.add, None, [dta], [dta])
        self.act(dta.ap, dta.ap, AF.Ln, [dta], [dta])
        aa = A.alloc("aa", NT * 16, F32)
        aav = aa.ap.rearrange("p (t c) -> p t c", c=16)
        self.tt("dve", aav, dtv, par.ap[:, 16:32].unsqueeze(1).to_broadcast([128, NT, 16]), ALU.mult, [dta, par], [aa])
        NQ = 7
        tab = A.alloc("ctab", NQ * 2 * NT * 8, F32)
        tb = tab.ap.rearrange("p (q d t c) -> p q d t c", q=NQ, d=2, t=NT)
        HT = NT // 2
        for d in range(2):
            tri = self.triU if d == 0 else self.triL
            for hf in range(2):
                t0 = hf * HT
                pc_ = self.ps[(d * 2 + hf) % 4]
                rhs = aav[:, t0:t0 + HT, d * 8:(d + 1) * 8]
                self.mm(pc_.ap[:, 0:HT * 8].rearrange("p (t c) -> p t c", c=8), tri, rhs, True, True, [self.C, aa], [pc_])
                self.cp("dve", tb[:, 0, d, t0:t0 + HT, :], pc_.ap[:, 0:HT * 8].rearrange("p (t c) -> p t c", c=8), [pc_], [tab])
                po_ = self.ps[4 + (d * 2 + hf) % 4]
                self.mm(po_.ap[:, 0:HT * 8].rearrange("p (t c) -> p t c", c=8), self.onesf, rhs, True, True, [self.C, aa], [po_])
                self.cp("dve", tb[:, 1, d, t0:t0 + HT, :], po_.ap[:, 0:HT * 8].rearrange("p (t c) -> p t c", c=8), [po_], [tab])
        n2 = 2 * NT * 8
        flat = lambda q: tab.ap[:, q * n2:(q + 1) * n2]
        self.act(flat(2), flat(0), AF.Exp, [tab], [tab])
        self.tt("dve", flat(3), flat(1), flat(0), ALU.subtract, [tab], [tab])
        self.act(flat(3), flat(3), AF.Exp, [tab], [tab])
        self.act(flat(4), flat(1), AF.Exp, [tab], [tab])
        self.ts("dve", flat(5), flat(0), -1.0, None, ALU.mult, None, [tab], [tab])
        for d in range(2):
            self.tt("dve", tb[:, 6, d, :, :], tb[:, 3, d, :, :], dtv[:, :, d * 8:(d + 1) * 8], ALU.mult, [tab, dta], [tab])
        xcg = [A.alloc(f"xcg{i}", 8 * 512, BF16) for i in range(3)]
        xcsrc = self.xcT_d.ap.rearrange("(c p) s -> p c s", p=128)
        St = A.alloc("St", 512, F32)
        prevb = A.alloc("prevb", 512, BF16)
        LA = 2
        NB = 5
        NF = 7
        xbt = [A.alloc(f"xbt{i}", 768, BF16) for i in range(NB)]
        Rt = [A.alloc(f"Rt{i}", 8 * 128, F32) for i in range(2)]
        lm = [A.alloc(f"lm{i}", 8 * 128, F32) for i in range(2)]
        MT = [A.alloc(f"MT{i}", 8 * 128, BF16) for i in range(NB)]
        xd = [A.alloc(f"xd{i}", 512, BF16) for i in range(NB)]
        xdd = [A.alloc(f"xdd{i}", 512, BF16) for i in range(NB)]
        yt = [A.alloc(f"yt{i}", 512, F32) for i in range(NF)]
        yo = [A.alloc(f"yo{i}", 512, F32) for i in range(NF)]
        yfl = [A.alloc(f"yfl{i}", 512, F32) for i in range(NF)]
        zt = [A.alloc(f"zt{i}", 512, BF16) for i in range(NF)]
        yb = [A.alloc(f"yb{i}", 512, BF16) for i in range(NF)]
        smc = [A.alloc(f"smc{i}", 8, F32) for i in range(NF)]
        sst = [A.alloc(f"sst{i}", 4 * 512, BF16) for i in range(3)]
        junk = A.alloc("cjunk", 512, BF16)
        ps = self.ps
        ps_tr, ps_cb, ps_seg0, ps_seg1, ps_y, ps_o, ps_s = ps[0], ps[1], ps[2], ps[3], ps[4], ps[5], ps[6]
        gstate = {"key": None, "n": 0, "tile": None}
        ctr = {"a": 0}
        dq = Deferred()

        def info(t):
            is_ctx = t < 2
            grp = 0 if is_ctx else (t - 2) // 4 + 1
            return is_ctx, grp, ((not is_ctx) or need_ctx)

        def stageA(d, t):
            is_ctx, grp, want_y = info(t)
            i = ctr["a"]; ctr["a"] += 1
            b = i % NB
            bf = i % NF
            tok = slice(t * 128, (t + 1) * 128)
            tri = self.triU if d == 0 else self.triL
            negb = self.negUb if d == 0 else self.negLb
            P.dma("sp", xbt[b].ap, self.xb_d.ap[tok, :], xbt[b], reads=[self.xb_d], writes=[xbt[b]])
            xt3 = xbt[b].ap[:, 0:512].rearrange("p (h q) -> p h q", q=64)
            self.tt("dve", xdd[b].ap.rearrange("p (h q) -> p h q", q=64), xt3, tb[:, 6, d, t, :].unsqueeze(2).to_broadcast([128, 8, 64]), ALU.mult, [xbt[b], tab], [xdd[b]])
            cx = {"b": b, "bf": bf, "want_y": want_y}
            if not want_y:
                return cx
            if d == 1:
                P.dma("sp", yfl[bf].ap, self.yf_d.ap[tok, :], yfl[bf], reads=[self.yf_d], writes=[yfl[bf]])
                P.dma("sp", zt[bf].ap, self.z_d.ap[tok, :], zt[bf], reads=[self.z_d], writes=[zt[bf]])
            if gstate["key"] != (d, grp):
                gstate["key"] = (d, grp)
                gstate["n"] += 1
                xg_t = xcg[gstate["n"] % 3]
                gstate["tile"] = xg_t
                g0_ = 0 if is_ctx else 2 + (grp - 1) * 4
                gw_ = 256 if is_ctx else 512
                P.dma("sp", xg_t.ap.rearrange("p (c s) -> p c s", c=8)[:, :, 0:gw_], xcsrc[:, :, g0_ * 128:g0_ * 128 + gw_], xg_t, reads=[self.xcT_d], writes=[xg_t])
            xc = gstate["tile"]
            xcv = xc.ap.rearrange("p (c s) -> p c s", c=8)
            goff = (t % 2 if is_ctx else (t - 2) % 4) * 128
            gtok = slice(goff, goff + 128)
            cx["xc"] = xc
            cx["cT"] = [xcv[:, 6 + g, gtok] for g in range(2)]
            a_c = aav[:, t, d * 8:(d + 1) * 8]
            dt_c = dtv[:, t, d * 8:(d + 1) * 8]
            self.tt("dve", xd[b].ap.rearrange("p (h q) -> p h q", q=64), xt3, dt_c.unsqueeze(2).to_broadcast([128, 8, 64]), ALU.mult, [xbt[b], dta], [xd[b]])
            r_ = i % 2
            R3 = Rt[r_].ap.rearrange("p (h w) -> p h w", h=8)
            self.tt("pool", R3, tri.unsqueeze(1).to_broadcast([128, 8, 128]), a_c.unsqueeze(2).to_broadcast([128, 8, 128]), ALU.mult, [self.C, aa], [Rt[r_]])
            for g in range(2):
                self.mm(ps_cb.ap[:, g * 128:(g + 1) * 128], xcv[:, 4 + g, gtok], xcv[:, 6 + g, gtok], True, True, [xc], [ps_cb])
            lm3 = lm[r_].ap.rearrange("p (h w) -> p h w", h=8)
            MT3 = MT[b].ap.rearrange("p (h w) -> p h w", h=8)
            for g in range(2):
                pseg = ps_seg0 if g == 0 else ps_seg1
                self.mm(pseg.ap, self.onesf, Rt[r_].ap[:, g * 512:(g + 1) * 512], True, False, [self.C, Rt[r_]], [pseg])
                for hh in range(4):
                    self.mm(pseg.ap[:, hh * 128:(hh + 1) * 128], ident, negb, False, hh == 3, [self.Cb], [pseg])
                for hh in range(4):
                    h = g * 4 + hh
                    self.act(lm3[:, h, :], pseg.ap[:, hh * 128:(hh + 1) * 128], AF.Exp, [pseg, tab], [lm[r_]], bias=tb[:, 5, d, t, h:h + 1])
                self.tt("dve", MT3[:, g * 4:(g + 1) * 4, :], ps_cb.ap[:, g * 128:(g + 1) * 128].unsqueeze(1).to_broadcast([128, 4, 128]), lm3[:, g * 4:(g + 1) * 4, :], ALU.mult, [ps_cb, lm[r_]], [MT[b]])
            return cx

        def stageB(d, t, cx):
            is_ctx, grp, want_y = info(t)
            b, bf = cx["b"], cx["bf"]
            tok = slice(t * 128, (t + 1) * 128)
            xt3 = xbt[b].ap[:, 0:512].rearrange("p (h q) -> p h q", q=64)
            if want_y:
                for g in range(2):
                    self.mm(ps_o.ap[:, g * 256:(g + 1) * 256], cx["cT"][g], prevb.ap[:, g * 256:(g + 1) * 256], True, True, [cx["xc"], prevb], [ps_o])
            for g in range(2):
                self.mm(ps_s.ap[:, g * 256:(g + 1) * 256], xbt[b].ap[:, 512 + g * 128:512 + (g + 1) * 128], xdd[b].ap[:, g * 256:(g + 1) * 256], True, True, [xbt[b], xdd[b]], [ps_s])
            St3 = St.ap.rearrange("p (h q) -> p h q", q=64)
            self.tt("dve", St3, St3, tb[:, 4, d, t, :].unsqueeze(2).to_broadcast([128, 8, 64]), ALU.mult, [St, tab], [St])
            self.tt("dve", St.ap, ps_s.ap, St.ap, ALU.add, [ps_s, St], [St])
            self.cp("act", prevb.ap, St.ap, [St], [prevb])
            if not want_y:
                return
            MT3 = MT[b].ap.rearrange("p (h w) -> p h w", h=8)
            for h in range(8):
                self.mm(ps_y.ap[:, h * 64:(h + 1) * 64], MT3[:, h, :], xd[b].ap[:, h * 64:(h + 1) * 64], True, True, [MT[b], xd[b]], [ps_y])
            self.tt("dve", yo[bf].ap.rearrange("p (h q) -> p h q", q=64), ps_o.ap.rearrange("p (h q) -> p h q", q=64), tb[:, 2, d, t, :].unsqueeze(2).to_broadcast([128, 8, 64]), ALU.mult, [ps_o, tab], [yo[bf]])
            self.tt("dve", yt[bf].ap, ps_y.ap, yo[bf].ap, ALU.add, [ps_y, yo[bf]], [yt[bf]])
            if d == 0:
                dq.at(1, lambda: P.dma("sp", self.yf_d.ap[tok, :], yt[bf].ap, yt[bf], reads=[yt[bf]], writes=[self.yf_d]))
                return
            sm_ = smc[bf]

            def f1():
                self.tt("pool", yt[bf].ap, yt[bf].ap, yfl[bf].ap, ALU.add, [yt[bf], yfl[bf]], [yt[bf]])
                self.tt("pool", yo[bf].ap.rearrange("p (h q) -> p h q", q=64), xt3, par.ap[:, 32:40].unsqueeze(2).to_broadcast([128, 8, 64]), ALU.mult, [xbt[b], par], [yo[bf]])
                self.tt("pool", yt[bf].ap, yt[bf].ap, yo[bf].ap, ALU.add, [yt[bf], yo[bf]], [yt[bf]])

            def f2():
                self.tt("dve", yt[bf].ap, yt[bf].ap, zt[bf].ap, ALU.mult, [yt[bf], zt[bf]], [yt[bf]])
                self.act(junk.ap, yt[bf].ap, AF.Square, [yt[bf]], [junk, sm_], accum=sm_.ap[:, 0:1])

            def f3():
                self.ts("dve", sm_.ap[:, 2:3], sm_.ap[:, 0:1], 1.0 / 512, EPS, ALU.mult, ALU.add, [sm_], [sm_])
                self.act(sm_.ap[:, 2:3], sm_.ap[:, 2:3], AF.Sqrt, [sm_], [sm_])

            def f4():
                P.op("dve", lambda e: e.reciprocal(sm_.ap[:, 1:2], sm_.ap[:, 2:3]), [sm_.res], [sm_.res])
                self.stt("dve", yb[bf].ap, yt[bf].ap, sm_.ap[:, 1:2], nwb.ap, ALU.mult, ALU.mult, [yt[bf], sm_, nwb], [yb[bf]])

            def f5():
                ptr = ps_tr.ap.bitcast(BF16)
                for k in range(4):
                    self.tr(ptr[:, k * 128:(k + 1) * 128], yb[bf].ap[:, k * 128:(k + 1) * 128], ident, [yb[bf], self.Cb], [ps_tr])
                gsz = 2 if is_ctx else 4
                slot = t % gsz if is_ctx else (t - 2) % 4
                stg_ = sst[grp % 3]
                sv = stg_.ap.rearrange("p (k w) -> p k w", k=4)
                self.cp("act", sv[:, :, slot * 128:(slot + 1) * 128], ptr[:, 0:512].rearrange("p (k w) -> p k w", k=4), [ps_tr], [stg_])
                if slot == 0:
                    g0 = 0 if is_ctx else 2 + (grp - 1) * 4
                    dq.at(1, lambda: P.dma("sp", self.ssdT_d.ap.rearrange("(k p) s -> p k s", p=128)[:, :, g0 * 128:(g0 + gsz) * 128], sv[:, :, 0:gsz * 128], stg_, reads=[stg_], writes=[self.ssdT_d]))
            dq.at(1, f1); dq.at(2, f2); dq.at(3, f3); dq.at(4, f4); dq.at(5, f5)

        for d in range(2):
            P.op("pool", lambda e: e.memset(St.ap, 0.0), [], [St])
            P.op("pool", lambda e: e.memset(prevb.ap, 0.0), [], [prevb])
            order = list(range(NT)) if d == 0 else [1, 0] + list(range(NT - 1, 1, -1))
            pend = [stageA(d, order[j]) for j in range(LA)]
            for j, t in enumerate(order):
                if j + LA < len(order):
                    pend.append(stageA(d, order[j + LA]))
                stageB(d, t, pend.pop(0))
                dq.step()
            dq.flush()

    def phase_d(self, l, need_ctx):
        A, P = self.A, self.P
        self.new_phase()
        self.consts()
        ident = self.ident
        SC = 128.0 ** -0.5
        par = A.alloc("dpar", 64, F32)
        self.bcast_row(par, par.ap[:, 0:8], self.ret_log_decay[l:l + 1, :], self.in_res)
        self.act(par.ap[:, 0:8], par.ap[:, 0:8], AF.Exp, [par], [par])
        self.ts("dve", par.ap[:, 0:8], par.ap[:, 0:8], -1.0, None, ALU.mult, None, [par], [par])
        self.act(par.ap[:, 8:16], par.ap[:, 0:8], AF.Exp, [par], [par], scale=128.0)
        for d in range(2):
            pos = self.misc[:, 1:2] if d == 0 else self.misc[:, 0:1]
            self.ts("dve", par.ap[:, 16 + d * 4:20 + d * 4], par.ap[:, d * 4:d * 4 + 4], pos, None, ALU.mult, None, [par, self.C], [par])
        self.act(par.ap[:, 16:24], par.ap[:, 16:24], AF.Exp, [par], [par])
        self.ts("dve", par.ap[:, 16:24], par.ap[:, 16:24], SC, None, ALU.mult, None, [par], [par])
        gnw = A.alloc("gnw", 512, F32)
        self.bcast_row(gnw, gnw.ap, self.ret_gn_w[l:l + 1, :], self.in_res)
        dm = A.alloc("dmat", 8 * 128, F32)
        dm3 = dm.ap.rearrange("p (a w) -> p a w", a=8)
        qd = A.alloc("qdec", 8 * 128, F32)
        qd3 = qd.ap.rearrange("p (a w) -> p a w", a=8)
        pt = A.alloc("ptab", 4 * 128, F32)
        pt3 = pt.ap.rearrange("p (a w) -> p a w", a=4)
        self.ts("dve", pt3[:, 0, :], self.posd, 0.0, None, ALU.max, None, [self.C], [pt])
        self.ts("dve", pt3[:, 1, :], self.posd, -1.0, 0.0, ALU.mult, ALU.max, [self.C], [pt])
        self.ts("dve", pt3[:, 2, :], self.posd, self.misc[:, 0:1], 1.0, ALU.add, ALU.add, [self.C], [pt])
        self.ts("dve", pt3[:, 3, :], pt3[:, 2, :], -1.0, 129.0, ALU.mult, ALU.add, [pt], [pt])
        mk = A.alloc("dmask", 2 * 128, F32)
        mk3 = mk.ap.rearrange("p (a w) -> p a w", a=2)
        self.ts("dve", mk3[:, 0, :], self.posd, 0.0, SC, ALU.is_ge, ALU.mult, [self.C], [mk])
        self.ts("dve", mk3[:, 1, :], self.posd, 0.0, SC, ALU.is_le, ALU.mult, [self.C], [mk])
        for d in range(2):
            for h in range(4):
                a = d * 4 + h
                self.act(dm3[:, a, :], pt3[:, d, :], AF.Exp, [pt, par], [dm], scale=par.ap[:, a:a + 1])
                self.act(qd3[:, a, :], pt3[:, 2 + d, :], AF.Exp, [pt, par], [qd], scale=par.ap[:, a:a + 1])
            self.tt("dve", dm3[:, d * 4:(d + 1) * 4, :], dm3[:, d * 4:(d + 1) * 4, :], mk3[:, d, :].unsqueeze(1).to_broadcast([128, 4, 128]), ALU.mult, [dm, mk], [dm])
        Sr = A.alloc("Sr", 512, F32)
        prevb = A.alloc("rprev", 512, BF16)
        LA = 2
        NB = 5
        NF = 7
        qT = [A.alloc(f"rqT{i}", 512, BF16) for i in range(NB)]
        kT = [A.alloc(f"rkT{i}", 512, BF16) for i in range(NB)]
        kt_ = [A.alloc(f"rk{i}", 512, BF16) for i in range(NB)]
        vt_ = [A.alloc(f"rv{i}", 512, BF16) for i in range(NB)]
        ST = [A.alloc(f"rST{i}", 512, BF16) for i in range(NB)]
        qdT = [A.alloc(f"rqd{i}", 512, BF16) for i in range(NB)]
        kd = [A.alloc(f"rkd{i}", 512, BF16) for i in range(NB)]
        yt = [A.alloc(f"ryt{i}", 512, F32) for i in range(NF)]
        yfl = [A.alloc(f"ryf{i}", 512, F32) for i in range(NF)]
        yc = [A.alloc(f"ryc{i}", 512, F32) for i in range(NF)]
        ysq = [A.alloc(f"rysq{i}", 512, F32) for i in range(3)]
        gt = [A.alloc(f"rgt{i}", 512, BF16) for i in range(NF)]
        yb = [A.alloc(f"ryb{i}", 512, BF16) for i in range(NF)]
        smr = [A.alloc(f"smr{i}", 32, F32) for i in range(NF)]
        sst = [A.alloc(f"rsst{i}", 4 * 512, BF16) for i in range(3)]
        ps = self.ps
        ps_qk = [ps[0], ps[1]]
        ps_y = [ps[2], ps[3]]
        ps_s = [ps[4], ps[5]]
        ps_tr = ps[6]
        ctr = {"a": 0, "b": 0}
        dq = Deferred()

        def info(t):
            is_ctx = t < 2
            return is_ctx, ((not is_ctx) or need_ctx)

        def stageA(d, t):
            is_ctx, want_y = info(t)
            i = ctr["a"]; ctr["a"] += 1
            b = i % NB
            bf = i % NF
            tok = slice(t * 128, (t + 1) * 128)
            P.dma("sp", kt_[b].ap, self.rk_d.ap[tok, :], kt_[b], reads=[self.rk_d], writes=[kt_[b]])
            P.dma("sp", vt_[b].ap, self.rv_d.ap[tok, :], vt_[b], reads=[self.rv_d], writes=[vt_[b]])
            self.tt("dve", kd[b].ap.rearrange("p (h w) -> p h w", h=4), kt_[b].ap.rearrange("p (h w) -> p h w", h=4), par.ap[:, 16 + d * 4:20 + d * 4].unsqueeze(2).to_broadcast([128, 4, 128]), ALU.mult, [kt_[b], par], [kd[b]])
            if want_y:
                if d == 1:
                    P.dma("sp", yfl[bf].ap, self.yf_d.ap[tok, :], yfl[bf], reads=[self.yf_d], writes=[yfl[bf]])
                    P.dma("sp", gt[bf].ap, self.rg_d.ap[tok, :], gt[bf], reads=[self.rg_d], writes=[gt[bf]])
                pq = ps_qk[i % 2]
                P.dma("sp", qT[b].ap.rearrange("p (a w) -> p a w", a=4), self.rqT_d.ap.rearrange("a p s -> p a s")[:, :, tok], qT[b], reads=[self.rqT_d], writes=[qT[b]])
                P.dma("sp", kT[b].ap.rearrange("p (a w) -> p a w", a=4), self.rkT_d.ap.rearrange("a p s -> p a s")[:, :, tok], kT[b], reads=[self.rkT_d], writes=[kT[b]])
                for h in range(4):
                    self.mm(pq.ap[:, h * 128:(h + 1) * 128], kT[b].ap[:, h * 128:(h + 1) * 128], qT[b].ap[:, h * 128:(h + 1) * 128], True, True, [kT[b], qT[b]], [pq])
                self.tt("dve", ST[b].ap, pq.ap, dm.ap[:, d * 512:(d + 1) * 512], ALU.mult, [pq, dm], [ST[b]])
                self.tt("pool", qdT[b].ap, qT[b].ap, qd.ap[:, d * 512:(d + 1) * 512], ALU.mult, [qT[b], qd], [qdT[b]])
            return (b, bf)

        def stageB(d, t, cx):
            b, bf = cx
            is_ctx, want_y = info(t)
            i = ctr["b"]; ctr["b"] += 1
            b2 = i % 2
            tok = slice(t * 128, (t + 1) * 128)
            py, pst = ps_y[b2], ps_s[b2]
            if want_y:
                for h in range(4):
                    sl = slice(h * 128, (h + 1) * 128)
                    self.mm(py.ap[:, sl], qdT[b].ap[:, sl], prevb.ap[:, sl], True, False, [qdT[b], prevb], [py])
                    self.mm(py.ap[:, sl], ST[b].ap[:, sl], vt_[b].ap[:, sl], False, True, [ST[b], vt_[b]], [py])
            for h in range(4):
                sl = slice(h * 128, (h + 1) * 128)
                self.mm(pst.ap[:, sl], kd[b].ap[:, sl], vt_[b].ap[:, sl], True, True, [kd[b], vt_[b]], [pst])
            Sr3 = Sr.ap.rearrange("p (h w) -> p h w", h=4)
            self.tt("dve", Sr3, Sr3, par.ap[:, 8 + d * 4:12 + d * 4].unsqueeze(2).to_broadcast([128, 4, 128]), ALU.mult, [Sr, par], [Sr])
            self.tt("dve", Sr.ap, pst.ap, Sr.ap, ALU.add, [pst, Sr], [Sr])
            self.cp("act", prevb.ap, Sr.ap, [Sr], [prevb])
            if not want_y:
                return
            if d == 0:
                self.cp("act", yt[bf].ap, py.ap, [py], [yt[bf]])
                dq.at(1, lambda: P.dma("sp", self.yf_d.ap[tok, :], yt[bf].ap, yt[bf], reads=[yt[bf]], writes=[self.yf_d]))
                return
            sm_ = smr[bf]
            self.tt("dve", yt[bf].ap, py.ap, yfl[bf].ap, ALU.add, [py, yfl[bf]], [yt[bf]])
            y3 = yt[bf].ap.rearrange("p (h w) -> p h w", h=4)
            yc3 = yc[bf].ap.rearrange("p (h w) -> p h w", h=4)
            ysq_ = ysq[i % 3]

            def f1():
                P.op("dve", lambda e: e.tensor_reduce(sm_.ap[:, 0:4], y3, AX.X, ALU.add), [yt[bf].res], [sm_.res])
                self.ts("dve", sm_.ap[:, 4:8], sm_.ap[:, 0:4], -1.0 / 128, None, ALU.mult, None, [sm_], [sm_])

            def f2():
                self.tt("dve", yc3, y3, sm_.ap[:, 4:8].unsqueeze(2).to_broadcast([128, 4, 128]), ALU.add, [yt[bf], sm_], [yc[bf]])
                self.act(ysq_.ap, yc[bf].ap, AF.Square, [yc[bf]], [ysq_])

            def f3():
                P.op("dve", lambda e: e.tensor_reduce(sm_.ap[:, 8:12], ysq_.ap.rearrange("p (h w) -> p h w", h=4), AX.X, ALU.add), [ysq_.res], [sm_.res])
                self.ts("dve", sm_.ap[:, 16:20], sm_.ap[:, 8:12], 1.0 / 128, EPS, ALU.mult, ALU.add, [sm_], [sm_])
                self.act(sm_.ap[:, 16:20], sm_.ap[:, 16:20], AF.Sqrt, [sm_], [sm_])

            def f4():
                P.op("dve", lambda e: e.reciprocal(sm_.ap[:, 12:16], sm_.ap[:, 16:20]), [sm_.res], [sm_.res])
                self.tt("dve", yc3, yc3, sm_.ap[:, 12:16].unsqueeze(2).to_broadcast([128, 4, 128]), ALU.mult, [yc[bf], sm_], [yc[bf]])
                self.tt("pool", yc[bf].ap, yc[bf].ap, gnw.ap, ALU.mult, [yc[bf], gnw], [yc[bf]])
                self.tt("pool", yb[bf].ap, yc[bf].ap, gt[bf].ap, ALU.mult, [yc[bf], gt[bf]], [yb[bf]])

            def f5():
                ptr = ps_tr.ap.bitcast(BF16)
                for k in range(4):
                    self.tr(ptr[:, k * 128:(k + 1) * 128], yb[bf].ap[:, k * 128:(k + 1) * 128], ident, [yb[bf], self.Cb], [ps_tr])
                grp = 0 if is_ctx else (t - 2) // 4 + 1
                gsz = 2 if is_ctx else 4
                slot = t % gsz if is_ctx else (t - 2) % 4
                stg_ = sst[grp % 3]
                sv = stg_.ap.rearrange("p (k w) -> p k w", k=4)
                self.cp("act", sv[:, :, slot * 128:(slot + 1) * 128], ptr[:, 0:512].rearrange("p (k w) -> p k w", k=4), [ps_tr], [stg_])
                if slot == 0:
                    g0 = 0 if is_ctx else 2 + (grp - 1) * 4
                    dq.at(1, lambda: P.dma("sp", self.retT_d.ap.rearrange("(k p) s -> p k s", p=128)[:, :, g0 * 128:(g0 + gsz) * 128], sv[:, :, 0:gsz * 128], stg_, reads=[stg_], writes=[self.retT_d]))
            dq.at(1, f1); dq.at(2, f2); dq.at(3, f3); dq.at(4, f4); dq.at(6, f5)

        for d in range(2):
            P.op("pool", lambda e: e.memset(Sr.ap, 0.0), [], [Sr])
            P.op("pool", lambda e: e.memset(prevb.ap, 0.0), [], [prevb])
            order = list(range(NT)) if d == 0 else [1, 0] + list(range(NT - 1, 1, -1))
            pend = [stageA(d, order[j]) for j in range(LA)]
            for j, t in enumerate(order):
                if j + LA < len(order):
                    pend.append(stageA(d, order[j + LA]))
                stageB(d, t, pend.pop(0))
                dq.step()
            dq.flush()

    def phase_e1(self, l, need_ctx):
        A, P = self.A, self.P
        self.new_phase()
        self.consts()
        ident = self.ident
        wb = A.alloc("wbr", 3 * 4 * D, BF16)
        wbv = wb.ap.rearrange("p (a n) -> p a n", a=12)
        self.load_weight_bf16(wb, wbv, self.w_branch[l].rearrange("b (k p) n -> p (b k) n", p=128), 12, D)
        wo = A.alloc("wo", 8 * D, BF16)
        wov = wo.ap.rearrange("p (k n) -> p k n", k=8)
        self.load_weight_bf16(wo, wov, self.w_out[l].rearrange("(k p) n -> p k n", p=128), 8, D)
        rows = (0, 1) if need_ctx else (0,)
        al = {r: self.mod_tile(f"al{r}", r, 2) for r in rows}
        NG, NH, NM = 4, 6, 3
        brT = [[A.alloc(f"brT{i}_{j}", 4 * 512, BF16) for j in range(3)] for i in range(2)]
        gt = [A.alloc(f"gt{i}", 3072, BF16) for i in range(NG)]
        ht = [A.alloc(f"ht{i}", D, F32) for i in range(NH)]
        mg = [A.alloc(f"mg{i}", D, F32) for i in range(NM)]
        mgt = [A.alloc(f"mgt{i}", D, F32) for i in range(NM)]
        mgb = [A.alloc(f"mgb{i}", D, BF16) for i in range(NM)]
        mT = [A.alloc(f"mT{i}", D, BF16) for i in range(NM)]
        ps = self.ps
        ps_t = [ps[0], ps[1]]
        mmb = ps[2:8]
        mi = [0]
        supers = ([(0, 2)] if need_ctx else []) + [(2 + 4 * j, 4) for j in range(16)]
        srcs = [self.attnT_d, self.ssdT_d, self.retT_d]
        tiles = [(si, t0, nt, tc) for si, (t0, nt) in enumerate(supers) for tc in range(nt)]
        dq = Deferred()

        def load_super(si):
            if si >= len(supers):
                return
            t0, nt = supers[si]
            W = nt * 128
            bt = brT[si % 2]
            for j in range(3):
                P.dma("sp", bt[j].ap.rearrange("p (k w) -> p k w", k=4)[:, :, 0:W], srcs[j].ap.rearrange("(k p) s -> p k s", p=128)[:, :, t0 * 128:t0 * 128 + W], bt[j], reads=[srcs[j]], writes=[bt[j]])

        def load_tile(gi):
            if gi >= len(tiles):
                return
            si, t0, nt, tc = tiles[gi]
            t = t0 + tc
            tok = slice(t * 128, (t + 1) * 128)
            P.dma("sp", gt[gi % NG].ap, self.gate_d.ap[tok, :], gt[gi % NG], reads=[self.gate_d], writes=[gt[gi % NG]])
            src, sres = self.hsrc(l, t)
            P.dma("sp", ht[gi % NH].ap, src, ht[gi % NH], reads=[sres], writes=[ht[gi % NH]])

        def nb():
            pb = mmb[mi[0] % 6]
            mi[0] += 1
            return pb

        load_super(0)
        load_tile(0)
        load_tile(1)
        for gi, (si, t0, nt, tc) in enumerate(tiles):
            if tc == 0:
                load_super(si + 1)
            load_tile(gi + 2)
            t = t0 + tc
            r = 1 if t < 2 else 0
            tok = slice(t * 128, (t + 1) * 128)
            bt = brT[si % 2]
            g_, h_ = gt[gi % NG], ht[gi % NH]
            m_, mt_, mb_, mT_ = mg[gi % NM], mgt[gi % NM], mgb[gi % NM], mT[gi % NM]
            for j in range(3):
                bv = bt[j].ap.rearrange("p (k w) -> p k w", k=4)
                for nh in range(2):
                    pb = nb()
                    for k in range(4):
                        self.mm(pb.ap, bv[:, k, tc * 128:(tc + 1) * 128], wbv[:, j * 4 + k, nh * 512:(nh + 1) * 512], k == 0, k == 3, [bt[j], wb], [pb])
                    gsl = g_.ap[:, j * 1024 + nh * 512:j * 1024 + (nh + 1) * 512]
                    msl = slice(nh * 512, (nh + 1) * 512)
                    if j == 0:
                        self.tt("dve", m_.ap[:, msl], pb.ap, gsl, ALU.mult, [pb, g_], [m_])
                    else:
                        self.tt("dve", mt_.ap[:, msl], pb.ap, gsl, ALU.mult, [pb, g_], [mt_])
                        if j == 1:
                            self.tt("dve", m_.ap[:, msl], m_.ap[:, msl], mt_.ap[:, msl], ALU.add, [m_, mt_], [m_])
                        else:
                            self.tt("dve", mb_.ap[:, msl], m_.ap[:, msl], mt_.ap[:, msl], ALU.add, [m_, mt_], [mb_])

            def s2(mb_=mb_, mT_=mT_, pt_=ps_t[gi % 2]):
                ptr = pt_.ap.bitcast(BF16)
                for k in range(8):
                    self.tr(ptr[:, k * 128:(k + 1) * 128], mb_.ap[:, k * 128:(k + 1) * 128], ident, [mb_, self.Cb], [pt_])
                self.cp("act", mT_.ap, ptr[:, 0:1024], [pt_], [mT_])

            def s3(mT_=mT_, mt_=mt_, h_=h_, r=r):
                for nh in range(2):
                    pb = nb()
                    msl = slice(nh * 512, (nh + 1) * 512)
                    for k in range(8):
                        self.mm(pb.ap, mT_.ap[:, k * 128:(k + 1) * 128], wov[:, k, msl], k == 0, k == 7, [mT_, wo], [pb])
                    self.tt("dve", mt_.ap[:, msl], pb.ap, al[r].ap[:, msl], ALU.mult, [pb, al[r]], [mt_])
                    self.tt("pool", h_.ap[:, msl], h_.ap[:, msl], mt_.ap[:, msl], ALU.add, [h_, mt_], [h_])

            def s4(h_=h_, tok=tok):
                P.dma("sp", self.h1_d.ap[tok, :], h_.ap, h_, reads=[h_], writes=[self.h1_d])
            dq.at(1, s2); dq.at(2, s3); dq.at(3, s4)
            dq.step()
        dq.flush()

    def phase_e2(self, l, need_ctx, final):
        A, P = self.A, self.P
        self.new_phase()
        self.consts()
        w1 = A.alloc("w1", 8 * 4096, BF16)
        w1v = w1.ap.rearrange("p (k n) -> p k n", k=8)
        w2 = A.alloc("w2", 32 * 1024, BF16)
        w2v = w2.ap.rearrange("p (k n) -> p k n", k=32)
        self.load_weight_bf16(w1, w1v, self.w_mlp1[l].rearrange("(k p) n -> p k n", p=128), 8, 4096)
        self.load_weight_bf16(w2, w2v, self.w_mlp2[l].rearrange("(k p) n -> p k n", p=128), 32, 1024)
        fnw = None
        if final:
            fnw = A.alloc("fnw", D, F32)
            self.bcast_row(fnw, fnw.ap, self.final_norm_w[0:1, :], self.in_res)
        rows = (0, 1) if need_ctx else (0,)
        mc, mv = self.mod_cols(l, self.norm2_w, 3, 4, rows)
        Alt = A.alloc("Al2", D, F32)
        TW = 256
        NHB = 4
        ht = [A.alloc(f"ht{i}", D, F32) for i in range(NHB)]
        hn = [A.alloc(f"hn{i}", D, F32) for i in range(2)]
        sm = [A.alloc(f"sm{i}", 8, F32) for i in range(NHB)]
        vT = [A.alloc(f"vT{i}", 8 * TW, BF16) for i in range(2)]
        hT = A.alloc("hT", 32 * TW, BF16)
        rl = [A.alloc(f"rl{i}", TW, F32) for i in range(2)]
        pst = [Tl(self.psa[:, 0:1024], "pst")]
        mmb = self.ps[2:8]
        mi = 0
        supers = ([(0, 2, 1)] if need_ctx else []) + [(2 + 2 * j, 2, 0) for j in range(32)]
        cur_r = [None]
        gic = [0]
        ri = 0
        hsv = {}

        def load_h(si):
            if si >= len(supers):
                return
            t0, nt, r = supers[si]
            lst = []
            for tc in range(nt):
                t = t0 + tc
                gi = gic[0]; gic[0] += 1
                h_, s_ = ht[gi % NHB], sm[gi % NHB]
                P.dma("sp", h_.ap, self.h1_d.ap[t * 128:(t + 1) * 128, :], h_, reads=[self.h1_d], writes=[h_])
                lst.append((h_, s_, gi))
            hsv[si] = lst

        def norm1(si):
            if si >= len(supers):
                return
            t0, nt, r = supers[si]
            vTt = vT[si % 2]
            vTv = vTt.ap.rearrange("p (k w) -> p k w", k=8)
            for tc in range(nt):
                h_, s_, gi = hsv[si][tc]
                self.norm_mod_T(h_, s_, hn[gi % 2], pst[0], mc, mv, r, vTv, vTt, tc * 128)

        load_h(0)
        load_h(1)
        norm1(0)
        for si, (t0, nt, r) in enumerate(supers):
            if r != cur_r[0]:
                cur_r[0] = r
                self.bcast_row(Alt, Alt.ap, self.mod_d.ap[r:r + 1, 5 * D:6 * D], self.mod_d.res)
            W = nt * 128
            vTt = vT[si % 2]
            vTv = vTt.ap.rearrange("p (k w) -> p k w", k=8)
            hs = hsv[si]
            hTv = hT.ap.rearrange("p (f w) -> p f w", f=32)
            for f in range(32):
                if f == 16:
                    norm1(si + 1)
                pb = mmb[mi % 6]; mi += 1
                for k in range(8):
                    self.mm(pb.ap[:, 0:W], w1v[:, k, f * 128:(f + 1) * 128], vTv[:, k, 0:W], k == 0, k == 7, [w1, vTt], [pb])
                rl_ = rl[ri % 2]; ri += 1
                self.act(rl_.ap[:, 0:W], pb.ap[:, 0:W], AF.Relu, [pb], [rl_])
                self.tt("dve", hTv[:, f, 0:W], rl_.ap[:, 0:W], rl_.ap[:, 0:W], ALU.mult, [rl_], [hT])
            for tc in range(nt):
                t = t0 + tc
                h_, s_, gi = hs[tc]
                tok = slice(t * 128, (t + 1) * 128)
                hx = hn[gi % 2]
                for nh in range(2):
                    pb = mmb[mi % 6]; mi += 1
                    msl = slice(nh * 512, (nh + 1) * 512)
                    for f in range(32):
                        self.mm(pb.ap, hTv[:, f, tc * 128:(tc + 1) * 128], w2v[:, f, msl], f == 0, f == 31, [hT, w2], [pb])
                    self.tt("dve", hx.ap[:, msl], pb.ap, Alt.ap[:, msl], ALU.mult, [pb, Alt], [hx])
                    self.tt("pool", h_.ap[:, msl], h_.ap[:, msl], hx.ap[:, msl], ALU.add, [h_, hx], [h_])
                if not final:
                    P.dma("sp", self.h_d.ap[tok, :], h_.ap, h_, reads=[h_], writes=[self.h_d])
                else:
                    self.act(hx.ap, h_.ap, AF.Square, [h_], [hx, s_], accum=s_.ap[:, 4:5])
                    self.rstd(s_.ap[:, 5:6], s_.ap[:, 4:5], D, s_, s_.ap[:, 6:7])
                    self.stt("dve", h_.ap, h_.ap, s_.ap[:, 5:6], fnw.ap, ALU.mult, ALU.mult, [h_, s_, fnw], [h_])
                    P.dma("sp", self.out[(t - 2) * 128:(t - 1) * 128, :], h_.ap, h_, reads=[h_], writes=[])
            load_h(si + 2)

    def build(self):
        self.out_dmas = []
        ph = self.phases
        for l in self.layers:
            need_ctx = l < DEPTH - 1
            final = l == DEPTH - 1
            if ph is None or "M" in ph:
                self.phase_mod(l)
            for nm, fn in (("A0", self.phase_a0), ("A1", self.phase_a1), ("A2", self.phase_a2), ("A3", self.phase_a3), ("A4", self.phase_a4)):
                if ph is None or nm in ph:
                    fn(l)
            if ph is None or "B" in ph:
                self.phase_b(l, need_ctx)
            if ph is None or "C0" in ph:
                self.phase_c0(l)
            if ph is None or "C" in ph:
                self.phase_c(l, need_ctx)
            if ph is None or "D" in ph:
                self.phase_d(l, need_ctx)
            if ph is None or "E1" in ph:
                self.phase_e1(l, need_ctx)
            if ph is None or "E2" in ph:
                self.phase_e2(l, need_ctx, final)
        self.P.barrier()
        self.P.emit()
        self.P.close()
        return self.nc


def _tables():
    f32 = np.float32
    rows = NLAT // GRID_W
    row = np.repeat(np.arange(rows, dtype=f32), GRID_W)
    col = np.tile(np.arange(GRID_W, dtype=f32), rows)
    inv = (np.float32(10000.0) ** (-np.arange(16, dtype=f32) / np.float32(16))).astype(f32)
    ang = np.concatenate([row[:, None] * inv, col[:, None] * inv], axis=-1).astype(f32)
    ropeA = np.zeros((S, 64), f32)
    ropeA[:NCTX, 0:32] = 1.0
    ropeA[NCTX:, 0:32] = np.cos(ang)
    ropeA[NCTX:, 32:64] = np.sin(ang)
    pos = np.arange(S, dtype=f32)
    invr = (np.float32(10000.0) ** (-np.linspace(0.0, 1.0, 64, dtype=f32))).astype(f32)
    angr = (pos[:, None] * invr).astype(f32)
    ropeR = np.concatenate([np.cos(angr), np.sin(angr)], axis=-1).astype(f32)
    j = np.arange(128)[:, None]
    l = np.arange(128)[None, :]
    cst = np.zeros((128, 8, 128), f32)
    cst[:, 0] = (j <= l)
    cst[:, 1] = (j >= l)
    cst[:, 2] = 1.0
    cst[:, 3] = (j == l)
    cst[:, 4] = np.where(l < j, NEG, 0.0)
    cst[:, 5] = np.where(l > j, NEG, 0.0)
    cst[:, 6] = (l - j)
    cst[:, 7, 0] = np.arange(128)
    cst[:, 7, 1] = 127 - np.arange(128)
    return ropeA, ropeR, cst.reshape(128, 1024)


_CACHE = {}


def _in_maps(inputs, cores):
    ropeA, ropeR, cst = _tables()
    f = lambda a: np.ascontiguousarray(np.asarray(a, dtype=np.float32))
    shared = {
        "w_mod": f(inputs["w_mod"]), "b_mod": f(inputs["b_mod"]),
        "norm1_w": f(inputs["norm1_w"]), "norm2_w": f(inputs["norm2_w"]),
        "w_in": f(inputs["w_in"]),
        "attn_q_norm": f(inputs["attn_q_norm"]), "attn_k_norm": f(inputs["attn_k_norm"]),
        "ssd_conv_w": f(inputs["ssd_conv_w"]), "ssd_conv_b": f(inputs["ssd_conv_b"]),
        "ssd_dt_bias": f(inputs["ssd_dt_bias"]).reshape(DEPTH, 16),
        "ssd_a_log": f(inputs["ssd_a_log"]).reshape(DEPTH, 16),
        "ssd_d": f(inputs["ssd_d"]), "ssd_norm_w": f(inputs["ssd_norm_w"]),
        "ret_log_decay": f(inputs["ret_log_decay"]).reshape(DEPTH, 8),
        "ret_gn_w": f(inputs["ret_gn_w"]),
        "w_branch": f(inputs["w_branch"]), "w_out": f(inputs["w_out"]),
        "w_mlp1": f(inputs["w_mlp1"]), "w_mlp2": f(inputs["w_mlp2"]),
        "final_norm_w": f(inputs["final_norm_w"]).reshape(1, D),
        "ropeA": ropeA, "ropeR": ropeR, "cst": cst,
    }
    maps = []
    for b in cores:
        m = dict(shared)
        m["x"] = f(inputs["x"][b])
        m["ctx"] = f(inputs["ctx"][b])
        c2 = np.stack([f(inputs["c"][b]), f(inputs["c_ctx"])], 0)
        m["c2"] = np.ascontiguousarray(c2.reshape(2, 8, 128).transpose(2, 0, 1).reshape(128, 16))
        maps.append(m)
    return maps


def kernel(**inputs):
    if "nc" not in _CACHE:
        _CACHE["nc"] = Builder().build()
    nc = _CACHE["nc"]
    maps = _in_maps(inputs, list(range(BATCH)))
    res = run_bass_kernel_spmd(nc, maps, core_ids=list(range(BATCH)))
    out = np.stack([np.asarray(r["out"], dtype=np.float32) for r in res.results], 0)
    return out
```

```python
import math
import re
import numpy as np
import ml_dtypes
import concourse.bass as bass
import concourse.mybir as mybir
from concourse.bass_utils import run_bass_kernel_spmd

F32 = mybir.dt.float32
BF16 = mybir.dt.bfloat16
AF = mybir.ActivationFunctionType
ALU = mybir.AluOpType
AX = mybir.AxisListType

ENGS = ("pe", "act", "dve", "pool", "sp")

D = 1024
BATCH = 4
NLAT = 8192
NCTX = 256
S = NLAT + NCTX
NT = S // 128
DEPTH = 2
GRID_W = 64
EPS = 1e-6
IN_DIM = 7440
C_AQ, C_AK, C_AV, C_Z, C_XBC, C_DT, C_RQ, C_RK, C_RV, C_RG, C_GATE = (
    0, 512, 640, 768, 1280, 2304, 2320, 2832, 3344, 3856, 4368)
NEG = -30000.0


class Res:
    __slots__ = ("name", "last_w", "readers", "sem", "dma_total", "last_dma")

    def __init__(self, name):
        self.name = name
        self.last_w = None
        self.readers = []
        self.sem = None
        self.dma_total = 0
        self.last_dma = None


class Ins:
    __slots__ = ("eng", "fn", "deps", "signal", "count", "sem", "is_dma")

    def __init__(self, eng, fn, is_dma=False):
        self.eng = eng
        self.fn = fn
        self.deps = []
        self.signal = False
        self.count = None
        self.sem = None
        self.is_dma = is_dma


class Tl:
    __slots__ = ("ap", "res")

    def __init__(self, ap, name):
        self.ap = ap
        self.res = Res(name)


def _res(lst):
    return [x.res if isinstance(x, Tl) else x for x in lst]


class Prog:
    def __init__(self, nc):
        self.nc = nc
        self.streams = {e: [] for e in ENGS}
        self.eng_sem = {}
        self.stack = []
        self.slots = []
        self.phase_map = {}
        self.MAX_DMA_SEMS = 72

    def enter(self, cm):
        v = cm.__enter__()
        self.stack.append(cm)
        return v

    def close(self):
        for cm in reversed(self.stack):
            cm.__exit__(None, None, None)
        self.stack = []

    def sem(self, name):
        return self.enter(self.nc.semaphore(name))

    def _add_dep(self, ins, dep):
        if dep is None or dep is ins:
            return
        if dep.eng == "pe" and ins.eng == "pe" and not dep.is_dma and not ins.is_dma:
            return
        ins.deps.append(dep)
        dep.signal = True

    def _track(self, ins, reads, writes):
        for r in reads:
            self._add_dep(ins, r.last_w)
        for w in writes:
            self._add_dep(ins, w.last_w)
            for rd in w.readers:
                self._add_dep(ins, rd)
        for r in reads:
            r.readers.append(ins)
        for w in writes:
            w.last_w = ins
            w.readers = []

    def op(self, eng, fn, reads=(), writes=()):
        ins = Ins(eng, fn)
        self._track(ins, _res(reads), _res(writes))
        self.streams[eng].append(ins)
        return ins

    def dma(self, eng, out_ap, in_ap, semres, reads=(), writes=(), semkey=None, slow=False):
        if isinstance(semres, Tl):
            semres = semres.res
        if semres.sem is None:
            key = semkey or semres.name
            if semkey is None:
                m = re.match(r"^(.*?)(\d+)$", key)
                if m:
                    key = m.group(1) + str(int(m.group(2)) % 3)
            if key not in self.phase_map:
                idx = len(self.phase_map) % self.MAX_DMA_SEMS
                if idx >= len(self.slots):
                    self.slots.append([self.sem(f"d{idx}"), 0, None])
                self.phase_map[key] = self.slots[idx]
            semres.sem = self.phase_map[key]
        slot = semres.sem
        ins = Ins(eng, None, is_dma=True)
        ins.sem = slot[0]
        if slot[2] is not None:
            ins.deps.append(slot[2])
        slot[1] += 16
        ins.count = slot[1]
        slot[2] = ins
        ins.signal = True

        if slow:
            def fn(e, out_ap=out_ap, in_ap=in_ap):
                return e.dma_start(out=out_ap, in_=in_ap, allow_slow_non_contiguous=True)
        else:
            def fn(e, out_ap=out_ap, in_ap=in_ap):
                return e.dma_start(out=out_ap, in_=in_ap)
        ins.fn = fn
        self._track(ins, _res(reads), _res(writes))
        self.streams[eng].append(ins)
        return ins

    def barrier(self):
        lasts = []
        for e in ENGS:
            for i in reversed(self.streams[e]):
                if not i.is_dma:
                    lasts.append(i)
                    break
        dmas = [slot[2] for slot in self.slots if slot[2] is not None]
        self.phase_map = {}
        for e in ENGS:
            ins = Ins(e, lambda eng: eng.nop())
            for l in lasts:
                if l.eng != e:
                    ins.deps.append(l)
                    l.signal = True
            for d_ in dmas:
                ins.deps.append(d_)
            self.streams[e].append(ins)

    def emit(self):
        nc = self.nc
        for e in ENGS:
            if any(i.signal and not i.is_dma for i in self.streams[e]):
                self.eng_sem[e] = self.sem("e_" + e)
        for e in ENGS:
            c = 0
            for i in self.streams[e]:
                if i.is_dma:
                    continue
                if i.signal:
                    c += 1
                    i.count = c
                    i.sem = self.eng_sem[e]
        engmap = {"pe": "tensor", "act": "scalar", "dve": "vector", "pool": "gpsimd", "sp": "sync"}
        streams = self.streams
        self.n_instr = sum(len(v) for v in streams.values())

        def body(ename):
            def run(eng):
                known = {}
                for i in streams[ename]:
                    need = {}
                    for d_ in i.deps:
                        k = id(d_.sem)
                        if known.get(k, 0) >= d_.count:
                            continue
                        if k not in need or need[k][1] < d_.count:
                            need[k] = (d_.sem, d_.count)
                    needl = list(need.items())
                    for k, (s_, v) in needl[:-1]:
                        eng.wait_ge(s_, v)
                        known[k] = v
                    bi = i.fn(eng)
                    if needl:
                        k, (s_, v) = needl[-1]
                        bi._wait_ge(s_, v)
                        known[k] = v
                    if i.signal:
                        bi.then_inc(i.sem, 16 if i.is_dma else 1)
            return run

        with nc.Block() as block:
            for e in ENGS:
                if streams[e]:
                    getattr(block, engmap[e])(body(e))


class Deferred:
    def __init__(self):
        self.q = {}
        self.j = 0

    def at(self, k, fn):
        self.q.setdefault(self.j + k, []).append(fn)

    def step(self):
        for fn in self.q.pop(self.j, []):
            fn()
        self.j += 1

    def flush(self):
        while self.q:
            self.step()


class Arena:
    def __init__(self, ap, nelem_bf16):
        self.ap = ap
        self.cap = nelem_bf16
        self.off = 0
        self.peak = 0

    def reset(self):
        self.off = 0

    def alloc(self, name, n, dtype):
        nb = n * 2 if dtype == F32 else n
        nb = (nb + 1) // 2 * 2
        assert self.off + nb <= self.cap, f"arena overflow at {name}: {self.off}+{nb}>{self.cap}"
        ap = self.ap[:, self.off:self.off + nb]
        if dtype == F32:
            ap = ap.bitcast(F32)
        self.off += nb
        self.peak = max(self.peak, self.off)
        return Tl(ap, name)


class Builder:
    def __init__(self, debug=False, phases=None, layers=(0, 1)):
        self.debug = debug
        self.phases = phases
        self.layers = layers
        nc = bass.Bass("TRN2", target_bir_lowering=False)
        self.nc = nc
        self.P = Prog(nc)
        P = self.P
        ein = lambda n, sh, dt=F32: nc.dram_tensor(n, list(sh), dt, kind="ExternalInput").ap()
        self.x = ein("x", [NLAT, D])
        self.ctx = ein("ctx", [NCTX, D])
        self.c2 = ein("c2", [128, 16])
        self.w_mod = ein("w_mod", [DEPTH, D, 6 * D])
        self.b_mod = ein("b_mod", [DEPTH, 6 * D])
        self.norm1_w = ein("norm1_w", [DEPTH, D])
        self.norm2_w = ein("norm2_w", [DEPTH, D])
        self.w_in = ein("w_in", [DEPTH, D, IN_DIM])
        self.attn_q_norm = ein("attn_q_norm", [DEPTH, 64])
        self.attn_k_norm = ein("attn_k_norm", [DEPTH, 64])
        self.ssd_conv_w = ein("ssd_conv_w", [DEPTH, 3, 1024])
        self.ssd_conv_b = ein("ssd_conv_b", [DEPTH, 1024])
        self.ssd_dt_bias = ein("ssd_dt_bias", [DEPTH, 16])
        self.ssd_a_log = ein("ssd_a_log", [DEPTH, 16])
        self.ssd_d = ein("ssd_d", [DEPTH, 8])
        self.ssd_norm_w = ein("ssd_norm_w", [DEPTH, 512])
        self.ret_log_decay = ein("ret_log_decay", [DEPTH, 8])
        self.ret_gn_w = ein("ret_gn_w", [DEPTH, 512])
        self.w_branch = ein("w_branch", [DEPTH, 3, 512, D])
        self.w_out = ein("w_out", [DEPTH, D, D])
        self.w_mlp1 = ein("w_mlp1", [DEPTH, D, 4 * D])
        self.w_mlp2 = ein("w_mlp2", [DEPTH, 4 * D, D])
        self.final_norm_w = ein("final_norm_w", [1, D])
        self.ropeA = ein("ropeA", [S, 64])
        self.ropeR = ein("ropeR", [S, 128])
        self.cst = ein("cst", [128, 8 * 128])
        self.out = nc.dram_tensor("out", [NLAT, D], F32, kind="ExternalOutput").ap()

        kind = "ExternalOutput" if debug else "Internal"
        def scr(n, sh, dt):
            t = nc.dram_tensor(n, list(sh), dt, kind=kind).ap()
            return Tl(t, n)
        self.mod_d = scr("mod_d", [2, 6 * D], F32)
        self.h_d = scr("h_d", [S, D], F32)
        self.h1_d = scr("h1_d", [S, D], F32)
        self.qT_d = scr("qT_d", [4, 128, S], BF16)
        self.kT_d = scr("kT_d", [128, S], BF16)
        self.v_d = scr("v_d", [S, 128], BF16)
        self.z_d = scr("z_d", [S, 512], BF16)
        self.xbcT_d = scr("xbcT_d", [1024, S], BF16)
        self.dt_d = scr("dt_d", [S, 16], F32)
        self.rqT_d = scr("rqT_d", [4, 128, S], BF16)
        self.rkT_d = scr("rkT_d", [4, 128, S], BF16)
        self.rk_d = scr("rk_d", [S, 512], BF16)
        self.rv_d = scr("rv_d", [S, 512], BF16)
        self.rg_d = scr("rg_d", [S, 512], BF16)
        self.gate_d = scr("gate_d", [S, 3072], BF16)
        self.attnT_d = scr("attnT_d", [512, S], BF16)
        self.ssdT_d = scr("ssdT_d", [512, S], BF16)
        self.retT_d = scr("retT_d", [512, S], BF16)
        self.yf_d = scr("yf_d", [S, 512], F32)
        self.uT_d = scr("uT_d", [D, S], BF16)
        self.xcT_d = scr("xcT_d", [D, S], BF16)
        self.xb_d = scr("xb_d", [S, 768], BF16)
        self.in_res = Res("inputs")

        ar = P.enter(nc.sbuf_tensor("arena", [128, 98304], BF16))
        self.A = Arena(ar, 98304)
        psa = P.enter(nc.psum_tensor("psa", [128, 4096], F32))
        self.psa = psa
        self.ps = [Tl(psa[:, 512 * i:512 * (i + 1)], f"ps{i}") for i in range(8)]

    def new_phase(self):
        self.P.barrier()
        self.A.reset()
        self.ps = [Tl(self.psa[:, 512 * i:512 * (i + 1)], f"ps{i}") for i in range(8)]

    def mm(self, out, lhsT, rhs, start, stop, reads, writes):
        self.P.op("pe", lambda e: e.matmul(out, lhsT, rhs, start=start, stop=stop), reads, writes)

    def tr(self, out, in_, ident, reads, writes):
        self.P.op("pe", lambda e: e.transpose(out, in_, ident), reads, writes)

    def act(self, out, in_, func, reads, writes, bias=None, scale=None, accum=None):
        kw = {}
        if bias is not None:
            kw["bias"] = bias
        if scale is not None:
            kw["scale"] = scale
        if accum is not None:
            kw["accum_out"] = accum
        self.P.op("act", lambda e: e.activation(out, in_, func, **kw), reads, writes)

    def tt(self, eng, out, in0, in1, op, reads, writes):
        self.P.op(eng, lambda e: e.tensor_tensor(out, in0, in1, op), reads, writes)

    def ts(self, eng, out, in0, s1, s2, op0, op1, reads, writes):
        if op1 is None:
            self.P.op(eng, lambda e: e.tensor_scalar(out, in0, s1, None, op0), reads, writes)
        else:
            self.P.op(eng, lambda e: e.tensor_scalar(out, in0, s1, s2, op0, op1), reads, writes)

    def stt(self, eng, out, in0, scalar, in1, op0, op1, reads, writes):
        self.P.op(eng, lambda e: e.scalar_tensor_tensor(out, in0, scalar, in1, op0, op1), reads, writes)

    def cp(self, eng, out, in_, reads, writes):
        if eng == "act":
            self.P.op("act", lambda e: e.copy(out, in_), reads, writes)
        else:
            self.P.op(eng, lambda e: e.tensor_copy(out, in_), reads, writes)

    def rstd(self, out, ss, n, reads_writes_tile, tmp):
        t = reads_writes_tile
        self.ts("dve", tmp, ss, 1.0 / n, EPS, ALU.mult, ALU.add, [t], [t])
        self.act(tmp, tmp, AF.Sqrt, [t], [t])
        self.P.op("dve", lambda e: e.reciprocal(out, tmp), [t.res], [t.res])

    def hsrc(self, layer, t):
        if layer == 0:
            if t < 2:
                return self.ctx[t * 128:(t + 1) * 128, :], self.in_res
            return self.x[(t - 2) * 128:(t - 1) * 128, :], self.in_res
        return self.h_d.ap[t * 128:(t + 1) * 128, :], self.h_d.res

    def consts(self):
        A, P = self.A, self.P
        c = A.alloc("cst", 8 * 128, F32)
        P.dma("sp", c.ap, self.cst[:, :], c, reads=[self.in_res], writes=[c])
        self.C = c
        cv = c.ap.rearrange("p (a b) -> p a b", a=8)
        self.triU = cv[:, 0, :]
        self.triL = cv[:, 1, :]
        self.onesf = cv[:, 2, :]
        self.identf = cv[:, 3, :]
        self.negU = cv[:, 4, :]
        self.negL = cv[:, 5, :]
        self.posd = cv[:, 6, :]
        self.misc = cv[:, 7, :]
        cb = A.alloc("cstb", 3 * 128, BF16)
        cbv = cb.ap.rearrange("p (a b) -> p a b", a=3)
        self.Cb = cb
        self.ident = cbv[:, 0, :]
        self.negUb = cbv[:, 1, :]
        self.negLb = cbv[:, 2, :]
        self.cp("dve", self.ident, self.identf, [c], [cb])
        self.cp("dve", self.negUb, self.negU, [c], [cb])
        self.cp("dve", self.negLb, self.negL, [c], [cb])

    def bcast_row(self, dst_tile, dst_ap, src_ap, src_res, n_part=128):
        self.P.dma("sp", dst_ap, src_ap.partition_broadcast(n_part), dst_tile, reads=[src_res], writes=[dst_tile])

    def phase_mod(self, l):
        A, P = self.A, self.P
        self.new_phase()
        self.consts()
        cc = A.alloc("cc", 16, F32)
        ccv = cc.ap.rearrange("p (r k) -> p k r", r=2)
        P.dma("sp", cc.ap, self.c2[:, :], cc, reads=[self.in_res], writes=[cc])
        self.act(cc.ap, cc.ap, AF.Silu, [cc], [cc])
        bm = A.alloc("bm", 6 * D, F32)
        for r in range(2):
            P.dma("sp", bm.ap[r:r + 1, :], self.b_mod[l:l + 1, :], bm, reads=[self.in_res], writes=[bm])
        modsb = A.alloc("modsb", 6 * D, F32)
        wst = [A.alloc(f"wst{i}", 8 * 512, F32) for i in range(2)]
        wm = self.w_mod[l].rearrange("(k p) n -> p k n", p=128)
        for j in range(12):
            st = wst[j % 2]
            stv = st.ap.rearrange("p (k n) -> p k n", k=8)
            P.dma("sp" if j % 2 == 0 else "pool", stv, wm[:, :, j * 512:(j + 1) * 512], st, reads=[self.in_res], writes=[st])
            ps = self.ps[j % 2]
            for k in range(8):
                self.mm(ps.ap[0:2, :], ccv[:, k, :], stv[:, k, :], k == 0, k == 7, [cc, st], [ps])
            self.tt("dve", modsb.ap[0:2, j * 512:(j + 1) * 512], ps.ap[0:2, :], bm.ap[0:2, j * 512:(j + 1) * 512], ALU.add, [ps, bm], [modsb])
        P.dma("sp", self.mod_d.ap[:, :], modsb.ap[0:2, :], modsb, reads=[modsb], writes=[self.mod_d])

    def mod_tile(self, name, r, idx):
        t = self.A.alloc(name, D, F32)
        self.bcast_row(t, t.ap, self.mod_d.ap[r:r + 1, idx * D:(idx + 1) * D], self.mod_d.res)
        return t

    def gain_shift_tiles(self, l, norm_w_ap, idx_shift, idx_scale, rows):
        nw = self.A.alloc("nwb", D, F32)
        self.bcast_row(nw, nw.ap, norm_w_ap, self.in_res)
        out = {}
        for r in rows:
            g = self.mod_tile(f"G{r}", r, idx_scale)
            self.stt("dve", g.ap, g.ap, 1.0, nw.ap, ALU.add, ALU.mult, [g, nw], [g])
            sh = self.mod_tile(f"Sh{r}", r, idx_shift)
            out[r] = (g, sh)
        return out

    def load_weight_bf16(self, dst_tile, dst_view, src_view, K, N, chunk=None, stg_tiles=None, engs=None):
        P = self.P
        lst = []
        for k in range(K):
            lst.append(P.dma("pool", dst_view[:, k, :], src_view[:, k, :], Res("wld"), semkey=f"wld{k % 6}"))
        join = P.op("pool", lambda e: e.nop(), [], [dst_tile])
        join.deps.extend(lst)

    def mod_cols(self, l, norm_w, idx_shift, idx_scale, rows):
        A, P = self.A, self.P
        mc = A.alloc("mcols", 48, F32)
        mv = mc.ap[:, 0:32].rearrange("p (r a k) -> p r a k", r=2, a=2)
        nwc = mc.ap[:, 32:40]
        P.dma("sp", nwc, norm_w[l, :].rearrange("(k p) -> p k", p=128), mc, reads=[self.in_res], writes=[mc], slow=True)
        for r in rows:
            P.dma("sp", mv[:, r, 0, :], self.mod_d.ap[r, idx_scale * D:(idx_scale + 1) * D].rearrange("(k p) -> p k", p=128), mc, reads=[self.mod_d], writes=[mc], slow=True)
            P.dma("sp", mv[:, r, 1, :], self.mod_d.ap[r, idx_shift * D:(idx_shift + 1) * D].rearrange("(k p) -> p k", p=128), mc, reads=[self.mod_d], writes=[mc], slow=True)
            self.stt("dve", mv[:, r, 0, :], mv[:, r, 0, :], 1.0, nwc, ALU.add, ALU.mult, [mc], [mc])
        return mc, mv

    def norm_mod_T(self, h_, s_, hn, pst, mc, mv, r, dstv, dst_tile, col0, dq=None):
        self.act(hn.ap, h_.ap, AF.Square, [h_], [hn, s_], accum=s_.ap[:, 0:1])
        self.rstd(s_.ap[:, 1:2], s_.ap[:, 0:1], D, s_, s_.ap[:, 2:3])
        self.ts("dve", hn.ap, h_.ap, s_.ap[:, 1:2], None, ALU.mult, None, [h_, s_], [hn])

        def part2():
            for k in range(8):
                self.tr(pst.ap[:, k * 128:(k + 1) * 128], hn.ap[:, k * 128:(k + 1) * 128], self.identf, [hn, self.C], [pst])
            for k in range(8):
                self.act(dstv[:, k, col0:col0 + 128], pst.ap[:, k * 128:(k + 1) * 128], AF.Identity, [pst, mc], [dst_tile],
                         scale=mv[:, r, 0, k:k + 1], bias=mv[:, r, 1, k:k + 1])
        if dq is None:
            part2()
        else:
            dq.at(1, part2)

    def phase_a0(self, l):
        A, P = self.A, self.P
        self.new_phase()
        self.consts()
        mc, mv = self.mod_cols(l, self.norm1_w, 0, 1, (0, 1))
        NHB = 4
        ht = [A.alloc(f"ht{i}", D, F32) for i in range(NHB)]
        hn = [A.alloc(f"hn{i}", D, F32) for i in range(3)]
        sm = [A.alloc(f"sm{i}", 8, F32) for i in range(NHB)]
        uTs = [A.alloc(f"uTs{i}", 8 * 512, BF16) for i in range(3)]
        pst = [Tl(self.psa[:, 0:1024], "pst0"), Tl(self.psa[:, 1024:2048], "pst1"), Tl(self.psa[:, 2048:3072], "pst2")]
        supers = [(0, 2)] + [(2 + 4 * j, 4) for j in range(16)]
        tiles = [(si, t0, nt, tc) for si, (t0, nt) in enumerate(supers) for tc in range(nt)]
        dst = self.uT_d.ap.rearrange("(k p) s -> p k s", p=128)
        dq = Deferred()

        def loadh(gi):
            if gi < len(tiles):
                si, t0, nt, tc = tiles[gi]
                src, sres = self.hsrc(l, t0 + tc)
                P.dma("sp", ht[gi % NHB].ap, src, ht[gi % NHB], reads=[sres], writes=[ht[gi % NHB]])
        loadh(0)
        loadh(1)
        for gi, (si, t0, nt, tc) in enumerate(tiles):
            loadh(gi + 2)
            W = nt * 128
            t = t0 + tc
            u = uTs[si % 3]
            uv = u.ap.rearrange("p (k w) -> p k w", k=8)
            self.norm_mod_T(ht[gi % NHB], sm[gi % NHB], hn[gi % 3], pst[gi % 3], mc, mv, 1 if t < 2 else 0, uv, u, tc * 128, dq=dq)
            if tc == nt - 1:
                dq.at(2, lambda u=u, uv=uv, t0=t0, W=W: P.dma("sp", dst[:, :, t0 * 128:t0 * 128 + W], uv[:, :, 0:W], u, reads=[u], writes=[self.uT_d]))
            dq.step()
        dq.flush()

    def a_common(self, l, c0, ncols, name):
        A = self.A
        w = A.alloc(name, 8 * ncols, BF16)
        wv = w.ap.rearrange("p (k n) -> p k n", k=8)
        self.load_weight_bf16(w, wv, self.w_in[l].rearrange("(k p) n -> p k n", p=128)[:, :, c0:c0 + ncols], 8, ncols)
        return w, wv

    def load_uT(self, uTt, t0, W, eng="sp"):
        uv = uTt.ap.rearrange("p (k w) -> p k w", k=8)
        self.P.dma(eng, uv[:, :, 0:W], self.uT_d.ap.rearrange("(k p) s -> p k s", p=128)[:, :, t0 * 128:t0 * 128 + W], uTt, reads=[self.uT_d], writes=[uTt])
        return uv

    SUPERS = [(0, 2)] + [(2 + 4 * j, 4) for j in range(16)]

    def phase_a1(self, l):
        A, P = self.A, self.P
        self.new_phase()
        self.consts()
        ident = self.ident
        wsb, wv = self.a_common(l, C_AQ, 768, "w_a1")
        qkw = A.alloc("qkw", 128, F32)
        self.bcast_row(qkw, qkw.ap[:, 0:64], self.attn_q_norm[l:l + 1, :], self.in_res)
        self.bcast_row(qkw, qkw.ap[:, 64:128], self.attn_k_norm[l:l + 1, :], self.in_res)
        NB = 3
        uT = [A.alloc(f"uT{i}", 8 * 512, BF16) for i in range(2)]
        rA = [A.alloc(f"rA{i}", 64, F32) for i in range(NB)]
        sm = [A.alloc(f"sm{i}", 64, F32) for i in range(NB)]
        sqt = [A.alloc(f"sqt{i}", 640, F32) for i in range(NB)]
        qn = [A.alloc(f"qn{i}", 640, F32) for i in range(NB)]
        rt = [A.alloc(f"rt{i}", 4 * 320, F32) for i in range(NB)]
        qr = [A.alloc(f"qr{i}", 640, BF16) for i in range(NB)]
        vt = [A.alloc(f"vt{i}", 128, BF16) for i in range(NB)]
        qTs = [A.alloc(f"qTs{i}", 4 * 512, BF16) for i in range(2)]
        kTs = [A.alloc(f"kTs{i}", 512, BF16) for i in range(2)]
        ps = self.ps
        ps_t = [ps[0], ps[1]]
        mmb = ps[2:8]
        mi = 0
        gi = 0
        dq = Deferred()
        uvs = {}

        def pre(si_):
            if si_ < len(self.SUPERS):
                uvs[si_] = self.load_uT(uT[si_ % 2], self.SUPERS[si_][0], self.SUPERS[si_][1] * 128, "sp")
        pre(0)
        for si, (t0, nt) in enumerate(self.SUPERS):
            W = nt * 128
            pre(si + 1)
            uTt = uT[si % 2]
            uv = uvs[si]
            qT, kT = qTs[si % 2], kTs[si % 2]
            qTv = qT.ap.rearrange("p (a w) -> p a w", a=4)
            for tc in range(nt):
                t = t0 + tc
                b = gi % NB
                gi += 1
                tok = slice(t * 128, (t + 1) * 128)
                P.dma("sp", rA[b].ap, self.ropeA[tok, :], rA[b], reads=[self.in_res], writes=[rA[b]])
                pq = mmb[mi % 6]; mi += 1
                pk = mmb[mi % 6]; mi += 1
                for k in range(8):
                    self.mm(pq.ap, uv[:, k, tc * 128:(tc + 1) * 128], wv[:, k, 0:512], k == 0, k == 7, [uTt, wsb], [pq])
                for k in range(8):
                    self.mm(pk.ap[:, 0:256], uv[:, k, tc * 128:(tc + 1) * 128], wv[:, k, 512:768], k == 0, k == 7, [uTt, wsb], [pk])
                self.act(sqt[b].ap[:, 0:512], pq.ap, AF.Square, [pq], [sqt[b]])
                self.act(sqt[b].ap[:, 512:640], pk.ap[:, 0:128], AF.Square, [pk], [sqt[b]])
                self.cp("act", vt[b].ap, pk.ap[:, 128:256], [pk], [vt[b]])
                P.op("dve", lambda e, o=sm[b].ap[:, 8:18], i=sqt[b].ap.rearrange("p (h d) -> p h d", d=64): e.tensor_reduce(o, i, AX.X, ALU.add), [sqt[b].res], [sm[b].res])
                self.rstd(sm[b].ap[:, 20:30], sm[b].ap[:, 8:18], 64, sm[b], sm[b].ap[:, 32:42])
                qn3 = qn[b].ap.rearrange("p (h d) -> p h d", d=64)
                self.tt("dve", qn3[:, 0:8, :], pq.ap.rearrange("p (h d) -> p h d", d=64), sm[b].ap[:, 20:28].unsqueeze(2).to_broadcast([128, 8, 64]), ALU.mult, [pq, sm[b]], [qn[b]])
                self.tt("dve", qn3[:, 8:10, :], pk.ap[:, 0:128].rearrange("p (h d) -> p h d", d=64), sm[b].ap[:, 28:30].unsqueeze(2).to_broadcast([128, 2, 64]), ALU.mult, [pk, sm[b]], [qn[b]])
                self.tt("dve", qn3[:, 0:8, :], qn3[:, 0:8, :], qkw.ap[:, 0:64].unsqueeze(1).to_broadcast([128, 8, 64]), ALU.mult, [qn[b], qkw], [qn[b]])
                self.tt("dve", qn3[:, 8:10, :], qn3[:, 8:10, :], qkw.ap[:, 64:128].unsqueeze(1).to_broadcast([128, 2, 64]), ALU.mult, [qn[b], qkw], [qn[b]])
                cosb = rA[b].ap[:, 0:32].unsqueeze(1).to_broadcast([128, 10, 32])
                sinb = rA[b].ap[:, 32:64].unsqueeze(1).to_broadcast([128, 10, 32])
                x1, x2 = qn3[:, :, 0:32], qn3[:, :, 32:64]
                rtv = rt[b].ap.rearrange("p (a h d) -> p a h d", a=4, h=10)
                qr3 = qr[b].ap.rearrange("p (h d) -> p h d", d=64)
                self.tt("dve", rtv[:, 0], x1, cosb, ALU.mult, [qn[b], rA[b]], [rt[b]])
                self.tt("dve", rtv[:, 1], x2, sinb, ALU.mult, [qn[b], rA[b]], [rt[b]])
                self.tt("dve", rtv[:, 2], x1, sinb, ALU.mult, [qn[b], rA[b]], [rt[b]])
                self.tt("dve", rtv[:, 3], x2, cosb, ALU.mult, [qn[b], rA[b]], [rt[b]])
                self.tt("dve", qr3[:, :, 0:32], rtv[:, 0], rtv[:, 1], ALU.subtract, [rt[b]], [qr[b]])
                self.tt("dve", qr3[:, :, 32:64], rtv[:, 2], rtv[:, 3], ALU.add, [rt[b]], [qr[b]])
                def fin(b=b, tc=tc, tok=tok, qT=qT, kT=kT, qTv=qTv, pt_=ps_t[gi % 2]):
                    p1 = pt_.ap.bitcast(BF16)
                    for a in range(5):
                        self.tr(p1[:, a * 128:(a + 1) * 128], qr[b].ap[:, a * 128:(a + 1) * 128], ident, [qr[b], self.Cb], [pt_])
                    self.cp("act", qTv[:, :, tc * 128:(tc + 1) * 128], p1[:, 0:512].rearrange("p (a w) -> p a w", a=4), [pt_], [qT])
                    self.cp("act", kT.ap[:, tc * 128:(tc + 1) * 128], p1[:, 512:640], [pt_], [kT])
                    P.dma("sp", self.v_d.ap[tok, :], vt[b].ap, vt[b], reads=[vt[b]], writes=[self.v_d])
                dq.at(1, fin)
                if tc == nt - 1:
                    def st(t0=t0, W=W, qT=qT, kT=kT, qTv=qTv):
                        tsl = slice(t0 * 128, t0 * 128 + W)
                        P.dma("sp", self.qT_d.ap.rearrange("a p s -> p a s")[:, :, tsl], qTv[:, :, 0:W], qT, reads=[qT], writes=[self.qT_d])
                        P.dma("sp", self.kT_d.ap[:, tsl], kT.ap[:, 0:W], kT, reads=[kT], writes=[self.kT_d])
                    dq.at(2, st)
                dq.step()
        dq.flush()

    def phase_a2(self, l):
        A, P = self.A, self.P
        self.new_phase()
        wsb, wv = self.a_common(l, C_Z, 1552, "w_a2")
        NB = 2
        uT = [A.alloc(f"uT{i}", 8 * 512, BF16) for i in range(2)]
        zt = [A.alloc(f"zt{i}", 512, BF16) for i in range(NB)]
        dtt = [A.alloc(f"dtt{i}", 16, F32) for i in range(NB)]
        xst = [A.alloc(f"xst{i}", 8 * 512, BF16) for i in range(2)]
        mmb = self.ps
        mi = 0
        gi = 0
        uvs = {}

        def pre(si_):
            if si_ < len(self.SUPERS):
                uvs[si_] = self.load_uT(uT[si_ % 2], self.SUPERS[si_][0], self.SUPERS[si_][1] * 128, "sp")
        pre(0)
        for si, (t0, nt) in enumerate(self.SUPERS):
            W = nt * 128
            pre(si + 1)
            uTt = uT[si % 2]
            uv = uvs[si]
            xs = xst[si % 2]
            xsv = xs.ap.rearrange("p (c w) -> p c w", c=8)
            for c8 in range(8):
                pb = mmb[mi % 8]; mi += 1
                for k in range(8):
                    self.mm(pb.ap[:, 0:W], wv[:, k, 512 + c8 * 128:512 + (c8 + 1) * 128], uv[:, k, 0:W], k == 0, k == 7, [wsb, uTt], [pb])
                self.cp("dve", xsv[:, c8, 0:W], pb.ap[:, 0:W], [pb], [xs])
            P.dma("pool", self.xbcT_d.ap.rearrange("(c p) s -> p c s", p=128)[:, :, t0 * 128:t0 * 128 + W], xsv[:, :, 0:W], xs, reads=[xs], writes=[self.xbcT_d])
            for tc in range(nt):
                t = t0 + tc
                b = gi % NB
                gi += 1
                tok = slice(t * 128, (t + 1) * 128)
                pz = mmb[mi % 8]; mi += 1
                for k in range(8):
                    self.mm(pz.ap, uv[:, k, tc * 128:(tc + 1) * 128], wv[:, k, 0:512], k == 0, k == 7, [uTt, wsb], [pz])
                self.act(zt[b].ap, pz.ap, AF.Silu, [pz], [zt[b]])
                pd = mmb[mi % 8]; mi += 1
                for k in range(8):
                    self.mm(pd.ap[:, 0:16], uv[:, k, tc * 128:(tc + 1) * 128], wv[:, k, 1536:1552], k == 0, k == 7, [uTt, wsb], [pd])
                self.cp("dve", dtt[b].ap, pd.ap[:, 0:16], [pd], [dtt[b]])
                P.dma("sp", self.z_d.ap[tok, :], zt[b].ap, zt[b], reads=[zt[b]], writes=[self.z_d])
                P.dma("sp", self.dt_d.ap[tok, :], dtt[b].ap, dtt[b], reads=[dtt[b]], writes=[self.dt_d])

    def phase_a3(self, l):
        A, P = self.A, self.P
        self.new_phase()
        self.consts()
        ident = self.ident
        wsb, wv = self.a_common(l, C_RQ, 2048, "w_a3")
        NB = 3
        dq = Deferred()
        uT = [A.alloc(f"uT{i}", 8 * 512, BF16) for i in range(2)]
        rR = [A.alloc(f"rR{i}", 128, F32) for i in range(NB)]
        rt2 = [A.alloc(f"rtb{i}", 4 * 512, F32) for i in range(NB)]
        rqk = [A.alloc(f"rqk{i}", 1024, BF16) for i in range(NB)]
        rvg = [A.alloc(f"rvg{i}", 1024, BF16) for i in range(NB)]
        rqTs = [A.alloc(f"rqTs{i}", 4 * 512, BF16) for i in range(2)]
        rkTs = [A.alloc(f"rkTs{i}", 4 * 512, BF16) for i in range(2)]
        ps = self.ps
        ps_t = [ps[0], ps[1]]
        mmb = ps[2:8]
        mi = 0
        gi = 0
        uvs = {}

        def pre(si_):
            if si_ < len(self.SUPERS):
                uvs[si_] = self.load_uT(uT[si_ % 2], self.SUPERS[si_][0], self.SUPERS[si_][1] * 128, "sp")
        pre(0)
        for si, (t0, nt) in enumerate(self.SUPERS):
            W = nt * 128
            pre(si + 1)
            uTt = uT[si % 2]
            uv = uvs[si]
            rqT, rkT = rqTs[si % 2], rkTs[si % 2]
            rqTv = rqT.ap.rearrange("p (a w) -> p a w", a=4)
            rkTv = rkT.ap.rearrange("p (a w) -> p a w", a=4)
            for tc in range(nt):
                t = t0 + tc
                b = gi % NB
                gi += 1
                tok = slice(t * 128, (t + 1) * 128)
                P.dma("sp", rR[b].ap, self.ropeR[tok, :], rR[b], reads=[self.in_res], writes=[rR[b]])
                banks = []
                for j in range(4):
                    pb = mmb[mi % 6]; mi += 1
                    banks.append(pb)
                    for k in range(8):
                        self.mm(pb.ap, uv[:, k, tc * 128:(tc + 1) * 128], wv[:, k, j * 512:(j + 1) * 512], k == 0, k == 7, [uTt, wsb], [pb])
                cosr = rR[b].ap[:, 0:64].unsqueeze(1).to_broadcast([128, 4, 64])
                sinr = rR[b].ap[:, 64:128].unsqueeze(1).to_broadcast([128, 4, 64])
                r2 = rt2[b].ap.rearrange("p (a h d) -> p a h d", a=4, h=8)
                rq3 = rqk[b].ap.rearrange("p (h d) -> p h d", d=128)
                for j in range(2):
                    x3 = banks[j].ap.rearrange("p (h d) -> p h d", d=128)
                    y1, y2 = x3[:, :, 0:64], x3[:, :, 64:128]
                    hs = slice(j * 4, (j + 1) * 4)
                    self.tt("dve", r2[:, 0, hs], y1, cosr, ALU.mult, [banks[j], rR[b]], [rt2[b]])
                    self.tt("dve", r2[:, 1, hs], y2, sinr, ALU.mult, [banks[j], rR[b]], [rt2[b]])
                    self.tt("dve", r2[:, 2, hs], y1, sinr, ALU.mult, [banks[j], rR[b]], [rt2[b]])
                    self.tt("dve", r2[:, 3, hs], y2, cosr, ALU.mult, [banks[j], rR[b]], [rt2[b]])
                self.tt("pool", rq3[:, :, 0:64], r2[:, 0], r2[:, 1], ALU.subtract, [rt2[b]], [rqk[b]])
                self.tt("pool", rq3[:, :, 64:128], r2[:, 2], r2[:, 3], ALU.add, [rt2[b]], [rqk[b]])
                self.cp("act", rvg[b].ap[:, 0:512], banks[2].ap, [banks[2]], [rvg[b]])
                self.act(rvg[b].ap[:, 512:1024], banks[3].ap, AF.Silu, [banks[3]], [rvg[b]])
                def fin(b=b, tc=tc, tok=tok, rqT=rqT, rkT=rkT, rqTv=rqTv, rkTv=rkTv, pt_=ps_t[gi % 2]):
                    p2 = pt_.ap.bitcast(BF16)
                    for a in range(8):
                        self.tr(p2[:, a * 128:(a + 1) * 128], rqk[b].ap[:, a * 128:(a + 1) * 128], ident, [rqk[b], self.Cb], [pt_])
                    self.cp("act", rqTv[:, :, tc * 128:(tc + 1) * 128], p2[:, 0:512].rearrange("p (a w) -> p a w", a=4), [pt_], [rqT])
                    self.cp("act", rkTv[:, :, tc * 128:(tc + 1) * 128], p2[:, 512:1024].rearrange("p (a w) -> p a w", a=4), [pt_], [rkT])
                    P.dma("sp", self.rk_d.ap[tok, :], rqk[b].ap[:, 512:1024], rqk[b], reads=[rqk[b]], writes=[self.rk_d])
                    P.dma("sp", self.rv_d.ap[tok, :], rvg[b].ap[:, 0:512], rvg[b], reads=[rvg[b]], writes=[self.rv_d])
                    P.dma("sp", self.rg_d.ap[tok, :], rvg[b].ap[:, 512:1024], rvg[b], reads=[rvg[b]], writes=[self.rg_d])
                dq.at(1, fin)
                if tc == nt - 1:
                    def st(t0=t0, W=W, rqT=rqT, rkT=rkT, rqTv=rqTv, rkTv=rkTv):
                        tsl = slice(t0 * 128, t0 * 128 + W)
                        P.dma("sp", self.rqT_d.ap.rearrange("a p s -> p a s")[:, :, tsl], rqTv[:, :, 0:W], rqT, reads=[rqT], writes=[self.rqT_d])
                        P.dma("sp", self.rkT_d.ap.rearrange("a p s -> p a s")[:, :, tsl], rkTv[:, :, 0:W], rkT, reads=[rkT], writes=[self.rkT_d])
                    dq.at(2, st)
                dq.step()
        dq.flush()

    def phase_a4(self, l):
        A, P = self.A, self.P
        self.new_phase()
        wsb, wv = self.a_common(l, C_GATE, 3072, "w_a4")
        NB = 2
        uT = [A.alloc(f"uT{i}", 8 * 512, BF16) for i in range(2)]
        gat = [A.alloc(f"gat{i}", 3072, BF16) for i in range(NB)]
        mmb = self.ps
        mi = 0
        gi = 0
        uvs = {}

        def pre(si_):
            if si_ < len(self.SUPERS):
                uvs[si_] = self.load_uT(uT[si_ % 2], self.SUPERS[si_][0], self.SUPERS[si_][1] * 128, "sp")
        pre(0)
        for si, (t0, nt) in enumerate(self.SUPERS):
            W = nt * 128
            pre(si + 1)
            uTt = uT[si % 2]
            uv = uvs[si]
            for tc in range(nt):
                t = t0 + tc
                b = gi % NB
                gi += 1
                tok = slice(t * 128, (t + 1) * 128)
                for gch in range(6):
                    pg = mmb[mi % 8]; mi += 1
                    for k in range(8):
                        self.mm(pg.ap, uv[:, k, tc * 128:(tc + 1) * 128], wv[:, k, gch * 512:(gch + 1) * 512], k == 0, k == 7, [uTt, wsb], [pg])
                    self.act(gat[b].ap[:, gch * 512:(gch + 1) * 512], pg.ap, AF.Sigmoid, [pg], [gat[b]])
                P.dma("pool", self.gate_d.ap[tok, :], gat[b].ap, gat[b], reads=[gat[b]], writes=[self.gate_d])

    def phase_b(self, l, need_ctx):
        A, P = self.A, self.P
        self.new_phase()
        kTz = A.alloc("kTz", 4 * S, BF16)
        kv = kTz.ap.rearrange("p (a s) -> p a s", a=4)
        P.op("pool", lambda e: e.memset(kTz.ap, 0.0), [], [kTz])
        for g in range(2):
            src = self.kT_d.ap[g * 64:(g + 1) * 64, :]
            P.dma("sp", kv[0:64, g * 2 + 0, :], src, kTz, reads=[self.kT_d], writes=[kTz], semkey=f"kTz{g}a")
            P.dma("pool", kv[64:128, g * 2 + 1, :], src, kTz, reads=[self.kT_d], writes=[kTz], semkey=f"kTz{g}b")
        V1 = A.alloc("V1", NT * 2 * 128, BF16)
        V1v = V1.ap.rearrange("p (t g c) -> p t g c", t=NT, g=2)
        P.op("pool", lambda e: e.memset(V1.ap, 1.0), [], [V1])
        vsrc = self.v_d.ap.rearrange("(t p) (g c) -> p t g c", p=128, g=2)
        for j in range(0, NT, 6):
            for g in range(2):
                P.dma("sp" if g == 0 else "pool", V1v[:, j:j + 6, g, 0:64], vsrc[:, j:j + 6, g, :], V1, reads=[self.v_d], writes=[V1], semkey=f"V1_{g}")
        qc = [A.alloc(f"qc{i}", 512, BF16) for i in range(3)]
        NP = 4
        pb = [A.alloc(f"pb{i}", 512, BF16) for i in range(NP)]
        rec = [A.alloc(f"rec{i}", 512, F32) for i in range(2)]
        aT = [A.alloc(f"aT{i}", 512, BF16) for i in range(2)]
        NS = 6
        pss = self.ps[0:NS]
        pso = self.ps[NS:NS + 2]
        LAG = 2
        chunks = ([(0, 256)] if need_ctx else []) + [(256 + 512 * j, 512) for j in range(16)]
        it = 0
        hi = 0
        units = [(tok0, W, p) for (tok0, W) in chunks for p in range(4)]

        def loadq(ui):
            if ui < len(units):
                tok0, W, p = units[ui]
                q = qc[ui % 3]
                P.dma("sp", q.ap[:, 0:W], self.qT_d.ap[p, :, tok0:tok0 + W], q, reads=[self.qT_d], writes=[q])
        loadq(0)
        loadq(1)
        for ui, (tok0, W, p) in enumerate(units):
            loadq(ui + 2)
            kts = list(range(2)) if tok0 == 0 else list(range(NT))
            n = len(kts)
            q = qc[ui % 3]
            at = aT[ui % 2]
            g = p // 2
            for half in range(2):
                po = pso[hi % 2]
                rc = rec[hi % 2]
                hi += 1
                its = []
                for ii in range(n + LAG):
                    if ii < n:
                        kt = kts[ii]
                        sb = pss[it % NS]
                        pbuf = pb[it % NP]
                        its.append((sb, pbuf))
                        it += 1
                        self.mm(sb.ap[:, 0:W], kv[:, g * 2 + half, kt * 128:(kt + 1) * 128], q.ap[:, 0:W], True, True, [kTz, q], [sb])
                        self.act(pbuf.ap[:, 0:W], sb.ap[:, 0:W], AF.Exp, [sb], [pbuf], scale=0.125)
                    jj = ii - LAG
                    if jj >= 0:
                        kt = kts[jj]
                        sb, pbuf = its[jj]
                        self.mm(po.ap[:, 0:W], V1v[:, kt, g, :], pbuf.ap[:, 0:W], jj == 0, jj == n - 1, [V1, pbuf], [po])
                P.op("dve", lambda e, o=rc.ap[64:128, 0:W], i=po.ap[64:128, 0:W]: e.reciprocal(o, i), [po.res], [rc.res])
                self.tt("dve", at.ap[half * 64:(half + 1) * 64, 0:W], po.ap[0:64, 0:W], rc.ap[64:128, 0:W], ALU.mult, [po, rc], [at])
            P.dma("pool", self.attnT_d.ap[p * 128:(p + 1) * 128, tok0:tok0 + W], at.ap[:, 0:W], at, reads=[at], writes=[self.attnT_d])

    def phase_c0(self, l):
        A, P = self.A, self.P
        self.new_phase()
        self.consts()
        ident = self.ident
        cw = A.alloc("convw", 8 * 4, F32)
        cwv = cw.ap.rearrange("p (c k) -> p c k", k=4)
        for kk in range(3):
            P.dma("sp", cwv[:, :, kk], self.ssd_conv_w[l, kk, :].rearrange("(c p) -> p c", p=128), cw, reads=[self.in_res], writes=[cw], slow=True)
        P.dma("sp", cwv[:, :, 3], self.ssd_conv_b[l, :].rearrange("(c p) -> p c", p=128), cw, reads=[self.in_res], writes=[cw], slow=True)
        PW = 1024
        xr = [A.alloc(f"xr{i}", 8 * (PW + 2), BF16) for i in range(2)]
        xo = [A.alloc(f"xo{i}", 8 * PW, BF16) for i in range(2)]
        acc = [A.alloc(f"cacc{i}", PW, F32) for i in range(2)]
        tst = [A.alloc(f"tst{i}", 768, BF16) for i in range(3)]
        ptr = [self.ps[0], self.ps[1]]
        pieces = [(0, 256)] + [(256 + PW * j, PW) for j in range(8)]
        xsrc = self.xbcT_d.ap.rearrange("(c p) s -> p c s", p=128)
        xdst = self.xcT_d.ap.rearrange("(c p) s -> p c s", p=128)
        ai = 0
        ti = 0
        for pi, (s0, w) in enumerate(pieces):
            x_ = xr[pi % 2]
            xv = x_.ap.rearrange("p (c s) -> p c s", c=8)
            o_ = xo[pi % 2]
            ov = o_.ap.rearrange("p (c s) -> p c s", c=8)
            left_edge = s0 in (0, 256)
            right_edge = (s0 + w) in (256, S)
            lo = s0 if left_edge else s0 - 1
            hi_ = s0 + w if right_edge else s0 + w + 1
            if left_edge:
                P.op("pool", lambda e, a=xv[:, :, 0:1]: e.memset(a, 0.0), [], [x_])
            if right_edge:
                P.op("pool", lambda e, a=xv[:, :, w + 1:w + 2]: e.memset(a, 0.0), [], [x_])
            d0 = 1 if left_edge else 0
            P.dma("sp", xv[:, :, d0:d0 + (hi_ - lo)], xsrc[:, :, lo:hi_], x_, reads=[self.xbcT_d], writes=[x_])
            for c8 in range(8):
                ac = acc[ai % 2]
                ai += 1
                a_ = ac.ap[:, 0:w]
                self.ts("dve", a_, xv[:, c8, 1:w + 1], cwv[:, c8, 1:2], cwv[:, c8, 3:4], ALU.mult, ALU.add, [x_, cw], [ac])
                self.stt("dve", a_, xv[:, c8, 0:w], cwv[:, c8, 0:1], a_, ALU.mult, ALU.add, [x_, cw, ac], [ac])
                self.stt("dve", a_, xv[:, c8, 2:w + 2], cwv[:, c8, 2:3], a_, ALU.mult, ALU.add, [x_, cw, ac], [ac])
                self.act(ov[:, c8, 0:w], a_, AF.Silu, [ac], [o_])
            P.dma("pool", xdst[:, :, s0:s0 + w], ov[:, :, 0:w], o_, reads=[o_], writes=[self.xcT_d])
            for tt_ in range(w // 128):
                pt_ = ptr[ti % 2]
                st_ = tst[ti % 3]
                ti += 1
                pv = pt_.ap.bitcast(BF16)
                for k in range(6):
                    self.tr(pv[:, k * 128:(k + 1) * 128], ov[:, k, tt_ * 128:(tt_ + 1) * 128], ident, [o_, self.Cb], [pt_])
                self.cp("pool" if False else "act", st_.ap, pv[:, 0:768], [pt_], [st_])
                r0 = s0 + tt_ * 128
                P.dma("sp", self.xb_d.ap[r0:r0 + 128, :], st_.ap, st_, reads=[st_], writes=[self.xb_d])

    def phase_c(self, l, need_ctx):
        A, P = self.A, self.P
        self.new_phase()
        self.consts()
        ident = self.ident
        par = A.alloc("cpar", 64, F32)
        self.bcast_row(par, par.ap[:, 0:16], self.ssd_dt_bias[l:l + 1, :], self.in_res)
        self.bcast_row(par, par.ap[:, 16:32], self.ssd_a_log[l:l + 1, :], self.in_res)
        self.bcast_row(par, par.ap[:, 32:40], self.ssd_d[l:l + 1, :], self.in_res)
        self.act(par.ap[:, 16:32], par.ap[:, 16:32], AF.Exp, [par], [par])
        self.ts("dve", par.ap[:, 16:32], par.ap[:, 16:32], -1.0, None, ALU.mult, None, [par], [par])
        nwb = A.alloc("ssdnw", 512, F32)
        self.bcast_row(nwb, nwb.ap, self.ssd_norm_w[l:l + 1, :], self.in_res)
        dta = A.alloc("dta", NT * 16, F32)
        dtv = dta.ap.rearrange("p (t c) -> p t c", c=16)
        P.dma("sp", dtv, self.dt_d.ap.rearrange("(t p) c -> p t c", p=128), dta, reads=[self.dt_d], writes=[dta])
        self.tt("dve", dtv, dtv, par.ap[:, 0:16].unsqueeze(1).to_broadcast([128, NT, 16]), ALU.add, [dta, par], [dta])
        self.act(dta.ap, dta.ap, AF.Exp, [dta], [dta])
        self.ts("dve", dta.ap, dta.ap, 1.0, None, ALU.add, None, [dta], [dta])
        self.act(dta.ap, dta.ap, AF.Ln, [dta], [dta])
        aa = A.alloc("aa", NT * 16, F32)
        aav = aa.ap.rearrange("p (t c) -> p t c", c=16)
        self.tt("dve", aav, dtv, par.ap[:, 16:32].unsqueeze(1).to_broadcast([128, NT, 16]), ALU.mult, [dta, par], [aa])
        NQ = 7
        tab = A.alloc("ctab", NQ * 2 * NT * 8, F32)
        tb = tab.ap.rearrange("p (q d t c) -> p q d t c", q=NQ, d=2, t=NT)
        HT = NT // 2
        for d in range(2):
            tri = self.triU if d == 0 else self.triL
            for hf in range(2):
                t0 = hf * HT
                pc_ = self.ps[(d * 2 + hf) % 4]
                rhs = aav[:, t0:t0 + HT, d * 8:(d + 1) * 8]
                self.mm(pc_.ap[:, 0:HT * 8].rearrange("p (t c) -> p t c", c=8), tri, rhs, True, True, [self.C, aa], [pc_])
                self.cp("dve", tb[:, 0, d, t0:t0 + HT, :], pc_.ap[:, 0:HT * 8].rearrange("p (t c) -> p t c", c=8), [pc_], [tab])
                po_ = self.ps[4 + (d * 2 + hf) % 4]
                self.mm(po_.ap[:, 0:HT * 8].rearrange("p (t c) -> p t c", c=8), self.onesf, rhs, True, True, [self.C, aa], [po_])
                self.cp("dve", tb[:, 1, d, t0:t0 + HT, :], po_.ap[:, 0:HT * 8].rearrange("p (t c) -> p t c", c=8), [po_], [tab])
        n2 = 2 * NT * 8
        flat = lambda q: tab.ap[:, q * n2:(q + 1) * n2]
        self.act(flat(2), flat(0), AF.Exp, [tab], [tab])
        self.tt("dve", flat(3), flat(1), flat(0), ALU.subtract, [tab], [tab])
        self.act(flat(3), flat(3), AF.Exp, [tab], [tab])
        self.act(flat(4), flat(1), AF.Exp, [tab], [tab])
        self.ts("dve", flat(5), flat(0), -1.0, None, ALU.mult, None, [tab], [tab])
        for d in range(2):
            self.tt("dve", tb[:, 6, d, :, :], tb[:, 3, d, :, :], dtv[:, :, d * 8:(d + 1) * 8], ALU.mult, [tab, dta], [tab])
        xcg = [A.alloc(f"xcg{i}", 8 * 512, BF16) for i in range(3)]
        xcsrc = self.xcT_d.ap.rearrange("(c p) s -> p c s", p=128)
        St = A.alloc("St", 512, F32)
        prevb = A.alloc("prevb", 512, BF16)
        LA = 2
        NB = 5
        NF = 7
        xbt = [A.alloc(f"xbt{i}", 768, BF16) for i in range(NB)]
        Rt = [A.alloc(f"Rt{i}", 8 * 128, F32) for i in range(2)]
        lm = [A.alloc(f"lm{i}", 8 * 128, F32) for i in range(2)]
        MT = [A.alloc(f"MT{i}", 8 * 128, BF16) for i in range(NB)]
        xd = [A.alloc(f"xd{i}", 512, BF16) for i in range(NB)]
        xdd = [A.alloc(f"xdd{i}", 512, BF16) for i in range(NB)]
        yt = [A.alloc(f"yt{i}", 512, F32) for i in range(NF)]
        yo = [A.alloc(f"yo{i}", 512, F32) for i in range(NF)]
        yfl = [A.alloc(f"yfl{i}", 512, F32) for i in range(NF)]
        zt = [A.alloc(f"zt{i}", 512, BF16) for i in range(NF)]
        yb = [A.alloc(f"yb{i}", 512, BF16) for i in range(NF)]
        smc = [A.alloc(f"smc{i}", 8, F32) for i in range(NF)]
        sst = [A.alloc(f"sst{i}", 4 * 512, BF16) for i in range(3)]
        junk = A.alloc("cjunk", 512, BF16)
        ps = self.ps
        ps_tr, ps_cb, ps_seg0, ps_seg1, ps_y, ps_o, ps_s = ps[0], ps[1], ps[2], ps[3], ps[4], ps[5], ps[6]
        gstate = {"key": None, "n": 0, "tile": None}
        ctr = {"a": 0}
        dq = Deferred()

        def info(t):
            is_ctx = t < 2
            grp = 0 if is_ctx else (t - 2) // 4 + 1
            return is_ctx, grp, ((not is_ctx) or need_ctx)

        def stageA(d, t):
            is_ctx, grp, want_y = info(t)
            i = ctr["a"]; ctr["a"] += 1
            b = i % NB
            bf = i % NF
            tok = slice(t * 128, (t + 1) * 128)
            tri = self.triU if d == 0 else self.triL
            negb = self.negUb if d == 0 else self.negLb
            P.dma("sp", xbt[b].ap, self.xb_d.ap[tok, :], xbt[b], reads=[self.xb_d], writes=[xbt[b]])
            xt3 = xbt[b].ap[:, 0:512].rearrange("p (h q) -> p h q", q=64)
            self.tt("dve", xdd[b].ap.rearrange("p (h q) -> p h q", q=64), xt3, tb[:, 6, d, t, :].unsqueeze(2).to_broadcast([128, 8, 64]), ALU.mult, [xbt[b], tab], [xdd[b]])
            cx = {"b": b, "bf": bf, "want_y": want_y}
            if not want_y:
                return cx
            if d == 1:
                P.dma("sp", yfl[bf].ap, self.yf_d.ap[tok, :], yfl[bf], reads=[self.yf_d], writes=[yfl[bf]])
                P.dma("sp", zt[bf].ap, self.z_d.ap[tok, :], zt[bf], reads=[self.z_d], writes=[zt[bf]])
            if gstate["key"] != (d, grp):
                gstate["key"] = (d, grp)
                gstate["n"] += 1
                xg_t = xcg[gstate["n"] % 3]
                gstate["tile"] = xg_t
                g0_ = 0 if is_ctx else 2 + (grp - 1) * 4
                gw_ = 256 if is_ctx else 512
                P.dma("sp", xg_t.ap.rearrange("p (c s) -> p c s", c=8)[:, :, 0:gw_], xcsrc[:, :, g0_ * 128:g0_ * 128 + gw_], xg_t, reads=[self.xcT_d], writes=[xg_t])
            xc = gstate["tile"]
            xcv = xc.ap.rearrange("p (c s) -> p c s", c=8)
            goff = (t % 2 if is_ctx else (t - 2) % 4) * 128
            gtok = slice(goff, goff + 128)
            cx["xc"] = xc
            cx["cT"] = [xcv[:, 6 + g, gtok] for g in range(2)]
            a_c = aav[:, t, d * 8:(d + 1) * 8]
            dt_c = dtv[:, t, d * 8:(d + 1) * 8]
            self.tt("dve", xd[b].ap.rearrange("p (h q) -> p h q", q=64), xt3, dt_c.unsqueeze(2).to_broadcast([128, 8, 64]), ALU.mult, [xbt[b], dta], [xd[b]])
            r_ = i % 2
            R3 = Rt[r_].ap.rearrange("p (h w) -> p h w", h=8)
            self.tt("pool", R3, tri.unsqueeze(1).to_broadcast([128, 8, 128]), a_c.unsqueeze(2).to_broadcast([128, 8, 128]), ALU.mult, [self.C, aa], [Rt[r_]])
            for g in range(2):
                self.mm(ps_cb.ap[:, g * 128:(g + 1) * 128], xcv[:, 4 + g, gtok], xcv[:, 6 + g, gtok], True, True, [xc], [ps_cb])
            lm3 = lm[r_].ap.rearrange("p (h w) -> p h w", h=8)
            MT3 = MT[b].ap.rearrange("p (h w) -> p h w", h=8)
            for g in range(2):
                pseg = ps_seg0 if g == 0 else ps_seg1
                self.mm(pseg.ap, self.onesf, Rt[r_].ap[:, g * 512:(g + 1) * 512], True, False, [self.C, Rt[r_]], [pseg])
                for hh in range(4):
                    self.mm(pseg.ap[:, hh * 128:(hh + 1) * 128], ident, negb, False, hh == 3, [self.Cb], [pseg])
                for hh in range(4):
                    h = g * 4 + hh
                    self.act(lm3[:, h, :], pseg.ap[:, hh * 128:(hh + 1) * 128], AF.Exp, [pseg, tab], [lm[r_]], bias=tb[:, 5, d, t, h:h + 1])
                self.tt("dve", MT3[:, g * 4:(g + 1) * 4, :], ps_cb.ap[:, g * 128:(g + 1) * 128].unsqueeze(1).to_broadcast([128, 4, 128]), lm3[:, g * 4:(g + 1) * 4, :], ALU.mult, [ps_cb, lm[r_]], [MT[b]])
            return cx

        def stageB(d, t, cx):
            is_ctx, grp, want_y = info(t)
            b, bf = cx["b"], cx["bf"]
            tok = slice(t * 128, (t + 1) * 128)
            xt3 = xbt[b].ap[:, 0:512].rearrange("p (h q) -> p h q", q=64)
            if want_y:
                for g in range(2):
                    self.mm(ps_o.ap[:, g * 256:(g + 1) * 256], cx["cT"][g], prevb.ap[:, g * 256:(g + 1) * 256], True, True, [cx["xc"], prevb], [ps_o])
            for g in range(2):
                self.mm(ps_s.ap[:, g * 256:(g + 1) * 256], xbt[b].ap[:, 512 + g * 128:512 + (g + 1) * 128], xdd[b].ap[:, g * 256:(g + 1) * 256], True, True, [xbt[b], xdd[b]], [ps_s])
            St3 = St.ap.rearrange("p (h q) -> p h q", q=64)
            self.tt("dve", St3, St3, tb[:, 4, d, t, :].unsqueeze(2).to_broadcast([128, 8, 64]), ALU.mult, [St, tab], [St])
            self.tt("dve", St.ap, ps_s.ap, St.ap, ALU.add, [ps_s, St], [St])
            self.cp("act", prevb.ap, St.ap, [St], [prevb])
            if not want_y:
                return
            MT3 = MT[b].ap.rearrange("p (h w) -> p h w", h=8)
            for h in range(8):
                self.mm(ps_y.ap[:, h * 64:(h + 1) * 64], MT3[:, h, :], xd[b].ap[:, h * 64:(h + 1) * 64], True, True, [MT[b], xd[b]], [ps_y])
            self.tt("dve", yo[bf].ap.rearrange("p (h q) -> p h q", q=64), ps_o.ap.rearrange("p (h q) -> p h q", q=64), tb[:, 2, d, t, :].unsqueeze(2).to_broadcast([128, 8, 64]), ALU.mult, [ps_o, tab], [yo[bf]])
            self.tt("dve", yt[bf].ap, ps_y.ap, yo[bf].ap, ALU.add, [ps_y, yo[bf]], [yt[bf]])
            if d == 0:
                dq.at(1, lambda: P.dma("sp", self.yf_d.ap[tok, :], yt[bf].ap, yt[bf], reads=[yt[bf]], writes=[self.yf_d]))
                return
            sm_ = smc[bf]

            def f1():
                self.tt("pool", yt[bf].ap, yt[bf].ap, yfl[bf].ap, ALU.add, [yt[bf], yfl[bf]], [yt[bf]])
                self.tt("pool", yo[bf].ap.rearrange("p (h q) -> p h q", q=64), xt3, par.ap[:, 32:40].unsqueeze(2).to_broadcast([128, 8, 64]), ALU.mult, [xbt[b], par], [yo[bf]])
                self.tt("pool", yt[bf].ap, yt[bf].ap, yo[bf].ap, ALU.add, [yt[bf], yo[bf]], [yt[bf]])

            def f2():
                self.tt("dve", yt[bf].ap, yt[bf].ap, zt[bf].ap, ALU.mult, [yt[bf], zt[bf]], [yt[bf]])
                self.act(junk.ap, yt[bf].ap, AF.Square, [yt[bf]], [junk, sm_], accum=sm_.ap[:, 0:1])

            def f3():
                self.ts("dve", sm_.ap[:, 2:3], sm_.ap[:, 0:1], 1.0 / 512, EPS, ALU.mult, ALU.add, [sm_], [sm_])
                self.act(sm_.ap[:, 2:3], sm_.ap[:, 2:3], AF.Sqrt, [sm_], [sm_])

            def f4():
                P.op("dve", lambda e: e.reciprocal(sm_.ap[:, 1:2], sm_.ap[:, 2:3]), [sm_.res], [sm_.res])
                self.stt("dve", yb[bf].ap, yt[bf].ap, sm_.ap[:, 1:2], nwb.ap, ALU.mult, ALU.mult, [yt[bf], sm_, nwb], [yb[bf]])

            def f5():
                ptr = ps_tr.ap.bitcast(BF16)
                for k in range(4):
                    self.tr(ptr[:, k * 128:(k + 1) * 128], yb[bf].ap[:, k * 128:(k + 1) * 128], ident, [yb[bf], self.Cb], [ps_tr])
                gsz = 2 if is_ctx else 4
                slot = t % gsz if is_ctx else (t - 2) % 4
                stg_ = sst[grp % 3]
                sv = stg_.ap.rearrange("p (k w) -> p k w", k=4)
                self.cp("act", sv[:, :, slot * 128:(slot + 1) * 128], ptr[:, 0:512].rearrange("p (k w) -> p k w", k=4), [ps_tr], [stg_])
                if slot == 0:
                    g0 = 0 if is_ctx else 2 + (grp - 1) * 4
                    dq.at(1, lambda: P.dma("sp", self.ssdT_d.ap.rearrange("(k p) s -> p k s", p=128)[:, :, g0 * 128:(g0 + gsz) * 128], sv[:, :, 0:gsz * 128], stg_, reads=[stg_], writes=[self.ssdT_d]))
            dq.at(1, f1); dq.at(2, f2); dq.at(3, f3); dq.at(4, f4); dq.at(5, f5)

        for d in range(2):
            P.op("pool", lambda e: e.memset(St.ap, 0.0), [], [St])
            P.op("pool", lambda e: e.memset(prevb.ap, 0.0), [], [prevb])
            order = list(range(NT)) if d == 0 else [1, 0] + list(range(NT - 1, 1, -1))
            pend = [stageA(d, order[j]) for j in range(LA)]
            for j, t in enumerate(order):
                if j + LA < len(order):
                    pend.append(stageA(d, order[j + LA]))
                stageB(d, t, pend.pop(0))
                dq.step()
            dq.flush()

    def phase_d(self, l, need_ctx):
        A, P = self.A, self.P
        self.new_phase()
        self.consts()
        ident = self.ident
        SC = 128.0 ** -0.5
        par = A.alloc("dpar", 64, F32)
        self.bcast_row(par, par.ap[:, 0:8], self.ret_log_decay[l:l + 1, :], self.in_res)
        self.act(par.ap[:, 0:8], par.ap[:, 0:8], AF.Exp, [par], [par])
        self.ts("dve", par.ap[:, 0:8], par.ap[:, 0:8], -1.0, None, ALU.mult, None, [par], [par])
        self.act(par.ap[:, 8:16], par.ap[:, 0:8], AF.Exp, [par], [par], scale=128.0)
        for d in range(2):
            pos = self.misc[:, 1:2] if d == 0 else self.misc[:, 0:1]
            self.ts("dve", par.ap[:, 16 + d * 4:20 + d * 4], par.ap[:, d * 4:d * 4 + 4], pos, None, ALU.mult, None, [par, self.C], [par])
        self.act(par.ap[:, 16:24], par.ap[:, 16:24], AF.Exp, [par], [par])
        self.ts("dve", par.ap[:, 16:24], par.ap[:, 16:24], SC, None, ALU.mult, None, [par], [par])
        gnw = A.alloc("gnw", 512, F32)
        self.bcast_row(gnw, gnw.ap, self.ret_gn_w[l:l + 1, :], self.in_res)
        dm = A.alloc("dmat", 8 * 128, F32)
        dm3 = dm.ap.rearrange("p (a w) -> p a w", a=8)
        qd = A.alloc("qdec", 8 * 128, F32)
        qd3 = qd.ap.rearrange("p (a w) -> p a w", a=8)
        pt = A.alloc("ptab", 4 * 128, F32)
        pt3 = pt.ap.rearrange("p (a w) -> p a w", a=4)
        self.ts("dve", pt3[:, 0, :], self.posd, 0.0, None, ALU.max, None, [self.C], [pt])
        self.ts("dve", pt3[:, 1, :], self.posd, -1.0, 0.0, ALU.mult, ALU.max, [self.C], [pt])
        self.ts("dve", pt3[:, 2, :], self.posd, self.misc[:, 0:1], 1.0, ALU.add, ALU.add, [self.C], [pt])
        self.ts("dve", pt3[:, 3, :], pt3[:, 2, :], -1.0, 129.0, ALU.mult, ALU.add, [pt], [pt])
        mk = A.alloc("dmask", 2 * 128, F32)
        mk3 = mk.ap.rearrange("p (a w) -> p a w", a=2)
        self.ts("dve", mk3[:, 0, :], self.posd, 0.0, SC, ALU.is_ge, ALU.mult, [self.C], [mk])
        self.ts("dve", mk3[:, 1, :], self.posd, 0.0, SC, ALU.is_le, ALU.mult, [self.C], [mk])
        for d in range(2):
            for h in range(4):
                a = d * 4 + h
                self.act(dm3[:, a, :], pt3[:, d, :], AF.Exp, [pt, par], [dm], scale=par.ap[:, a:a + 1])
                self.act(qd3[:, a, :], pt3[:, 2 + d, :], AF.Exp, [pt, par], [qd], scale=par.ap[:, a:a + 1])
            self.tt("dve", dm3[:, d * 4:(d + 1) * 4, :], dm3[:, d * 4:(d + 1) * 4, :], mk3[:, d, :].unsqueeze(1).to_broadcast([128, 4, 128]), ALU.mult, [dm, mk], [dm])
        Sr = A.alloc("Sr", 512, F32)
        prevb = A.alloc("rprev", 512, BF16)
        LA = 2
        NB = 5
        NF = 7
        qT = [A.alloc(f"rqT{i}", 512, BF16) for i in range(NB)]
        kT = [A.alloc(f"rkT{i}", 512, BF16) for i in range(NB)]
        kt_ = [A.alloc(f"rk{i}", 512, BF16) for i in range(NB)]
        vt_ = [A.alloc(f"rv{i}", 512, BF16) for i in range(NB)]
        ST = [A.alloc(f"rST{i}", 512, BF16) for i in range(NB)]
        qdT = [A.alloc(f"rqd{i}", 512, BF16) for i in range(NB)]
        kd = [A.alloc(f"rkd{i}", 512, BF16) for i in range(NB)]
        yt = [A.alloc(f"ryt{i}", 512, F32) for i in range(NF)]
        yfl = [A.alloc(f"ryf{i}", 512, F32) for i in range(NF)]
        yc = [A.alloc(f"ryc{i}", 512, F32) for i in range(NF)]
        ysq = [A.alloc(f"rysq{i}", 512, F32) for i in range(3)]
        gt = [A.alloc(f"rgt{i}", 512, BF16) for i in range(NF)]
        yb = [A.alloc(f"ryb{i}", 512, BF16) for i in range(NF)]
        smr = [A.alloc(f"smr{i}", 32, F32) for i in range(NF)]
        sst = [A.alloc(f"rsst{i}", 4 * 512, BF16) for i in range(3)]
        ps = self.ps
        ps_qk = [ps[0], ps[1]]
        ps_y = [ps[2], ps[3]]
        ps_s = [ps[4], ps[5]]
        ps_tr = ps[6]
        ctr = {"a": 0, "b": 0}
        dq = Deferred()

        def info(t):
            is_ctx = t < 2
            return is_ctx, ((not is_ctx) or need_ctx)

        def stageA(d, t):
            is_ctx, want_y = info(t)
            i = ctr["a"]; ctr["a"] += 1
            b = i % NB
            bf = i % NF
            tok = slice(t * 128, (t + 1) * 128)
            P.dma("sp", kt_[b].ap, self.rk_d.ap[tok, :], kt_[b], reads=[self.rk_d], writes=[kt_[b]])
            P.dma("sp", vt_[b].ap, self.rv_d.ap[tok, :], vt_[b], reads=[self.rv_d], writes=[vt_[b]])
            self.tt("dve", kd[b].ap.rearrange("p (h w) -> p h w", h=4), kt_[b].ap.rearrange("p (h w) -> p h w", h=4), par.ap[:, 16 + d * 4:20 + d * 4].unsqueeze(2).to_broadcast([128, 4, 128]), ALU.mult, [kt_[b], par], [kd[b]])
            if want_y:
                if d == 1:
                    P.dma("sp", yfl[bf].ap, self.yf_d.ap[tok, :], yfl[bf], reads=[self.yf_d], writes=[yfl[bf]])
                    P.dma("sp", gt[bf].ap, self.rg_d.ap[tok, :], gt[bf], reads=[self.rg_d], writes=[gt[bf]])
                pq = ps_qk[i % 2]
                P.dma("sp", qT[b].ap.rearrange("p (a w) -> p a w", a=4), self.rqT_d.ap.rearrange("a p s -> p a s")[:, :, tok], qT[b], reads=[self.rqT_d], writes=[qT[b]])
                P.dma("sp", kT[b].ap.rearrange("p (a w) -> p a w", a=4), self.rkT_d.ap.rearrange("a p s -> p a s")[:, :, tok], kT[b], reads=[self.rkT_d], writes=[kT[b]])
                for h in range(4):
                    self.mm(pq.ap[:, h * 128:(h + 1) * 128], kT[b].ap[:, h * 128:(h + 1) * 128], qT[b].ap[:, h * 128:(h + 1) * 128], True, True, [kT[b], qT[b]], [pq])
                self.tt("dve", ST[b].ap, pq.ap, dm.ap[:, d * 512:(d + 1) * 512], ALU.mult, [pq, dm], [ST[b]])
                self.tt("pool", qdT[b].ap, qT[b].ap, qd.ap[:, d * 512:(d + 1) * 512], ALU.mult, [qT[b], qd], [qdT[b]])
            return (b, bf)

        def stageB(d, t, cx):
            b, bf = cx
            is_ctx, want_y = info(t)
            i = ctr["b"]; ctr["b"] += 1
            b2 = i % 2
            tok = slice(t * 128, (t + 1) * 128)
            py, pst = ps_y[b2], ps_s[b2]
            if want_y:
                for h in range(4):
                    sl = slice(h * 128, (h + 1) * 128)
                    self.mm(py.ap[:, sl], qdT[b].ap[:, sl], prevb.ap[:, sl], True, False, [qdT[b], prevb], [py])
                    self.mm(py.ap[:, sl], ST[b].ap[:, sl], vt_[b].ap[:, sl], False, True, [ST[b], vt_[b]], [py])
            for h in range(4):
                sl = slice(h * 128, (h + 1) * 128)
                self.mm(pst.ap[:, sl], kd[b].ap[:, sl], vt_[b].ap[:, sl], True, True, [kd[b], vt_[b]], [pst])
            Sr3 = Sr.ap.rearrange("p (h w) -> p h w", h=4)
            self.tt("dve", Sr3, Sr3, par.ap[:, 8 + d * 4:12 + d * 4].unsqueeze(2).to_broadcast([128, 4, 128]), ALU.mult, [Sr, par], [Sr])
            self.tt("dve", Sr.ap, pst.ap, Sr.ap, ALU.add, [pst, Sr], [Sr])
            self.cp("act", prevb.ap, Sr.ap, [Sr], [prevb])
            if not want_y:
                return
            if d == 0:
                self.cp("act", yt[bf].ap, py.ap, [py], [yt[bf]])
                dq.at(1, lambda: P.dma("sp", self.yf_d.ap[tok, :], yt[bf].ap, yt[bf], reads=[yt[bf]], writes=[self.yf_d]))
                return
            sm_ = smr[bf]
            self.tt("dve", yt[bf].ap, py.ap, yfl[bf].ap, ALU.add, [py, yfl[bf]], [yt[bf]])
            y3 = yt[bf].ap.rearrange("p (h w) -> p h w", h=4)
            yc3 = yc[bf].ap.rearrange("p (h w) -> p h w", h=4)
            ysq_ = ysq[i % 3]

            def f1():
                P.op("dve", lambda e: e.tensor_reduce(sm_.ap[:, 0:4], y3, AX.X, ALU.add), [yt[bf].res], [sm_.res])
                self.ts("dve", sm_.ap[:, 4:8], sm_.ap[:, 0:4], -1.0 / 128, None, ALU.mult, None, [sm_], [sm_])

            def f2():
                self.tt("dve", yc3, y3, sm_.ap[:, 4:8].unsqueeze(2).to_broadcast([128, 4, 128]), ALU.add, [yt[bf], sm_], [yc[bf]])
                self.act(ysq_.ap, yc[bf].ap, AF.Square, [yc[bf]], [ysq_])

            def f3():
                P.op("dve", lambda e: e.tensor_reduce(sm_.ap[:, 8:12], ysq_.ap.rearrange("p (h w) -> p h w", h=4), AX.X, ALU.add), [ysq_.res], [sm_.res])
                self.ts("dve", sm_.ap[:, 16:20], sm_.ap[:, 8:12], 1.0 / 128, EPS, ALU.mult, ALU.add, [sm_], [sm_])
                self.act(sm_.ap[:, 16:20], sm_.ap[:, 16:20], AF.Sqrt, [sm_], [sm_])

            def f4():
                P.op("dve", lambda e: e.reciprocal(sm_.ap[:, 12:16], sm_.ap[:, 16:20]), [sm_.res], [sm_.res])
                self.tt("dve", yc3, yc3, sm_.ap[:, 12:16].unsqueeze(2).to_broadcast([128, 4, 128]), ALU.mult, [yc[bf], sm_], [yc[bf]])
                self.tt("pool", yc[bf].ap, yc[bf].ap, gnw.ap, ALU.mult, [yc[bf], gnw], [yc[bf]])
                self.tt("pool", yb[bf].ap, yc[bf].ap, gt[bf].ap, ALU.mult, [yc[bf], gt[bf]], [yb[bf]])

            def f5():
                ptr = ps_tr.ap.bitcast(BF16)
                for k in range(4):
                    self.tr(ptr[:, k * 128:(k + 1) * 128], yb[bf].ap[:, k * 128:(k + 1) * 128], ident, [yb[bf], self.Cb], [ps_tr])
                grp = 0 if is_ctx else (t - 2) // 4 + 1
                gsz = 2 if is_ctx else 4
                slot = t % gsz if is_ctx else (t - 2) % 4
                stg_ = sst[grp % 3]
                sv = stg_.ap.rearrange("p (k w) -> p k w", k=4)
                self.cp("act", sv[:, :, slot * 128:(slot + 1) * 128], ptr[:, 0:512].rearrange("p (k w) -> p k w", k=4), [ps_tr], [stg_])
                if slot == 0:
                    g0 = 0 if is_ctx else 2 + (grp - 1) * 4
                    dq.at(1, lambda: P.dma("sp", self.retT_d.ap.rearrange("(k p) s -> p k s", p=128)[:, :, g0 * 128:(g0 + gsz) * 128], sv[:, :, 0:gsz * 128], stg_, reads=[stg_], writes=[self.retT_d]))
            dq.at(1, f1); dq.at(2, f2); dq.at(3, f3); dq.at(4, f4); dq.at(6, f5)

        for d in range(2):
            P.op("pool", lambda e: e.memset(Sr.ap, 0.0), [], [Sr])
            P.op("pool", lambda e: e.memset(prevb.ap, 0.0), [], [prevb])
            order = list(range(NT)) if d == 0 else [1, 0] + list(range(NT - 1, 1, -1))
            pend = [stageA(d, order[j]) for j in range(LA)]
            for j, t in enumerate(order):
                if j + LA < len(order):
                    pend.append(stageA(d, order[j + LA]))
                stageB(d, t, pend.pop(0))
                dq.step()
            dq.flush()

    def phase_e1(self, l, need_ctx):
        A, P = self.A, self.P
        self.new_phase()
        self.consts()
        ident = self.ident
        wb = A.alloc("wbr", 3 * 4 * D, BF16)
        wbv = wb.ap.rearrange("p (a n) -> p a n", a=12)
        self.load_weight_bf16(wb, wbv, self.w_branch[l].rearrange("b (k p) n -> p (b k) n", p=128), 12, D)
        wo = A.alloc("wo", 8 * D, BF16)
        wov = wo.ap.rearrange("p (k n) -> p k n", k=8)
        self.load_weight_bf16(wo, wov, self.w_out[l].rearrange("(k p) n -> p k n", p=128), 8, D)
        rows = (0, 1) if need_ctx else (0,)
        al = {r: self.mod_tile(f"al{r}", r, 2) for r in rows}
        NG, NH, NM = 4, 6, 3
        brT = [[A.alloc(f"brT{i}_{j}", 4 * 512, BF16) for j in range(3)] for i in range(2)]
        gt = [A.alloc(f"gt{i}", 3072, BF16) for i in range(NG)]
        ht = [A.alloc(f"ht{i}", D, F32) for i in range(NH)]
        mg = [A.alloc(f"mg{i}", D, F32) for i in range(NM)]
        mgt = [A.alloc(f"mgt{i}", D, F32) for i in range(NM)]
        mgb = [A.alloc(f"mgb{i}", D, BF16) for i in range(NM)]
        mT = [A.alloc(f"mT{i}", D, BF16) for i in range(NM)]
        ps = self.ps
        ps_t = [ps[0], ps[1]]
        mmb = ps[2:8]
        mi = [0]
        supers = ([(0, 2)] if need_ctx else []) + [(2 + 4 * j, 4) for j in range(16)]
        srcs = [self.attnT_d, self.ssdT_d, self.retT_d]
        tiles = [(si, t0, nt, tc) for si, (t0, nt) in enumerate(supers) for tc in range(nt)]
        dq = Deferred()

        def load_super(si):
            if si >= len(supers):
                return
            t0, nt = supers[si]
            W = nt * 128
            bt = brT[si % 2]
            for j in range(3):
                P.dma("sp", bt[j].ap.rearrange("p (k w) -> p k w", k=4)[:, :, 0:W], srcs[j].ap.rearrange("(k p) s -> p k s", p=128)[:, :, t0 * 128:t0 * 128 + W], bt[j], reads=[srcs[j]], writes=[bt[j]])

        def load_tile(gi):
            if gi >= len(tiles):
                return
            si, t0, nt, tc = tiles[gi]
            t = t0 + tc
            tok = slice(t * 128, (t + 1) * 128)
            P.dma("sp", gt[gi % NG].ap, self.gate_d.ap[tok, :], gt[gi % NG], reads=[self.gate_d], writes=[gt[gi % NG]])
            src, sres = self.hsrc(l, t)
            P.dma("sp", ht[gi % NH].ap, src, ht[gi % NH], reads=[sres], writes=[ht[gi % NH]])

        def nb():
            pb = mmb[mi[0] % 6]
            mi[0] += 1
            return pb

        load_super(0)
        load_tile(0)
        load_tile(1)
        for gi, (si, t0, nt, tc) in enumerate(tiles):
            if tc == 0:
                load_super(si + 1)
            load_tile(gi + 2)
            t = t0 + tc
            r = 1 if t < 2 else 0
            tok = slice(t * 128, (t + 1) * 128)
            bt = brT[si % 2]
            g_, h_ = gt[gi % NG], ht[gi % NH]
            m_, mt_, mb_, mT_ = mg[gi % NM], mgt[gi % NM], mgb[gi % NM], mT[gi % NM]
            for j in range(3):
                bv = bt[j].ap.rearrange("p (k w) -> p k w", k=4)
                for nh in range(2):
                    pb = nb()
                    for k in range(4):
                        self.mm(pb.ap, bv[:, k, tc * 128:(tc + 1) * 128], wbv[:, j * 4 + k, nh * 512:(nh + 1) * 512], k == 0, k == 3, [bt[j], wb], [pb])
                    gsl = g_.ap[:, j * 1024 + nh * 512:j * 1024 + (nh + 1) * 512]
                    msl = slice(nh * 512, (nh + 1) * 512)
                    if j == 0:
                        self.tt("dve", m_.ap[:, msl], pb.ap, gsl, ALU.mult, [pb, g_], [m_])
                    else:
                        self.tt("dve", mt_.ap[:, msl], pb.ap, gsl, ALU.mult, [pb, g_], [mt_])
                        if j == 1:
                            self.tt("dve", m_.ap[:, msl], m_.ap[:, msl], mt_.ap[:, msl], ALU.add, [m_, mt_], [m_])
                        else:
                            self.tt("dve", mb_.ap[:, msl], m_.ap[:, msl], mt_.ap[:, msl], ALU.add, [m_, mt_], [mb_])

            def s2(mb_=mb_, mT_=mT_, pt_=ps_t[gi % 2]):
                ptr = pt_.ap.bitcast(BF16)
                for k in range(8):
                    self.tr(ptr[:, k * 128:(k + 1) * 128], mb_.ap[:, k * 128:(k + 1) * 128], ident, [mb_, self.Cb], [pt_])
                self.cp("act", mT_.ap, ptr[:, 0:1024], [pt_], [mT_])

            def s3(mT_=mT_, mt_=mt_, h_=h_, r=r):
                for nh in range(2):
                    pb = nb()
                    msl = slice(nh * 512, (nh + 1) * 512)
                    for k in range(8):
                        self.mm(pb.ap, mT_.ap[:, k * 128:(k + 1) * 128], wov[:, k, msl], k == 0, k == 7, [mT_, wo], [pb])
                    self.tt("dve", mt_.ap[:, msl], pb.ap, al[r].ap[:, msl], ALU.mult, [pb, al[r]], [mt_])
                    self.tt("pool", h_.ap[:, msl], h_.ap[:, msl], mt_.ap[:, msl], ALU.add, [h_, mt_], [h_])

            def s4(h_=h_, tok=tok):
                P.dma("sp", self.h1_d.ap[tok, :], h_.ap, h_, reads=[h_], writes=[self.h1_d])
            dq.at(1, s2); dq.at(2, s3); dq.at(3, s4)
            dq.step()
        dq.flush()

    def phase_e2(self, l, need_ctx, final):
        A, P = self.A, self.P
        self.new_phase()
        self.consts()
        w1 = A.alloc("w1", 8 * 4096, BF16)
        w1v = w1.ap.rearrange("p (k n) -> p k n", k=8)
        w2 = A.alloc("w2", 32 * 1024, BF16)
        w2v = w2.ap.rearrange("p (k n) -> p k n", k=32)
        self.load_weight_bf16(w1, w1v, self.w_mlp1[l].rearrange("(k p) n -> p k n", p=128), 8, 4096)
        self.load_weight_bf16(w2, w2v, self.w_mlp2[l].rearrange("(k p) n -> p k n", p=128), 32, 1024)
        fnw = None
        if final:
            fnw = A.alloc("fnw", D, F32)
            self.bcast_row(fnw, fnw.ap, self.final_norm_w[0:1, :], self.in_res)
        rows = (0, 1) if need_ctx else (0,)
        mc, mv = self.mod_cols(l, self.norm2_w, 3, 4, rows)
        Alt = A.alloc("Al2", D, F32)
        TW = 256
        NHB = 4
        ht = [A.alloc(f"ht{i}", D, F32) for i in range(NHB)]
        hn = [A.alloc(f"hn{i}", D, F32) for i in range(2)]
        sm = [A.alloc(f"sm{i}", 8, F32) for i in range(NHB)]
        vT = [A.alloc(f"vT{i}", 8 * TW, BF16) for i in range(2)]
        hT = A.alloc("hT", 32 * TW, BF16)
        rl = [A.alloc(f"rl{i}", TW, F32) for i in range(2)]
        pst = [Tl(self.psa[:, 0:1024], "pst")]
        mmb = self.ps[2:8]
        mi = 0
        supers = ([(0, 2, 1)] if need_ctx else []) + [(2 + 2 * j, 2, 0) for j in range(32)]
        cur_r = [None]
        gic = [0]
        ri = 0
        hsv = {}

        def load_h(si):
            if si >= len(supers):
                return
            t0, nt, r = supers[si]
            lst = []
            for tc in range(nt):
                t = t0 + tc
                gi = gic[0]; gic[0] += 1
                h_, s_ = ht[gi % NHB], sm[gi % NHB]
                P.dma("sp", h_.ap, self.h1_d.ap[t * 128:(t + 1) * 128, :], h_, reads=[self.h1_d], writes=[h_])
                lst.append((h_, s_, gi))
            hsv[si] = lst

        def norm1(si):
            if si >= len(supers):
                return
            t0, nt, r = supers[si]
            vTt = vT[si % 2]
            vTv = vTt.ap.rearrange("p (k w) -> p k w", k=8)
            for tc in range(nt):
                h_, s_, gi = hsv[si][tc]
                self.norm_mod_T(h_, s_, hn[gi % 2], pst[0], mc, mv, r, vTv, vTt, tc * 128)

        load_h(0)
        load_h(1)
        norm1(0)
        for si, (t0, nt, r) in enumerate(supers):
            if r != cur_r[0]:
                cur_r[0] = r
                self.bcast_row(Alt, Alt.ap, self.mod_d.ap[r:r + 1, 5 * D:6 * D], self.mod_d.res)
            W = nt * 128
            vTt = vT[si % 2]
            vTv = vTt.ap.rearrange("p (k w) -> p k w", k=8)
            hs = hsv[si]
            hTv = hT.ap.rearrange("p (f w) -> p f w", f=32)
            for f in range(32):
                if f == 16:
                    norm1(si + 1)
                pb = mmb[mi % 6]; mi += 1
                for k in range(8):
                    self.mm(pb.ap[:, 0:W], w1v[:, k, f * 128:(f + 1) * 128], vTv[:, k, 0:W], k == 0, k == 7, [w1, vTt], [pb])
                rl_ = rl[ri % 2]; ri += 1
                self.act(rl_.ap[:, 0:W], pb.ap[:, 0:W], AF.Relu, [pb], [rl_])
                self.tt("dve", hTv[:, f, 0:W], rl_.ap[:, 0:W], rl_.ap[:, 0:W], ALU.mult, [rl_], [hT])
            for tc in range(nt):
                t = t0 + tc
                h_, s_, gi = hs[tc]
                tok = slice(t * 128, (t + 1) * 128)
                hx = hn[gi % 2]
                for nh in range(2):
                    pb = mmb[mi % 6]; mi += 1
                    msl = slice(nh * 512, (nh + 1) * 512)
                    for f in range(32):
                        self.mm(pb.ap, hTv[:, f, tc * 128:(tc + 1) * 128], w2v[:, f, msl], f == 0, f == 31, [hT, w2], [pb])
                    self.tt("dve", hx.ap[:, msl], pb.ap, Alt.ap[:, msl], ALU.mult, [pb, Alt], [hx])
                    self.tt("pool", h_.ap[:, msl], h_.ap[:, msl], hx.ap[:, msl], ALU.add, [h_, hx], [h_])
                if not final:
                    P.dma("sp", self.h_d.ap[tok, :], h_.ap, h_, reads=[h_], writes=[self.h_d])
                else:
                    self.act(hx.ap, h_.ap, AF.Square, [h_], [hx, s_], accum=s_.ap[:, 4:5])
                    self.rstd(s_.ap[:, 5:6], s_.ap[:, 4:5], D, s_, s_.ap[:, 6:7])
                    self.stt("dve", h_.ap, h_.ap, s_.ap[:, 5:6], fnw.ap, ALU.mult, ALU.mult, [h_, s_, fnw], [h_])
                    P.dma("sp", self.out[(t - 2) * 128:(t - 1) * 128, :], h_.ap, h_, reads=[h_], writes=[])
            load_h(si + 2)

    def build(self):
        self.out_dmas = []
        ph = self.phases
        for l in self.layers:
            need_ctx = l < DEPTH - 1
            final = l == DEPTH - 1
            if ph is None or "M" in ph:
                self.phase_mod(l)
            for nm, fn in (("A0", self.phase_a0), ("A1", self.phase_a1), ("A2", self.phase_a2), ("A3", self.phase_a3), ("A4", self.phase_a4)):
                if ph is None or nm in ph:
                    fn(l)
            if ph is None or "B" in ph:
                self.phase_b(l, need_ctx)
            if ph is None or "C0" in ph:
                self.phase_c0(l)
            if ph is None or "C" in ph:
                self.phase_c(l, need_ctx)
            if ph is None or "D" in ph:
                self.phase_d(l, need_ctx)
            if ph is None or "E1" in ph:
                self.phase_e1(l, need_ctx)
            if ph is None or "E2" in ph:
                self.phase_e2(l, need_ctx, final)
        self.P.barrier()
        self.P.emit()
        self.P.close()
        return self.nc


def _tables():
    f32 = np.float32
    rows = NLAT // GRID_W
    row = np.repeat(np.arange(rows, dtype=f32), GRID_W)
    col = np.tile(np.arange(GRID_W, dtype=f32), rows)
    inv = (np.float32(10000.0) ** (-np.arange(16, dtype=f32) / np.float32(16))).astype(f32)
    ang = np.concatenate([row[:, None] * inv, col[:, None] * inv], axis=-1).astype(f32)
    ropeA = np.zeros((S, 64), f32)
    ropeA[:NCTX, 0:32] = 1.0
    ropeA[NCTX:, 0:32] = np.cos(ang)
    ropeA[NCTX:, 32:64] = np.sin(ang)
    pos = np.arange(S, dtype=f32)
    invr = (np.float32(10000.0) ** (-np.linspace(0.0, 1.0, 64, dtype=f32))).astype(f32)
    angr = (pos[:, None] * invr).astype(f32)
    ropeR = np.concatenate([np.cos(angr), np.sin(angr)], axis=-1).astype(f32)
    j = np.arange(128)[:, None]
    l = np.arange(128)[None, :]
    cst = np.zeros((128, 8, 128), f32)
    cst[:, 0] = (j <= l)
    cst[:, 1] = (j >= l)
    cst[:, 2] = 1.0
    cst[:, 3] = (j == l)
    cst[:, 4] = np.where(l < j, NEG, 0.0)
    cst[:, 5] = np.where(l > j, NEG, 0.0)
    cst[:, 6] = (l - j)
    cst[:, 7, 0] = np.arange(128)
    cst[:, 7, 1] = 127 - np.arange(128)
    return ropeA, ropeR, cst.reshape(128, 1024)


_CACHE = {}


def _in_maps(inputs, cores):
    ropeA, ropeR, cst = _tables()
    f = lambda a: np.ascontiguousarray(np.asarray(a, dtype=np.float32))
    shared = {
        "w_mod": f(inputs["w_mod"]), "b_mod": f(inputs["b_mod"]),
        "norm1_w": f(inputs["norm1_w"]), "norm2_w": f(inputs["norm2_w"]),
        "w_in": f(inputs["w_in"]),
        "attn_q_norm": f(inputs["attn_q_norm"]), "attn_k_norm": f(inputs["attn_k_norm"]),
        "ssd_conv_w": f(inputs["ssd_conv_w"]), "ssd_conv_b": f(inputs["ssd_conv_b"]),
        "ssd_dt_bias": f(inputs["ssd_dt_bias"]).reshape(DEPTH, 16),
        "ssd_a_log": f(inputs["ssd_a_log"]).reshape(DEPTH, 16),
        "ssd_d": f(inputs["ssd_d"]), "ssd_norm_w": f(inputs["ssd_norm_w"]),
        "ret_log_decay": f(inputs["ret_log_decay"]).reshape(DEPTH, 8),
        "ret_gn_w": f(inputs["ret_gn_w"]),
        "w_branch": f(inputs["w_branch"]), "w_out": f(inputs["w_out"]),
        "w_mlp1": f(inputs["w_mlp1"]), "w_mlp2": f(inputs["w_mlp2"]),
        "final_norm_w": f(inputs["final_norm_w"]).reshape(1, D),
        "ropeA": ropeA, "ropeR": ropeR, "cst": cst,
    }
    maps = []
    for b in cores:
        m = dict(shared)
        m["x"] = f(inputs["x"][b])
        m["ctx"] = f(inputs["ctx"][b])
        c2 = np.stack([f(inputs["c"][b]), f(inputs["c_ctx"])], 0)
        m["c2"] = np.ascontiguousarray(c2.reshape(2, 8, 128).transpose(2, 0, 1).reshape(128, 16))
        maps.append(m)
    return maps


REAL_CORES = (0, 1, 4, 5)


def kernel(**inputs):
    if "nc" not in _CACHE:
        _CACHE["nc"] = Builder().build()
    nc = _CACHE["nc"]
    real = _in_maps(inputs, list(range(BATCH)))
    zero = {k: np.zeros_like(v) for k, v in real[0].items()}
    maps = []
    for core in range(8):
        if core in REAL_CORES:
            maps.append(real[REAL_CORES.index(core)])
        else:
            maps.append(zero)
    res = run_bass_kernel_spmd(nc, maps, core_ids=list(range(8)))
    out = np.stack([np.asarray(res.results[c]["out"], dtype=np.float32) for c in REAL_CORES], 0)
    return out
```
